# Optimizing a Trainium2 kernel written in Bass

```python
import jax, jax.numpy as jnp
from jax import lax
import numpy as np

D_MODEL = 1024
BATCH = 2
SEQ = 8192
DEPTH = 2

A_HEADS = 8
A_HEAD_DIM = 64
A_PATTERNS = ((128, 1), (512, 4), (2048, 16))
A_WIDTH = A_HEADS * A_HEAD_DIM
B_GROUPS = 4
B_GROUP_DIM = 128
B_CHUNK = 128
B_WIDTH = B_GROUPS * B_GROUP_DIM
C_HEADS = 16
C_HEAD_DIM = 64
C_WIDTH = C_HEADS * C_HEAD_DIM
C_BLOCK = 256
C_TOPK = 3
C_QCHUNK = 32
D_FF = 2816
EPS = 1e-6

AB_IN_WIDTH = 3 * A_WIDTH + 2 * B_WIDTH
AB_MIX_WIDTH = A_WIDTH + B_WIDTH
N_EVEN = (DEPTH + 1) // 2
N_ODD = DEPTH // 2

kernel_name = 'hybrid_dilated_gmlp_moba_macaron'


def rms_norm(x, g):
    xf = x.astype(jnp.float32)
    y = xf * lax.rsqrt(jnp.mean(xf * xf, axis=-1, keepdims=True) + EPS)
    return (y * g.astype(jnp.float32)).astype(x.dtype)


def swiglu(x, w_gate, w_up, w_down):
    return (jax.nn.silu(x @ w_gate) * (x @ w_up)) @ w_down


def to_heads(t, n_heads, head_dim):
    b, s, _ = t.shape
    return t.reshape(b, s, n_heads, head_dim).transpose(0, 2, 1, 3)


def from_heads(t):
    b, h, s, hd = t.shape
    return t.transpose(0, 2, 1, 3).reshape(b, s, h * hd)


def dilated_window_attention(q, k, v, window, dilation):
    b, h, s, hd = q.shape
    w = window // dilation
    span = w * dilation
    sp = -(-s // span) * span
    nb = sp // span

    def to_sub(t):
        t = jnp.pad(t, ((0, 0), (0, 0), (0, sp - s), (0, 0)))
        t = t.reshape(b, h, sp // dilation, dilation, hd)
        t = jnp.swapaxes(t, 2, 3)
        return t.reshape(b, h, dilation, nb, w, hd)

    def with_prev(t):
        prev = jnp.pad(t[:, :, :, :-1], ((0, 0), (0, 0), (0, 0), (1, 0), (0, 0), (0, 0)))
        return jnp.concatenate([prev, t], axis=-2)

    qb = to_sub(q)
    kk = with_prev(to_sub(k))
    vv = with_prev(to_sub(v))
    scores = jnp.einsum('bhrnqd,bhrnkd->bhrnqk', qb, kk).astype(jnp.float32) * (hd ** -0.5)
    qi = jnp.arange(w)[:, None]
    kj = jnp.arange(2 * w)[None, :] - w
    dist = qi - kj
    band = (dist >= 0) & (dist <= w)
    has_prev = (jnp.arange(nb)[:, None, None] > 0) | (kj >= 0)[None]
    mask = band[None] & has_prev
    scores = jnp.where(mask, scores, -jnp.inf)
    m = jnp.max(scores, axis=-1, keepdims=True)
    p = jnp.exp(scores - m)
    l = jnp.sum(p, axis=-1, keepdims=True)
    out = jnp.einsum('bhrnqk,bhrnkd->bhrnqd', (p / l).astype(v.dtype), vv)
    lse = (m + jnp.log(l))[..., 0]

    def from_sub(t):
        rest = t.shape[5:]
        t = t.reshape((b, h, dilation, sp // dilation) + rest)
        t = jnp.swapaxes(t, 2, 3).reshape((b, h, sp) + rest)
        return t[:, :, :s]

    return from_sub(out), from_sub(lse)


def mixture_of_dilations(q, k, v):
    outs, lses = [], []
    for window, dilation in A_PATTERNS:
        o, lse = dilated_window_attention(q, k, v, window, dilation)
        outs.append(o.astype(jnp.float32))
        lses.append(lse)
    wts = jax.nn.softmax(jnp.stack(lses, axis=0), axis=0)
    return jnp.sum(wts[..., None] * jnp.stack(outs, axis=0), axis=0).astype(q.dtype)


def chunked_spatial_gating(u, v, v_norm_g, w_s, b_s):
    b, s, g, c = u.shape
    u = jax.nn.gelu(u)
    v = rms_norm(jax.nn.gelu(v), v_norm_g)
    sp = -(-s // B_CHUNK) * B_CHUNK
    v = jnp.pad(v, ((0, 0), (0, sp - s), (0, 0), (0, 0))).reshape(b, sp // B_CHUNK, B_CHUNK, g, c)
    causal = jnp.tril(jnp.ones((B_CHUNK, B_CHUNK), dtype=bool))
    ws = jnp.where(causal[None], w_s, jnp.zeros_like(w_s))
    mixed = jnp.einsum('gpq,bnqgc->bnpgc', ws, v) + jnp.swapaxes(b_s, 0, 1)[None, None, :, :, None]
    mixed = mixed.reshape(b, sp, g, c)[:, :s]
    return (u * mixed).reshape(b, s, g * c)


def ab_mixer(x, w_in, v_norm, w_s, b_s, w_out):
    z = x @ w_in
    qa, ka, va, ub, vb = jnp.split(
        z, [A_WIDTH, 2 * A_WIDTH, 3 * A_WIDTH, 3 * A_WIDTH + B_WIDTH], axis=-1)
    b, s, _ = x.shape
    a_out = mixture_of_dilations(to_heads(qa, A_HEADS, A_HEAD_DIM),
                                 to_heads(ka, A_HEADS, A_HEAD_DIM),
                                 to_heads(va, A_HEADS, A_HEAD_DIM))
    a_out = from_heads(a_out)
    b_out = chunked_spatial_gating(ub.reshape(b, s, B_GROUPS, B_GROUP_DIM),
                                   vb.reshape(b, s, B_GROUPS, B_GROUP_DIM),
                                   v_norm, w_s, b_s)
    return jnp.concatenate([a_out, b_out], axis=-1) @ w_out


def moba_attention(q, k, v):
    b, h, s, hd = q.shape
    sp = -(-s // C_BLOCK) * C_BLOCK
    pad = ((0, 0), (0, 0), (0, sp - s), (0, 0))
    q, k, v = jnp.pad(q, pad), jnp.pad(k, pad), jnp.pad(v, pad)
    nb = sp // C_BLOCK
    topk = min(C_TOPK, nb)
    kb = k.reshape(b, h, nb, C_BLOCK, hd)
    vb = v.reshape(b, h, nb, C_BLOCK, hd)
    k_mean = jnp.mean(kb.astype(jnp.float32), axis=3).astype(k.dtype)
    scale = hd ** -0.5
    n_chunks = sp // C_QCHUNK
    qc = jnp.moveaxis(q.reshape(b, h, n_chunks, C_QCHUNK, hd), 2, 0)
    bi = jnp.arange(b)[:, None, None, None]
    hi = jnp.arange(h)[None, :, None, None]
    block_ids = jnp.arange(nb)

    def one_chunk(args):
        c, qq = args
        start = c * C_QCHUNK
        own = start // C_BLOCK
        qpos = start + jnp.arange(C_QCHUNK)
        gate = jnp.einsum('bhqd,bhnd->bhqn', qq, k_mean).astype(jnp.float32)
        gate = jnp.where(block_ids < own, gate, -jnp.inf)
        _, idx = lax.top_k(gate, topk)
        valid = idx < own
        k_sel = kb[bi, hi, idx]
        v_sel = vb[bi, hi, idx]
        s_sel = jnp.einsum('bhqd,bhqtkd->bhqtk', qq, k_sel).astype(jnp.float32) * scale
        s_sel = jnp.where(valid[..., None], s_sel, -jnp.inf).reshape(b, h, C_QCHUNK, topk * C_BLOCK)
        k_own = lax.dynamic_index_in_dim(kb, own, axis=2, keepdims=False)
        v_own = lax.dynamic_index_in_dim(vb, own, axis=2, keepdims=False)
        s_own = jnp.einsum('bhqd,bhkd->bhqk', qq, k_own).astype(jnp.float32) * scale
        kpos = own * C_BLOCK + jnp.arange(C_BLOCK)
        s_own = jnp.where(kpos[None, :] <= qpos[:, None], s_own, -jnp.inf)
        p = jax.nn.softmax(jnp.concatenate([s_sel, s_own], axis=-1), axis=-1).astype(v.dtype)
        p_sel = p[..., :topk * C_BLOCK].reshape(b, h, C_QCHUNK, topk, C_BLOCK)
        p_own = p[..., topk * C_BLOCK:]
        return (jnp.einsum('bhqtk,bhqtkd->bhqd', p_sel, v_sel)
                + jnp.einsum('bhqk,bhkd->bhqd', p_own, v_own))

    out = lax.map(one_chunk, (jnp.arange(n_chunks), qc))
    out = jnp.moveaxis(out, 0, 2).reshape(b, h, sp, hd)
    return out[:, :, :s]


def c_mixer(x, w_in, w_out):
    q, k, v = jnp.split(x @ w_in, 3, axis=-1)
    o = moba_attention(to_heads(q, C_HEADS, C_HEAD_DIM),
                       to_heads(k, C_HEADS, C_HEAD_DIM),
                       to_heads(v, C_HEADS, C_HEAD_DIM))
    return from_heads(o) @ w_out


def setup_inputs(seed: int = 0) -> dict:
    key = jax.random.key(seed)
    ks = jax.random.split(key, 18)
    f32 = jnp.float32

    def nrm(k, shape, scale):
        return jax.random.normal(k, shape, f32) * scale

    def gain(k, shape):
        return 1.0 + 0.02 * jax.random.normal(k, shape, f32)

    return {
        'x': nrm(ks[0], (BATCH, SEQ, D_MODEL), 1.0),
        'ffn1_norm': gain(ks[1], (DEPTH, D_MODEL)),
        'ffn1_w_gate': nrm(ks[2], (DEPTH, D_MODEL, D_FF), D_MODEL ** -0.5),
        'ffn1_w_up': nrm(ks[3], (DEPTH, D_MODEL, D_FF), D_MODEL ** -0.5),
        'ffn1_w_down': nrm(ks[4], (DEPTH, D_FF, D_MODEL), D_FF ** -0.5),
        'mix_norm': gain(ks[5], (DEPTH, D_MODEL)),
        'ffn2_norm': gain(ks[6], (DEPTH, D_MODEL)),
        'ffn2_w_gate': nrm(ks[7], (DEPTH, D_MODEL, D_FF), D_MODEL ** -0.5),
        'ffn2_w_up': nrm(ks[8], (DEPTH, D_MODEL, D_FF), D_MODEL ** -0.5),
        'ffn2_w_down': nrm(ks[9], (DEPTH, D_FF, D_MODEL), D_FF ** -0.5),
        'ab_w_in': nrm(ks[10], (N_EVEN, D_MODEL, AB_IN_WIDTH), D_MODEL ** -0.5),
        'ab_v_norm': gain(ks[11], (N_EVEN, B_GROUPS, B_GROUP_DIM)),
        'ab_w_spatial': nrm(ks[12], (N_EVEN, B_GROUPS, B_CHUNK, B_CHUNK), B_CHUNK ** -0.5),
        'ab_b_spatial': gain(ks[13], (N_EVEN, B_GROUPS, B_CHUNK)),
        'ab_w_out': nrm(ks[14], (N_EVEN, AB_MIX_WIDTH, D_MODEL), AB_MIX_WIDTH ** -0.5),
        'c_w_in': nrm(ks[15], (N_ODD, D_MODEL, 3 * C_WIDTH), D_MODEL ** -0.5),
        'c_w_out': nrm(ks[16], (N_ODD, C_WIDTH, D_MODEL), C_WIDTH ** -0.5),
        'final_norm': gain(ks[17], (D_MODEL,)),
    }


def reference(x, ffn1_norm, ffn1_w_gate, ffn1_w_up, ffn1_w_down, mix_norm,
              ffn2_norm, ffn2_w_gate, ffn2_w_up, ffn2_w_down,
              ab_w_in, ab_v_norm, ab_w_spatial, ab_b_spatial, ab_w_out,
              c_w_in, c_w_out, final_norm):
    h = x
    for layer in range(DEPTH):
        h = h + 0.5 * swiglu(rms_norm(h, ffn1_norm[layer]),
                             ffn1_w_gate[layer], ffn1_w_up[layer], ffn1_w_down[layer])
        hn = rms_norm(h, mix_norm[layer])
        if layer % 2 == 0:
            i = layer // 2
            h = h + ab_mixer(hn, ab_w_in[i], ab_v_norm[i], ab_w_spatial[i],
                             ab_b_spatial[i], ab_w_out[i])
        else:
            i = layer // 2
            h = h + c_mixer(hn, c_w_in[i], c_w_out[i])
        h = h + 0.5 * swiglu(rms_norm(h, ffn2_norm[layer]),
                             ffn2_w_gate[layer], ffn2_w_up[layer], ffn2_w_down[layer])
    return rms_norm(h, final_norm)
```

```python
import contextlib
import numpy as np
import ml_dtypes
import concourse.bass as bass
import concourse.mybir as mybir
from concourse.bass_utils import run_bass_kernel_spmd

F32 = mybir.dt.float32
BF16 = mybir.dt.bfloat16
ALU = mybir.AluOpType
AF = mybir.ActivationFunctionType
AX = mybir.AxisListType

NCORES = 8
D = 1024
KC = 8
DFF = 2816
NFC = 22
T = 2048
NTT = 4
SEQ = 8192
EPS = 1e-6
FC_GROUPS = [(0, 6), (6, 12), (12, 18), (18, 22)]


class Buf:
    __slots__ = ("name", "lw", "rd")

    def __init__(self, name):
        self.name = name
        self.lw = None
        self.rd = []


class _Op:
    __slots__ = ("eng", "fn", "deps", "dma", "signal", "cnt", "dsem", "dtarget", "dprev", "cc", "wait_cc")

    def __init__(self, eng, fn, deps, dma):
        self.eng = eng
        self.fn = fn
        self.deps = deps
        self.dma = dma
        self.signal = False
        self.cnt = 0
        self.dsem = None
        self.dtarget = 0
        self.dprev = 0
        self.cc = False
        self.wait_cc = 0


ENGINES = ("pe", "act", "dve", "pool", "sp")
NDSEM = 10


class Sched:
    def __init__(self, nc, stack):
        self.nc = nc
        self.stack = stack
        self.ops = []
        self.bufs = set()
        self.csem = None
        self.ccount = {e: 0 for e in ENGINES}
        self.dcount = {e: 0 for e in ("sp", "pool", "act")}
        self.cccount = 0
        self.sp_init = None

    def op(self, eng, fn, reads=(), writes=(), dma=False, cc=False, wait_cc=0):
        idx = len(self.ops)
        deps = set()
        for b in reads:
            if b.lw is not None:
                deps.add(b.lw)
        for b in writes:
            if b.lw is not None:
                deps.add(b.lw)
            deps.update(b.rd)
        deps.discard(idx)
        if eng == "pe" and not dma:
            deps = {d for d in deps if not (self.ops[d].eng == "pe" and not self.ops[d].dma)}
        o = _Op(eng, fn, sorted(deps), dma or cc)
        o.cc = cc
        o.wait_cc = wait_cc
        if cc:
            self.cccount += 1
            o.dtarget = self.cccount
        self.ops.append(o)
        for b in reads:
            b.rd.append(idx)
            self.bufs.add(b)
        for b in writes:
            b.lw = idx
            b.rd = []
            self.bufs.add(b)
        return idx

    def _init_sems(self):
        nc = self.nc
        self.csem = {e: self.stack.enter_context(nc.semaphore("cs_" + e)) for e in ENGINES}
        self.dsems = {e: [self.stack.enter_context(nc.semaphore("ds_%s_%d" % (e, i))) for i in range(NDSEM)]
                      for e in ("sp", "pool", "act")}
        self.ccsem = self.stack.enter_context(nc.semaphore("ccsem"))

    def emit_phase(self, final=False):
        nc = self.nc
        ops = self.ops
        if self.csem is None:
            self._init_sems()
        csem, dsems, ccsem = self.csem, self.dsems, self.ccsem
        for o in ops:
            for d in o.deps:
                if not ops[d].dma:
                    ops[d].signal = True
        last = {}
        for i, o in enumerate(ops):
            if not o.dma:
                last[o.eng] = i
        for i in last.values():
            ops[i].signal = True
        for o in ops:
            if o.cc:
                o.dsem = ccsem
                o.dprev = 0
            elif o.dma:
                n = self.dcount[o.eng]
                self.dcount[o.eng] += 1
                o.dsem = dsems[o.eng][n % NDSEM]
                o.dprev = 16 * (n // NDSEM)
                o.dtarget = o.dprev + 16
            elif o.signal:
                self.ccount[o.eng] += 1
                o.cnt = self.ccount[o.eng]
        final_d = []
        for e in dsems:
            n = self.dcount[e]
            for i in range(NDSEM):
                k = (n - i + NDSEM - 1) // NDSEM if n > i else 0
                if k > 0:
                    final_d.append((dsems[e][i], 16 * k))
        final_c = [(csem[e], self.ccount[e]) for e in ENGINES if self.ccount[e] > 0]
        if self.cccount and final:
            final_d.append((ccsem, self.cccount))

        def run_engine(ename, eng):
            if self.sp_init is not None and ename in self.sp_init:
                self.sp_init.pop(ename)(eng)
            waited = {}
            for o in ops:
                if o.eng != ename:
                    continue
                for d in o.deps:
                    od = ops[d]
                    if od.dma:
                        key, val = ("d", id(od.dsem)), od.dtarget
                        sem = od.dsem
                    else:
                        key, val = ("c", od.eng), od.cnt
                        sem = csem[od.eng]
                    if waited.get(key, 0) < val:
                        eng.wait_ge(sem, val)
                        waited[key] = val
                if o.wait_cc and waited.get("ccw", 0) < o.wait_cc:
                    eng.wait_ge(ccsem, o.wait_cc)
                    waited["ccw"] = o.wait_cc
                if o.cc:
                    o.fn(eng).then_inc(o.dsem)
                elif o.dma:
                    if o.dprev > 0:
                        key = ("d", id(o.dsem))
                        if waited.get(key, 0) < o.dprev:
                            eng.wait_ge(o.dsem, o.dprev)
                            waited[key] = o.dprev
                    o.fn(eng).then_inc(o.dsem, 16)
                else:
                    ins = o.fn(eng)
                    if o.signal:
                        ins.then_inc(csem[ename], 1)
            for sem, val in final_c + final_d:
                eng.wait_ge(sem, val)

        with nc.Block() as block:
            @block.sync
            def _(e):
                run_engine("sp", e)

            @block.tensor
            def _(e):
                run_engine("pe", e)

            @block.scalar
            def _(e):
                run_engine("act", e)

            @block.vector
            def _(e):
                run_engine("dve", e)

            @block.gpsimd
            def _(e):
                run_engine("pool", e)
        for b in self.bufs:
            b.lw = None
            b.rd = []
        self.bufs = set()
        self.ops = []


class Ctx:
    def __init__(self):
        self.nc = bass.Bass("TRN2", target_bir_lowering=False)
        self.stack = contextlib.ExitStack()
        self.pstack = contextlib.ExitStack()
        self.s = Sched(self.nc, self.stack)
        self._n = 0
        self.psum_banks = []
        self.psum_bufs = []
        self.rank = None

    def dram_in(self, name, shape, dt=F32):
        return self.nc.dram_tensor(name, list(shape), dt, kind="ExternalInput").ap()

    def dram_out(self, name, shape, dt=F32):
        return self.nc.dram_tensor(name, list(shape), dt, kind="ExternalOutput").ap()

    def dram(self, name, shape, dt=F32):
        return self.nc.dram_tensor(name, list(shape), dt).ap()

    def sb(self, name, shape, dt):
        self._n += 1
        return self.pstack.enter_context(self.nc.sbuf_tensor("s%d_%s" % (self._n, name), list(shape), dt))

    def alloc_psum(self):
        for i in range(8):
            self.psum_banks.append(self.stack.enter_context(self.nc.psum_tensor("psb%d" % i, [128, 512], F32)))
            self.psum_bufs.append(Buf("psb%d" % i))

    def buf(self, name=None):
        self._n += 1
        return Buf(name or ("b%d" % self._n))

    def end_phase(self, final=False):
        self.s.emit_phase(final)
        self.pstack.close()
        self.pstack = contextlib.ExitStack()

    def finish(self):
        self.end_phase()
        self.stack.close()
        return self.nc


class Ring:
    def __init__(self, items):
        self.items = items
        self.i = 0

    def next(self):
        it = self.items[self.i % len(self.items)]
        self.i += 1
        return it


def emit_norm(cx, hT, hbufs, gcol, gb, xn, xnbufs, ones_bf, ones_b, sq_ring, ps_ring, st_ring):
    s = cx.s
    for tt in range(NTT):
        tsl = slice(tt * 512, (tt + 1) * 512)
        ps, psb = ps_ring.next()
        for c in range(KC):
            sq, sqb = sq_ring.next()
            s.op("act", lambda e, sq=sq, c=c, tsl=tsl: e.activation(out=sq[:], in_=hT[:, c, tsl], func=AF.Square),
                 reads=[hbufs[tt]], writes=[sqb])
            s.op("pe", lambda e, ps=ps, sq=sq, c=c: e.matmul(ps[:], lhsT=ones_bf[:], rhs=sq[:], start=(c == 0), stop=(c == KC - 1)),
                 reads=[sqb, ones_b], writes=[psb])
        (sd, rs), stb = st_ring.next()
        s.op("act", lambda e, ps=ps, sd=sd: e.activation(out=sd[:], in_=ps[:], func=AF.Sqrt, bias=EPS, scale=1.0 / D),
             reads=[psb], writes=[stb])
        s.op("dve", lambda e, sd=sd, rs=rs: e.reciprocal(out=rs[:], in_=sd[:]), reads=[stb], writes=[stb])
        for c in range(KC):
            s.op("dve", lambda e, c=c, tsl=tsl, rs=rs: e.scalar_tensor_tensor(
                out=xn[:, c, tsl], in0=hT[:, c, tsl], scalar=gcol[:, c:c + 1], in1=rs[:], op0=ALU.mult, op1=ALU.mult),
                reads=[hbufs[tt], stb, gb], writes=[xnbufs[tt]])


def emit_ffn(cx, hT, hbufs, xn, xnbufs, wgu_dram, wd_dram, res):
    s = cx.s
    wgu_ring, wd_ring, act_ring, sg_ring = res["wgu_ring"], res["wd_ring"], res["act_ring"], res["sg_ring"]
    psg_ring, psu_ring, psd_ring = res["psg_ring"], res["psu_ring"], res["psd_ring"]
    for (f0, f1) in FC_GROUPS:
        nf = f1 - f0
        act, actb = act_ring.next()
        wds = []
        for fi in range(nf):
            fc = f0 + fi
            wgu, wgub = wgu_ring.next()
            wd, wdb = wd_ring.next()
            wds.append((wd, wdb))
            s.op("pool", lambda e, wgu=wgu, fc=fc: e.dma_start(out=wgu[:], in_=wgu_dram[fc]), writes=[wgub], dma=True)
            s.op("pool", lambda e, wd=wd, fc=fc: e.dma_start(out=wd[:], in_=wd_dram[fc]), writes=[wdb], dma=True)
            for tt in range(NTT):
                tsl = slice(tt * 512, (tt + 1) * 512)
                pg, pgb = psg_ring.next()
                pu, pub = psu_ring.next()
                for kc in range(KC):
                    s.op("pe", lambda e, pg=pg, wgu=wgu, kc=kc, tsl=tsl: e.matmul(
                        pg[:], lhsT=wgu[:, kc * 128:(kc + 1) * 128], rhs=xn[:, kc, tsl], start=(kc == 0), stop=(kc == KC - 1)),
                        reads=[wgub, xnbufs[tt]], writes=[pgb])
                for kc in range(KC):
                    s.op("pe", lambda e, pu=pu, wgu=wgu, kc=kc, tsl=tsl: e.matmul(
                        pu[:], lhsT=wgu[:, 1024 + kc * 128:1024 + (kc + 1) * 128], rhs=xn[:, kc, tsl], start=(kc == 0), stop=(kc == KC - 1)),
                        reads=[wgub, xnbufs[tt]], writes=[pub])
                sg, sgb = sg_ring.next()
                s.op("act", lambda e, sg=sg, pg=pg: e.activation(out=sg[:], in_=pg[:], func=AF.Silu), reads=[pgb], writes=[sgb])
                s.op("dve", lambda e, sg=sg, pu=pu, act=act, fi=fi, tsl=tsl: e.tensor_tensor(
                    out=act[:, fi, tsl], in0=pu[:], in1=sg[:], op=ALU.mult), reads=[pub, sgb], writes=[actb])
        for dc in range(KC):
            for tt in range(NTT):
                tsl = slice(tt * 512, (tt + 1) * 512)
                pd, pdb = psd_ring.next()
                for fi in range(nf):
                    wd, wdb = wds[fi]
                    s.op("pe", lambda e, pd=pd, wd=wd, fi=fi, dc=dc, tsl=tsl, act=act, nf=nf: e.matmul(
                        pd[:], lhsT=wd[:, dc * 128:(dc + 1) * 128], rhs=act[:, fi, tsl], start=(fi == 0), stop=(fi == nf - 1)),
                        reads=[wdb, actb], writes=[pdb])
                s.op("dve", lambda e, pd=pd, dc=dc, tsl=tsl: e.scalar_tensor_tensor(
                    out=hT[:, dc, tsl], in0=pd[:], scalar=0.5, in1=hT[:, dc, tsl], op0=ALU.mult, op1=ALU.add),
                    reads=[pdb, hbufs[tt]], writes=[hbufs[tt]])


def alloc_ffn_resources(cx):
    res = {}
    mk = lambda name, shape, dt, n: Ring([(cx.sb("%s%d" % (name, i), shape, dt), cx.buf()) for i in range(n)])
    res["wgu_ring"] = mk("wgu", [128, 2048], BF16, 3)
    res["wd_ring"] = mk("wd", [128, 1024], BF16, 8)
    res["act_ring"] = mk("actb", [128, 6, 2048], BF16, 1)
    res["sg_ring"] = mk("sg", [128, 512], F32, 2)
    pb = list(zip(cx.psum_banks, cx.psum_bufs))
    res["psg_ring"] = Ring(pb[0:2])
    res["psu_ring"] = Ring(pb[2:4])
    res["psd_ring"] = Ring(pb[4:6])
    res["psm_ring"] = Ring(pb[6:8])
    res["sq_ring"] = mk("sq", [128, 512], BF16, 3)
    res["st_ring"] = Ring([((cx.sb("sd%d" % i, [128, 512], F32), cx.sb("rs%d" % i, [128, 512], F32)), cx.buf()) for i in range(1)])
    return res


def to_fm(x2d):
    t = x2d.shape[0]
    return np.ascontiguousarray(x2d.reshape(t, KC, 128).transpose(2, 1, 0))


def from_fm(a):
    t = a.shape[2]
    return np.ascontiguousarray(a.transpose(2, 1, 0).reshape(t, KC * 128))


def gain_fm(g):
    return np.ascontiguousarray(g.reshape(KC, 128).T)


def pack_wgu(wg, wu):
    def one(w):
        return w.reshape(KC, 128, NFC, 128).transpose(2, 1, 0, 3).reshape(NFC, 128, KC * 128)
    return np.ascontiguousarray(np.concatenate([one(wg), one(wu)], axis=2))


def pack_wd(wd):
    return np.ascontiguousarray(wd.reshape(NFC, 128, D))


def emit_proj_fm(cx, xn, xnbufs, w_dram, nunits, res, sink, kin=KC):
    s = cx.s
    wgu_ring, ring = res["wgu_ring"], res["psg_ring"]
    for i in range(nunits):
        wgu, wgub = wgu_ring.next()
        s.op("pool", lambda e, wgu=wgu, i=i: e.dma_start(out=wgu[:, 0:2 * kin * 128], in_=w_dram[i]), writes=[wgub], dma=True)
        for hf in range(2):
            for tt in range(NTT):
                tsl = slice(tt * 512, (tt + 1) * 512)
                ps, psb = ring.next()
                for kc in range(kin):
                    s.op("pe", lambda e, ps=ps, wgu=wgu, kc=kc, hf=hf, tsl=tsl: e.matmul(
                        ps[:], lhsT=wgu[:, (hf * kin + kc) * 128:(hf * kin + kc + 1) * 128], rhs=xn[:, kc, tsl],
                        start=(kc == 0), stop=(kc == kin - 1)), reads=[wgub, xnbufs[tt]], writes=[psb])
                sink(i, hf, tt, ps, psb)


def pack_pairs(w):
    k, n = w.shape
    kin = k // 128
    nu = n // 256
    a = w.reshape(kin, 128, nu, 2, 128).transpose(2, 1, 3, 0, 4)
    return np.ascontiguousarray(a.reshape(nu, 128, 2 * kin * 128))


GELU_C = 0.044715
GELU_S = 1.5957691216057308


def emit_gelu(cx, src, srcb, dst, dstb, tmp_ring, reads_extra=()):
    s = cx.s
    (t1, t2), tb = tmp_ring.next()
    s.op("act", lambda e: e.activation(out=t1[:], in_=src, func=AF.Square), reads=[srcb], writes=[tb])
    s.op("dve", lambda e: e.tensor_scalar(out=t1[:], in0=t1[:], scalar1=GELU_C, scalar2=1.0, op0=ALU.mult, op1=ALU.add),
         reads=[tb], writes=[tb])
    s.op("dve", lambda e: e.tensor_tensor(out=t1[:], in0=t1[:], in1=src, op=ALU.mult), reads=[tb, srcb], writes=[tb])
    s.op("act", lambda e: e.activation(out=t2[:], in_=t1[:], func=AF.Sigmoid, scale=GELU_S), reads=[tb], writes=[tb])
    s.op("dve", lambda e: e.tensor_tensor(out=dst, in0=t2[:], in1=src, op=ALU.mult), reads=[tb, srcb], writes=[dstb])


def phase_tok(cx, P):
    s = cx.s
    ffn_w = P.get("ffn_w", [])
    proj_units = P.get("proj_units", 0)
    gm = P.get("gmlp")
    final = P.get("final", False)
    hn_out = P.get("hn_out")
    ng = len(ffn_w) + (1 if (proj_units or final or hn_out) else 0)
    hT = cx.sb("hT", [128, KC, T], F32)
    xn = cx.sb("xn", [128, KC, T], BF16)
    gcol = cx.sb("gcol", [128, ng, KC], F32)
    ones = cx.sb("ones", [128, 128], BF16)
    hb = [cx.buf() for _ in range(NTT)]
    xb = [cx.buf() for _ in range(NTT)]
    gb, ob = cx.buf(), cx.buf()
    res = alloc_ffn_resources(cx)
    stg_ring = Ring([(cx.sb("stg%d" % i, [128, 512], BF16), cx.buf()) for i in range(4)])
    h_d, g_d, hout_d = P["h_in"], P["gains"], P["h_out"]
    for tt in range(NTT):
        s.op("sp", lambda e, tt=tt: e.dma_start(out=hT[:, :, tt * 512:(tt + 1) * 512], in_=h_d[:, :, tt * 512:(tt + 1) * 512]),
             writes=[hb[tt]], dma=True)
    s.op("sp", lambda e: e.dma_start(out=gcol[:], in_=g_d[:, :, :]), writes=[gb], dma=True)
    s.op("dve", lambda e: e.memset(ones[:], 1.0), writes=[ob])
    gi = 0
    if P.get("o_load") is not None:
        P["o_load"](xn, xb)
        wo_d = P["w_out"]

        def sink_mix(i, hf, tt, ps, psb):
            dc = 2 * i + hf
            tsl = slice(tt * 512, (tt + 1) * 512)
            s.op("dve", lambda e: e.tensor_tensor(out=hT[:, dc, tsl], in0=ps[:], in1=hT[:, dc, tsl], op=ALU.add),
                 reads=[psb, hb[tt]], writes=[hb[tt]])
        emit_proj_fm(cx, xn, xb, wo_d, 4, res, sink_mix, kin=8)
    for fi in range(len(ffn_w)):
        emit_norm(cx, hT, hb, gcol[:, gi, :], gb, xn, xb, ones, ob, res["sq_ring"], res["psm_ring"], res["st_ring"])
        gi += 1
        emit_ffn(cx, hT, hb, xn, xb, ffn_w[fi][0], ffn_w[fi][1], res)
    if proj_units:
        emit_norm(cx, hT, hb, gcol[:, gi, :], gb, xn, xb, ones, ob, res["sq_ring"], res["psm_ring"], res["st_ring"])
        gi += 1
        act, actb = res["act_ring"].items[0]
        gu = act
        nz_units = proj_units - (2 if gm else 0)
        tmp_ring = Ring([((cx.sb("gt1_%d" % i, [128, 512], F32), cx.sb("gt2_%d" % i, [128, 512], F32)), cx.buf()) for i in range(2)])

        def sink_z(i, hf, tt, ps, psb):
            tsl = slice(tt * 512, (tt + 1) * 512)
            ch = 2 * i + hf
            if i < nz_units:
                stg, stgb = stg_ring.next()
                s.op("act", lambda e: e.activation(out=stg[:], in_=ps[:], func=AF.Copy), reads=[psb], writes=[stgb])
                P["z_sink"](ch, tt, tsl, stg, stgb)
            else:
                g = ch - 2 * nz_units
                emit_gelu(cx, ps[:], psb, gu[:, g, tsl], actb, tmp_ring)
        emit_proj_fm(cx, xn, xb, P["w_proj"], proj_units, res, sink_z)
        if P.get("flush") is not None and not gm:
            P["flush"]()
        if gm:
            wvb_d, vg_d, ws_d, tril_d, bs_d, bo_d = gm["w_vb"], gm["vg_rep"], gm["wsT"], gm["trilT"], gm["bs_rep"], gm["bo_dst"]
            wd0 = cx.sb("wvb", [128, KC, 512], BF16)
            wd0b = cx.buf()
            s.op("pool", lambda e: e.dma_start(out=wd0[:, :, :], in_=wvb_d[:, :, :]), writes=[wd0b], dma=True)
            if P.get("flush") is not None:
                P["flush"]()
            bo_ring = Ring([(cx.sb("bo%d" % i, [128, 4, 128], BF16), cx.buf()) for i in range(2)])
            vg = cx.sb("vg", [128, 512], F32)
            wsT = cx.sb("wsT", [128, 4, 128], F32)
            tril = cx.sb("tril", [128, 4, 128], F32)
            wsm = cx.sb("wsm", [128, 4, 128], BF16)
            bsr = cx.sb("bsr", [128, 4, 128], F32)
            cb = cx.buf()
            s.op("sp", lambda e: e.dma_start(out=vg[:], in_=vg_d[:, :]), writes=[cb], dma=True)
            s.op("sp", lambda e: e.dma_start(out=wsT[:], in_=ws_d[:, :, :]), writes=[cb], dma=True)
            s.op("sp", lambda e: e.dma_start(out=tril[:], in_=tril_d[:, :, :]), writes=[cb], dma=True)
            s.op("sp", lambda e: e.dma_start(out=bsr[:], in_=bs_d[:, :, :]), writes=[cb], dma=True)
            s.op("dve", lambda e: e.tensor_tensor(out=wsm[:], in0=wsT[:], in1=tril[:], op=ALU.mult), reads=[cb], writes=[cb])
            gv_ring = Ring([(cx.sb("gv%d" % i, [128, 512], F32), cx.buf()) for i in range(2)])
            sqv_ring = Ring([(cx.sb("sqv%d" % i, [128, 512], F32), cx.buf()) for i in range(2)])
            vn_ring = Ring([(cx.sb("vn%d" % i, [128, 512], BF16), cx.buf()) for i in range(2)])
            ss_ring = Ring([((cx.sb("ss%d" % i, [128, 4], F32), cx.sb("rr%d" % i, [128, 4], F32)), cx.buf()) for i in range(2)])
            mx_ring = Ring([(cx.sb("mxd%d" % i, [128, 512], F32), cx.buf()) for i in range(2)])
            for n in range(T // 128):
                nsl = slice(n * 128, (n + 1) * 128)
                tt = n // 4
                ps, psb = res["psu_ring"].next()
                for kc in range(KC):
                    s.op("pe", lambda e, ps=ps, kc=kc, nsl=nsl: e.matmul(ps[:], lhsT=xn[:, kc, nsl], rhs=wd0[:, kc, 0:512],
                                                                       start=(kc == 0), stop=(kc == KC - 1)),
                         reads=[wd0b, xb[tt]], writes=[psb])
                gv, gvb = gv_ring.next()
                emit_gelu(cx, ps[:], psb, gv[:], gvb, tmp_ring)
                sqv, sqvb = sqv_ring.next()
                (ss, rr), ssb = ss_ring.next()
                s.op("dve", lambda e, sqv=sqv, gv=gv: e.tensor_tensor(out=sqv[:], in0=gv[:], in1=gv[:], op=ALU.mult), reads=[gvb], writes=[sqvb])
                s.op("dve", lambda e, sqv=sqv, ss=ss: e.tensor_reduce(out=ss[:], in_=sqv[:].rearrange("p (g c) -> p g c", g=4), axis=AX.X, op=ALU.add),
                     reads=[sqvb], writes=[ssb])
                s.op("act", lambda e, ss=ss: e.activation(out=ss[:], in_=ss[:], func=AF.Sqrt, bias=EPS, scale=1.0 / 128), reads=[ssb], writes=[ssb])
                s.op("dve", lambda e, ss=ss, rr=rr: e.reciprocal(out=rr[:], in_=ss[:]), reads=[ssb], writes=[ssb])
                vn, vnb = vn_ring.next()
                for g in range(4):
                    s.op("dve", lambda e, g=g, vn=vn, gv=gv, rr=rr: e.scalar_tensor_tensor(
                        out=vn[:, g * 128:(g + 1) * 128], in0=gv[:, g * 128:(g + 1) * 128], scalar=rr[:, g:g + 1],
                        in1=vg[:, g * 128:(g + 1) * 128], op0=ALU.mult, op1=ALU.mult), reads=[gvb, ssb, cb], writes=[vnb])
                pm, pmb = res["psd_ring"].next()
                for g in range(4):
                    s.op("pe", lambda e, g=g, pm=pm, vn=vn: e.matmul(pm[:, g * 128:(g + 1) * 128], lhsT=vn[:, g * 128:(g + 1) * 128],
                                                                   rhs=wsm[:, g, :], start=True, stop=True),
                         reads=[vnb, cb], writes=[pmb])
                mx, mxb = mx_ring.next()
                s.op("dve", lambda e, mx=mx, pm=pm: e.tensor_tensor(out=mx[:], in0=pm[:], in1=bsr[:].rearrange("p g c -> p (g c)"), op=ALU.add),
                     reads=[pmb, cb], writes=[mxb])
                bo, bob = bo_ring.next()
                s.op("dve", lambda e, mx=mx, nsl=nsl, bo=bo: e.tensor_tensor(out=bo[:], in0=mx[:].rearrange("p (g c) -> p g c", g=4),
                                                                             in1=gu[:, 0:4, nsl], op=ALU.mult), reads=[mxb, actb], writes=[bob])
                s.op("sp", lambda e, bo=bo, nsl=nsl: e.dma_start(out=bo_d[:, :, nsl], in_=bo[:]), reads=[bob], dma=True)
    if hn_out is not None:
        emit_norm(cx, hT, hb, gcol[:, gi, :], gb, xn, xb, ones, ob, res["sq_ring"], res["psm_ring"], res["st_ring"])
        gi += 1
        hn_out(xn, xb)
    if final:
        emit_norm(cx, hT, hb, gcol[:, gi, :], gb, hT, hb, ones, ob, res["sq_ring"], res["psm_ring"], res["st_ring"])
    for tt in range(NTT):
        s.op("sp", lambda e, tt=tt: e.dma_start(out=hout_d[:, :, tt * 512:(tt + 1) * 512], in_=hT[:, :, tt * 512:(tt + 1) * 512]),
             reads=[hb[tt]], dma=True)
    cx.end_phase(final=final)


DILS = (1, 4, 16)
BIG = 30000.0
LAG = 2


def emit_vext(cx, vx, vxb, vsrc, vsrcb, colsl, ident_blk, identb, pt_ring):
    s = cx.s
    for g in range(4):
        pt, ptb = pt_ring.next()
        ptv = pt[:].bitcast(BF16)
        for i in range(16):
            blk = g * 16 + i
            s.op("pe", lambda e, ptv=ptv, i=i, blk=blk: e.transpose(out=ptv[:, i * 64:(i + 1) * 64], in_=vsrc[:, colsl(blk)], identity=ident_blk),
                 reads=[vsrcb, identb], writes=[ptb])
        s.op("act", lambda e, ptv=ptv, g=g: e.activation(out=vx[:, g * 16:(g + 1) * 16, 0:64], in_=ptv[:, :].rearrange("p (b c) -> p b c", c=64), func=AF.Copy),
             reads=[ptb], writes=[vxb])


def phase_dil(cx, P):
    s = cx.s
    zg, os_d, m_d, id_d = P["zg"], P["os"], P["masks"], P["ident"]
    pb = list(zip(cx.psum_banks, cx.psum_bufs))
    st_ring, po_ring, pt_ring = Ring(pb[0:5]), Ring(pb[5:7]), Ring(pb[7:8])
    qkv = cx.sb("qkv", [128, 3, SEQ], BF16)
    q, k, vT = qkv[:, 0, :], qkv[:, 1, :], qkv[:, 2, :]
    ldb = [cx.buf(), cx.buf()]
    acc = cx.sb("acc", [128, SEQ], F32)
    accall = cx.buf()
    masks = cx.sb("masks", [128, 2, 512], BF16)
    ident = cx.sb("ident", [128, 128], BF16)
    mb, idb = cx.buf(), cx.buf()
    s.op("sp", lambda e: e.dma_start(out=masks[:], in_=m_d.rearrange("m p f -> p m f")), writes=[mb], dma=True)
    s.op("sp", lambda e: e.dma_start(out=ident[:], in_=id_d[:, :]), writes=[idb], dma=True)
    for hh in range(2):
        for t in range(3):
            s.op("sp", lambda e, hh=hh, t=t: e.dma_start(
                out=qkv[hh * 64:hh * 64 + 64, t, :].rearrange("p (r t) -> p r t", r=4),
                in_=zg[3 * hh + t].rearrange("(r x) t -> x r t", r=4)[bass.ds(cx.rk64["sp"], 64)]), writes=[ldb[hh]], dma=True,
                 wait_cc=P["wait"](hh))
    vx_items = []
    for i in range(2):
        vx = cx.sb("vx%d" % i, [128, 64, 128], BF16)
        vxb = cx.buf()
        s.op("pool", lambda e, vx=vx: e.memset(vx[:, :, 64:128], 1.0), writes=[vxb])
        vx_items.append((vx, vxb))
    vx_ring = Ring(vx_items)
    p_ring = Ring([(cx.sb("p%d" % i, [128, 512], BF16), cx.buf()) for i in range(6)])
    rsh_ring = Ring([(cx.sb("rsh%d" % i, [64, 2048], F32), cx.buf()) for i in range(2)])
    rcp_ring = Ring([(cx.sb("rcp%d" % i, [128, 2048], F32), cx.buf()) for i in range(1)])
    ostg_ring = Ring([(cx.sb("ostg%d" % i, [64, 2048], BF16), cx.buf()) for i in range(2)])
    osb = [[cx.buf() for _ in range(4)] for _ in range(2)]
    qkvp = cx.sb("qkvp", [128, 3, SEQ], BF16)
    qp, kp, vp = qkvp[:, 0, :], qkvp[:, 1, :], qkvp[:, 2, :]
    qpb, kpb, vpb = cx.buf(), cx.buf(), cx.buf()

    def colsl(blk):
        return slice(blk * 128, (blk + 1) * 128)
    for hh in range(2):
        hs = slice(hh * 64, hh * 64 + 64)
        for pi, d in enumerate(DILS):
            nb = 64 // d
            qb = kb = vTb = ldb[hh]
            if d == 1:
                qs, ks, vs_, qsb, ksb, vsb_ = q, k, vT, qb, kb, vTb
            else:
                s.op("dve", lambda e, d=d, hs=hs: e.tensor_copy(out=qp[hs].rearrange("p (r m) -> p r m", r=d),
                                                                in_=q[hs].rearrange("p (m r) -> p r m", r=d)), reads=[qb], writes=[qpb])
                for (src, srcb, dst, dstb) in ((k, kb, kp, kpb), (vT, vTb, vp, vpb)):
                    s.op("act", lambda e, src=src, dst=dst, d=d, hs=hs: e.activation(out=dst[hs].rearrange("p (r m) -> p r m", r=d),
                                                                                    in_=src[hs].rearrange("p (m r) -> p r m", r=d), func=AF.Copy),
                         reads=[srcb], writes=[dstb])
                qs, ks, vs_, qsb, ksb, vsb_ = qp, kp, vp, qpb, kpb, vpb
            v, vb = vx_ring.next()
            emit_vext(cx, v, vb, vs_[hs], vsb_, colsl, ident[hs, hs], idb, pt_ring)
            po, pob = None, None
            pending = []
            for jp in range(32):
                j0 = 2 * jp
                n0 = j0 % nb
                first = (n0 == 0)
                ps, psb = st_ring.next()
                kprev = j0 if first else j0 - 1
                for ci, (kblk, qblk) in enumerate(((kprev, j0), (j0, j0), (j0, j0 + 1), (j0 + 1, j0 + 1))):
                    s.op("pe", lambda e, ps=ps, ci=ci, kblk=kblk, qblk=qblk, ks=ks, qs=qs, hs=hs: e.matmul(
                        ps[:, ci * 128:(ci + 1) * 128], lhsT=ks[hs, colsl(kblk)], rhs=qs[hs, colsl(qblk)],
                        start=True, stop=True), reads=[ksb, qsb], writes=[psb])
                p, pbuf = p_ring.next()
                s.op("act", lambda e, p=p, ps=ps: e.activation(out=p[:], in_=ps[:], func=AF.Exp, scale=0.125), reads=[psb], writes=[pbuf])
                mi = 1 if first else 0
                s.op("dve", lambda e, p=p, mi=mi: e.tensor_tensor(out=p[:], in0=p[:], in1=masks[:, mi, :], op=ALU.mult),
                     reads=[pbuf, mb], writes=[pbuf])
                if jp % 2 == 0:
                    po, pob = po_ring.next()

                def pv(po=po, pob=pob, v=v, vb=vb, j0=j0, p=p, pbuf=pbuf, first=first, jp=jp, nb=nb, d=d, pi=pi):
                    c0 = (j0 % 4) * 128
                    if not first:
                        s.op("pe", lambda e: e.matmul(po[:, c0:c0 + 128], lhsT=v[:, j0 - 1, :], rhs=p[:, 0:128], start=True, stop=False),
                             reads=[vb, pbuf], writes=[pob])
                    s.op("pe", lambda e: e.matmul(po[:, c0:c0 + 128], lhsT=v[:, j0, :], rhs=p[:, 128:256], start=first, stop=True),
                         reads=[vb, pbuf], writes=[pob])
                    s.op("pe", lambda e: e.matmul(po[:, c0 + 128:c0 + 256], lhsT=v[:, j0, :], rhs=p[:, 256:384], start=True, stop=False),
                         reads=[vb, pbuf], writes=[pob])
                    s.op("pe", lambda e: e.matmul(po[:, c0 + 128:c0 + 256], lhsT=v[:, j0 + 1, :], rhs=p[:, 384:512], start=False, stop=True),
                         reads=[vb, pbuf], writes=[pob])
                    if jp % 2 == 1:
                        j = j0 - 2
                        start = (j // nb) + d * 128 * (j % nb)
                        dst = acc[:, start:start + 511 * d + 1:d]
                        if pi == 0:
                            s.op("dve", lambda e: e.tensor_copy(out=dst, in_=po[:]), reads=[pob], writes=[accall])
                        else:
                            s.op("dve", lambda e: e.tensor_tensor(out=dst, in0=po[:], in1=dst, op=ALU.add), reads=[pob, accall], writes=[accall])
                pending.append(pv)
                if len(pending) > 4:
                    pending.pop(0)()
            while pending:
                pending.pop(0)()
        for c in range(4):
            csl = slice(c * 2048, (c + 1) * 2048)
            rcp, rcpb = rcp_ring.next()
            s.op("dve", lambda e, rcp=rcp, csl=csl: e.reciprocal(out=rcp[64:128, :], in_=acc[64:128, csl]), reads=[accall], writes=[rcpb])
            rsh, rshb = rsh_ring.next()
            s.op("sp", lambda e, rsh=rsh, rcp=rcp: e.dma_start(out=rsh[:, :], in_=rcp[64:128, :]), reads=[rcpb], writes=[rshb], dma=True)
            og, ogb = ostg_ring.next()
            s.op("dve", lambda e, og=og, csl=csl, rsh=rsh: e.tensor_tensor(out=og[:], in0=acc[0:64, csl], in1=rsh[:, :], op=ALU.mult),
                 reads=[accall, rshb], writes=[ogb])
            s.op("sp", lambda e, og=og, c=c, hh=hh: e.dma_start(out=os_d[hh, :, c * 2048:(c + 1) * 2048], in_=og[:]), reads=[ogb], writes=[osb[hh][c]], dma=True)
        P["after_head"](hh, osb[0] + osb[1])
    cx.end_phase()


def phase_hproj(cx, P):
    s = cx.s
    hg, qk_s, v_s = P["hg"], P["qk_s"], P["v_s"]
    pb = list(zip(cx.psum_banks, cx.psum_bufs))
    ps_ring, pv_ring = Ring(pb[0:4]), Ring(pb[4:8])
    wt = cx.sb("wqkv", [128, 3, KC, 256], BF16)
    wb = cx.buf()
    for i, wd_ in enumerate((P["wq"], P["wk"], P["wv"])):
        s.op("pool", lambda e, i=i, wd_=wd_: e.dma_start(out=wt[:, i, :, :], in_=wd_[:, :, :]), writes=[wb], dma=True)
    hn_ring = Ring([(cx.sb("hn%d" % i, [128, KC, 512], BF16), cx.buf()) for i in range(4)])
    stg_ring = Ring([(cx.sb("stg%d" % i, [128, 512], BF16), cx.buf()) for i in range(4)])
    vst_items = []
    for i in range(2):
        t = cx.sb("vst%d" % i, [128, 4, 4, 128], BF16)
        b = cx.buf()
        s.op("dve", lambda e, t=t: e.memset(t[:, :, :, 64:128], 1.0), writes=[b])
        vst_items.append((t, b))
    vst_ring = Ring(vst_items)
    tiles = [(qd, r) for qd in range(4) for r in range(4)]
    loaded = {}

    def load(i):
        qd, r = tiles[i]
        hn, hnb = hn_ring.next()
        s.op("sp", lambda e: e.dma_start(out=hn[:], in_=hg[qd, r * 1024:(r + 1) * 1024, :].rearrange("(c p) t -> p c t", p=128)),
             writes=[hnb], dma=True, wait_cc=P["wait"](qd))
        loaded[i] = (hn, hnb)
    load(0)
    load(1)
    for ti, (qd, r) in enumerate(tiles):
        if True:
            if ti + 2 < len(tiles):
                load(ti + 2)
            hn, hnb = loaded.pop(ti)
            gsl = slice(r * T + qd * 512, r * T + (qd + 1) * 512)
            for which in range(2):
                for cp in range(2):
                    ps, psb = ps_ring.next()
                    for kc in range(KC):
                        s.op("pe", lambda e, ps=ps, which=which, cp=cp, kc=kc, hn=hn: e.matmul(
                            ps[:], lhsT=wt[:, which, kc, cp * 128:(cp + 1) * 128], rhs=hn[:, kc, :], start=(kc == 0), stop=(kc == KC - 1)),
                            reads=[wb, hnb], writes=[psb])
                    stg, stgb = stg_ring.next()
                    s.op("act", lambda e, stg=stg, ps=ps: e.activation(out=stg[:], in_=ps[:], func=AF.Copy), reads=[psb], writes=[stgb])
                    for hh in range(2):
                        s.op("sp", lambda e, stg=stg, hh=hh, cp=cp, which=which, gsl=gsl: e.dma_start(
                            out=qk_s[2 * cp + hh, which, :, gsl], in_=stg[hh * 64:(hh + 1) * 64, :]), reads=[stgb], dma=True)
            vst, vstb = vst_ring.next()
            for n4 in range(4):
                nsl = slice(n4 * 128, (n4 + 1) * 128)
                pv, pvb = pv_ring.next()
                for kc in range(KC):
                    s.op("pe", lambda e, pv=pv, kc=kc, hn=hn, nsl=nsl: e.matmul(pv[:, 0:256], lhsT=hn[:, kc, nsl], rhs=wt[:, 2, kc, :],
                                                                              start=(kc == 0), stop=(kc == KC - 1)),
                         reads=[wb, hnb], writes=[pvb])
                s.op("dve", lambda e, vst=vst, pv=pv, n4=n4: e.tensor_copy(out=vst[:, :, n4, 0:64], in_=pv[:, 0:256].rearrange("p (i c) -> p i c", c=64)),
                     reads=[pvb], writes=[vstb])
            n0 = r * 16 + qd * 4
            for it in range(4):
                s.op("sp", lambda e, vst=vst, it=it, n0=n0: e.dma_start(out=v_s[it, :, n0:n0 + 4, :], in_=vst[:, it, :, :]), reads=[vstb], dma=True)
    cx.end_phase()


def phase_moba(cx, P, nitems=4):
    s = cx.s
    os_d = P["os"]
    er_d, id_d, cb_d, past_d, ownb_d, dm_d = P["erows"], P["ident"], P["cb"], P["past"], P["ownb"], P["dm"]
    pb = list(zip(cx.psum_banks, cx.psum_bufs))
    st_ring, po_ring, pg_ring, pt_ring = Ring(pb[0:4]), Ring(pb[4:6]), Ring(pb[6:7]), Ring(pb[7:8])
    ident = cx.sb("ident", [128, 128], BF16)
    cbt = cx.sb("cbt", [128, 4, 512], F32)
    pastt = cx.sb("pastt", [128, 4, 512], F32)
    ownbt = cx.sb("ownbt", [128, 4, 512], F32)
    dmt = cx.sb("dmt", [128, 4, 512], BF16)
    cb = cx.buf()
    s.op("sp", lambda e: e.dma_start(out=ident[:], in_=id_d[:, :]), writes=[cb], dma=True)
    s.op("sp", lambda e: e.dma_start(out=cbt[:], in_=cb_d[:, :, :]), writes=[cb], dma=True)
    s.op("sp", lambda e: e.dma_start(out=pastt[:], in_=past_d[:, :, :]), writes=[cb], dma=True)
    s.op("sp", lambda e: e.dma_start(out=ownbt[:], in_=ownb_d[:, :, :]), writes=[cb], dma=True)
    s.op("sp", lambda e: e.dma_start(out=dmt[:], in_=dm_d[:, :, :]), writes=[cb], dma=True)
    items = []
    for i in range(2):
        t3 = cx.sb("t3_%d" % i, [96, 2, SEQ], BF16)
        qa, ka = t3[:, 0, :], t3[:, 1, :]
        va = cx.sb("va%d" % i, [128, 64, 128], BF16)
        tb, eb, vab, qbias = cx.buf(), cx.buf(), cx.buf(), cx.buf()
        s.op("sp", lambda e, ka=ka: e.dma_start(out=ka[64:96], in_=er_d[:, :]), writes=[eb], dma=True)
        items.append(((t3, qa, ka, va, None), (tb, eb, vab, qbias)))
    it_ring = Ring(items)
    kmf = cx.sb("kmf", [64, 32], F32)
    km = cx.sb("km", [64, 32], BF16)
    kmb = cx.buf()
    gm = cx.sb("gm", [128, 512], F32)
    sel = cx.sb("sel", [128, 512], F32)
    mx = cx.sb("mx", [128, 16, 8], F32)
    bq_ring = Ring([(cx.sb("bq%d" % i, [128, 512], BF16), cx.buf()) for i in range(2)])
    gmb = cx.buf()
    p_ring = Ring([(cx.sb("p%d" % i, [128, 512], BF16), cx.buf()) for i in range(6)])
    rec_ring = Ring([(cx.sb("rec%d" % i, [128, 512], F32), cx.buf()) for i in range(2)])
    ostg_ring = Ring([(cx.sb("ostg%d" % i, [64, 512], BF16), cx.buf()) for i in range(3)])
    osb = [[cx.buf() for _ in range(4)] for _ in range(nitems)]
    qk_s, v_s = P["qk_s"], P["v_s"]

    def issue_loads(it, item):
        (t3, qa, ka, va, _), (tb, eb, vab, qbias) = item
        s.op("sp", lambda e: e.dma_start(out=t3[0:64, :, :], in_=qk_s[it].rearrange("w p t -> p w t")), writes=[tb], dma=True)
        s.op("sp", lambda e: e.dma_start(out=va[:], in_=v_s[it]), writes=[vab], dma=True)
    cur = it_ring.next()
    issue_loads(0, cur)
    for it in range(nitems):
        (t3, qa, ka, va, vs), (tb, eb, vab, qbias) = cur
        qab = kab = vsb = tb
        s.op("dve", lambda e, ka=ka: e.tensor_reduce(out=kmf[:], in_=ka[0:64].rearrange("p (j k) -> p j k", k=256), axis=AX.X, op=ALU.add),
             reads=[kab], writes=[kmb])
        s.op("act", lambda e: e.activation(out=km[:], in_=kmf[:], func=AF.Copy, scale=1.0 / 256), reads=[kmb], writes=[kmb])
        for grp in range(4):
            pg, pgb = pg_ring.next()
            for i in range(16):
                qt = grp * 16 + i
                s.op("pe", lambda e, pg=pg, i=i, qt=qt, qa=qa: e.matmul(pg[:, i * 32:(i + 1) * 32], lhsT=qa[0:64, qt * 128:(qt + 1) * 128], rhs=km[:, :],
                                                                       start=True, stop=True), reads=[qab, kmb], writes=[pgb])
            s.op("dve", lambda e, pg=pg, grp=grp: e.tensor_tensor(out=gm[:], in0=pg[:], in1=cbt[:, grp, :], op=ALU.add), reads=[pgb, cb], writes=[gmb])
            for i in range(16):
                s.op("dve", lambda e, i=i: e.max(out=mx[:, i, :], in_=gm[:, i * 32:(i + 1) * 32]), reads=[gmb], writes=[gmb])
            s.op("dve", lambda e: e.tensor_tensor(out=sel[:].rearrange("p (a j) -> p a j", j=32), in0=gm[:].rearrange("p (a j) -> p a j", j=32),
                                                  in1=mx[:, :, 2:3].to_broadcast([128, 16, 32]), op=ALU.is_ge), reads=[gmb], writes=[gmb])
            s.op("dve", lambda e, grp=grp: e.tensor_tensor(out=sel[:], in0=sel[:], in1=pastt[:, grp, :], op=ALU.mult), reads=[gmb, cb], writes=[gmb])
            bq, bqb = bq_ring.next()
            s.op("dve", lambda e, grp=grp, bq=bq: e.scalar_tensor_tensor(out=bq[:], in0=sel[:], scalar=BIG, in1=ownbt[:, grp, :], op0=ALU.mult, op1=ALU.add),
                 reads=[gmb, cb], writes=[bqb])
            for half in range(2):
                pt, ptb = pt_ring.next()
                ptv = pt[:].bitcast(BF16)
                for i8 in range(8):
                    i = half * 8 + i8
                    s.op("pe", lambda e, ptv=ptv, i8=i8, i=i, bq=bq: e.transpose(out=ptv[0:32, i8 * 128:(i8 + 1) * 128], in_=bq[:, i * 32:(i + 1) * 32],
                                                                              identity=ident[:]), reads=[bqb, cb], writes=[ptb])
                c0 = (grp * 16 + half * 8) * 128
                s.op("act", lambda e, ptv=ptv, c0=c0, qa=qa: e.activation(out=qa[64:96, c0:c0 + 1024], in_=ptv[0:32, :], func=AF.Copy),
                     reads=[ptb], writes=[qbias])
        pending = []
        nxt = None
        for tq in range(16):
            if tq == 13 and it + 1 < nitems:
                nxt = it_ring.next()
                issue_loads(it + 1, nxt)
            po, pob = po_ring.next()
            nk = 4 * tq + 4
            for kt in range(nk):
                ps, psb = st_ring.next()
                s.op("pe", lambda e, ps=ps, kt=kt, tq=tq, ka=ka, qa=qa: e.matmul(ps[:], lhsT=ka[:, kt * 128:(kt + 1) * 128], rhs=qa[:, tq * 512:(tq + 1) * 512],
                                                                              start=True, stop=True), reads=[kab, eb, qab, qbias], writes=[psb])
                p, pbuf = p_ring.next()
                s.op("act", lambda e, p=p, ps=ps: e.activation(out=p[:], in_=ps[:], func=AF.Exp, scale=0.125), reads=[psb], writes=[pbuf])
                if kt >= 4 * tq:
                    di = kt - 4 * tq
                    s.op("dve", lambda e, p=p, di=di: e.tensor_tensor(out=p[:], in0=p[:], in1=dmt[:, di, :], op=ALU.mult), reads=[pbuf, cb], writes=[pbuf])

                def pv(po=po, pob=pob, kt=kt, p=p, pbuf=pbuf, nk=nk, tq=tq, va=va, vab=vab, it=it):
                    s.op("pe", lambda e: e.matmul(po[:], lhsT=va[:, kt, :], rhs=p[:], start=(kt == 0), stop=(kt == nk - 1)),
                         reads=[vab, pbuf], writes=[pob])
                    if kt == nk - 1:
                        rec, recb = rec_ring.next()
                        s.op("dve", lambda e: e.reciprocal(out=rec[64:128, :], in_=po[64:128, :]), reads=[pob], writes=[recb])
                        og, ogb = ostg_ring.next()
                        s.op("dve", lambda e: e.tensor_tensor(out=og[:], in0=po[0:64, :], in1=rec[64:128, :], op=ALU.mult),
                             reads=[pob, recb], writes=[ogb])
                        s.op("sp", lambda e: e.dma_start(out=os_d[it, :, tq * 512:(tq + 1) * 512], in_=og[:]),
                             reads=[ogb], writes=[osb[it][tq // 4]], dma=True)
                pending.append(pv)
                if len(pending) > LAG:
                    pending.pop(0)()
        while pending:
            pending.pop(0)()
        if it + 1 < nitems:
            cur = nxt
        P["after_item"](it, osb[it])
    cx.end_phase()


RG = [[0, 1, 2, 3], [4, 5, 6, 7]]


def build_fused():
    cx = Ctx()
    s = cx.s
    I32 = mybir.dt.int32
    di = cx.dram_in
    xT_d = di("xT", [128, KC, T])
    rk_d = cx.nc.dram_tensor("rk", [1, 4], I32, kind="ExternalInput").ap()
    out_d = cx.dram_out("out_fm", [128, KC, T])
    A = {"gains": di("A_gains", [128, 2, KC]), "ffn": [(di("A_fwgu0", [NFC, 128, 2048]), di("A_fwd0", [NFC, 128, D]))],
         "w_proj": di("A_w_proj", [8, 128, 2048]), "w_vb": di("A_w_vb", [128, KC, 512]), "vg_rep": di("A_vg_rep", [128, 512]),
         "wsT": di("A_wsT", [128, 4, 128]), "trilT": di("A_trilT", [128, 4, 128]), "bs_rep": di("A_bs_rep", [128, 4, 128])}
    Bc = {"masks": di("B_masks", [2, 128, 512], BF16), "ident": di("ident", [128, 128], BF16)}
    C = {"gains": di("C_gains", [128, 3, KC]), "w_out": di("C_w_out", [4, 128, 2048]),
         "ffn": [(di("C_fwgu0", [NFC, 128, 2048]), di("C_fwd0", [NFC, 128, D])), (di("C_fwgu1", [NFC, 128, 2048]), di("C_fwd1", [NFC, 128, D]))],
         "wq": di("D_wq", [128, KC, 256]), "wk": di("D_wk", [128, KC, 256]), "wv": di("D_wv", [128, KC, 256])}
    Dc = {"erows": di("D_erows", [32, SEQ], BF16), "cb": di("D_cb", [128, 4, 512]), "past": di("D_past", [128, 4, 512]),
          "ownb": di("D_ownb", [128, 4, 512]), "dm": di("D_dm", [128, 4, 512], BF16)}
    E = {"gains": di("E_gains", [128, 2, KC]), "w_out": di("E_w_out", [4, 128, 2048]),
         "ffn": [(di("E_fwgu0", [NFC, 128, 2048]), di("E_fwd0", [NFC, 128, D]))]}
    h_s = cx.dram("h_s", [128, KC, T])
    bo_s = cx.dram("bo_s", [128, 4, T], BF16)
    zsA = cx.dram("zsA", [6, 256, T], BF16)
    zgA = cx.dram("zgA", [6, 1024, T], BF16)
    osA = cx.dram("osA", [2, 64, SEQ], BF16)
    ogA = cx.dram("ogA", [2, 256, SEQ], BF16)
    hs_d = cx.dram("hs_d", [4, 1024, 512], BF16)
    hg_d = cx.dram("hg_d", [4, 4096, 512], BF16)
    qk_s = cx.dram("qk_s", [4, 2, 64, SEQ], BF16)
    v_s = cx.dram("v_s", [4, 128, 64, 128], BF16)
    osC = cx.dram("osC", [4, 64, SEQ], BF16)
    ogC = cx.dram("ogC", [4, 256, SEQ], BF16)
    cx.alloc_psum()

    cx.rk = {}
    cx.rk64 = {}
    cx.rk2048 = {}

    def mk_init(name):
        def init(eng):
            reg = eng.alloc_register("rkreg_" + name)
            eng.reg_load(reg, rk_d[0:1, 0:1])
            cx.rk[name] = eng.snap(reg, min_val=0, max_val=3)
            reg2 = eng.alloc_register("rkreg64_" + name)
            eng.reg_load(reg2, rk_d[0:1, 1:2])
            cx.rk64[name] = eng.snap(reg2, min_val=0, max_val=192)
            if name == "sp":
                reg3 = eng.alloc_register("rkreg2k_" + name)
                eng.reg_load(reg3, rk_d[0:1, 2:3])
                cx.rk2048[name] = eng.snap(reg3, min_val=0, max_val=3 * T)
        return init
    s.sp_init = {"sp": mk_init("sp")}
    gdone = cx.buf()

    cct = {}

    def gather(key, src, dst, rbufs):
        s.op("pool", lambda e: e.collective_compute("AllGather", ALU.bypass, replica_groups=RG, ins=[src], outs=[dst]),
             reads=rbufs, cc=True)
        cct[key] = s.cccount

    def make_sink(name, zs, zg, nunits):
        bufs = [cx.buf() for _ in range(nunits)]
        todo = []

        def sink(ch, tt, tsl, stg, stgb):
            u, hf = ch // 2, ch % 2
            s.op("sp", lambda e: e.dma_start(out=zs[u, hf * 128:(hf + 1) * 128, tsl], in_=stg[:]), reads=[stgb], writes=[bufs[u]], dma=True)
            if hf == 1 and tt == NTT - 1:
                todo.append(u)

        def flush():
            for u in todo:
                gather((name, u), zs[u], zg[u], [bufs[u]])
        return sink, flush

    sinkA, flushA = make_sink("A", zsA, zgA, 6)
    phase_tok(cx, {"h_in": xT_d, "h_out": h_s, "gains": A["gains"], "ffn_w": A["ffn"], "w_proj": A["w_proj"], "proj_units": 8,
                   "z_sink": sinkA, "flush": flushA,
                   "gmlp": {"w_vb": A["w_vb"], "vg_rep": A["vg_rep"], "wsT": A["wsT"], "trilT": A["trilT"], "bs_rep": A["bs_rep"], "bo_dst": bo_s}})

    def after_head_B(hh, osb):
        if hh == 1:
            for h2 in range(2):
                gather(("oA", h2), osA[h2], ogA[h2], list(osb))
    phase_dil(cx, {"zg": zgA, "os": osA, "masks": Bc["masks"], "ident": Bc["ident"], "after_head": after_head_B,
                   "wait": lambda hh: cct[("A", 3 * hh + 2)]})

    def o_load_C(xn, xb):
        for hh in range(2):
            s.op("sp", lambda e, hh=hh: e.dma_start(out=xn[hh * 64:hh * 64 + 64, 0:4, :], in_=ogA[hh].rearrange("(r x) t -> x r t", r=4)[:, :, bass.ds(cx.rk2048["sp"], T)]),
                 writes=list(xb), dma=True, wait_cc=cct[("oA", hh)])
        for tt in range(NTT):
            tsl = slice(tt * 512, (tt + 1) * 512)
            s.op("sp", lambda e, tsl=tsl: e.dma_start(out=xn[:, 4:8, tsl], in_=bo_s[:, :, tsl]), writes=[xb[tt]], dma=True)
    def hn_out_C(xn, xb):
        for tt in range(NTT):
            b = cx.buf()
            tsl = slice(tt * 512, (tt + 1) * 512)
            s.op("sp", lambda e, tt=tt, tsl=tsl: e.dma_start(out=hs_d[tt].rearrange("(c p) t -> p c t", p=128), in_=xn[:, :, tsl]),
                 reads=[xb[tt]], writes=[b], dma=True)
            gather(("H", tt), hs_d[tt], hg_d[tt], [b])
    phase_tok(cx, {"h_in": h_s, "h_out": h_s, "gains": C["gains"], "o_load": o_load_C, "w_out": C["w_out"], "ffn_w": C["ffn"],
                   "hn_out": hn_out_C})

    phase_hproj(cx, {"hg": hg_d, "qk_s": qk_s, "v_s": v_s, "wq": C["wq"], "wk": C["wk"], "wv": C["wv"], "wait": lambda qd: cct[("H", qd)]})

    def after_item_D(it, osb):
        gather(("oC", it), osC[it], ogC[it], list(osb))
    phase_moba(cx, {"qk_s": qk_s, "v_s": v_s, "os": osC, "erows": Dc["erows"], "ident": Bc["ident"], "cb": Dc["cb"], "past": Dc["past"],
                    "ownb": Dc["ownb"], "dm": Dc["dm"], "after_item": after_item_D})

    def o_load_E(xn, xb):
        for it in range(4):
            ps = slice((it % 2) * 64, (it % 2) * 64 + 64)
            s.op("sp", lambda e, it=it, ps=ps: e.dma_start(out=xn[ps, (it // 2):8:2, :], in_=ogC[it].rearrange("(r x) t -> x r t", r=4)[:, :, bass.ds(cx.rk2048["sp"], T)]),
                 writes=list(xb), dma=True, wait_cc=cct[("oC", it)])
    phase_tok(cx, {"h_in": h_s, "h_out": out_d, "gains": E["gains"], "o_load": o_load_E, "w_out": E["w_out"], "ffn_w": E["ffn"], "final": True})
    cx.stack.close()
    return cx.nc


BF = ml_dtypes.bfloat16
_PROGS = {}


def _gains(*gs):
    return np.ascontiguousarray(np.stack([gain_fm(np.asarray(g, np.float32)) for g in gs], axis=1))


def _head_perm(w, nheads, per_core):
    cols = []
    for i in range(per_core):
        for t in range(3):
            for c in range(4):
                h = per_core * c + i
                cols.append(w[:, t * nheads * 64 + h * 64: t * nheads * 64 + (h + 1) * 64])
    return np.concatenate(cols, axis=1)


def _dil_consts():
    k = np.arange(128)[:, None]
    q = np.arange(128)[None, :]
    prev = (k >= q).astype(np.float32)
    own = (k <= q).astype(np.float32)
    z = np.zeros_like(prev)
    m = np.stack([np.concatenate([prev, own, prev, own], 1), np.concatenate([z, own, prev, own], 1)], 0)
    return m.astype(BF)


def _moba_consts():
    er = (np.arange(SEQ)[None, :] // 256 == np.arange(32)[:, None]).astype(np.float32).astype(BF)
    ident = np.eye(128, dtype=np.float32).astype(BF)
    cb = np.zeros((128, 4, 16, 32), np.float32)
    past = np.zeros((128, 4, 16, 32), np.float32)
    ownb = np.zeros((128, 4, 16, 32), np.float32)
    j = np.arange(32)
    for grp in range(4):
        for i in range(16):
            own = (grp * 16 + i) // 2
            cb[:, grp, i, :] = np.where(j < own, 0.0, -2 * BIG)
            past[:, grp, i, :] = (j < own)
            ownb[:, grp, i, :] = np.where(j == own, 0.0, -BIG)
    dm = np.ones((128, 4, 512), np.float32)
    kk = np.arange(128)[:, None]
    qq = np.arange(512)[None, :]
    for di in range(4):
        kpos = 128 * di + kk
        same = (kpos // 256) == (qq // 256)
        dm[:, di, :] = np.where(same & (qq < kpos), 0.0, 1.0)
    return {"D_erows": er, "ident": ident, "D_cb": cb.reshape(128, 4, 512), "D_past": past.reshape(128, 4, 512),
            "D_ownb": ownb.reshape(128, 4, 512), "D_dm": dm.astype(BF)}


def kernel(x, ffn1_norm, ffn1_w_gate, ffn1_w_up, ffn1_w_down, mix_norm,
           ffn2_norm, ffn2_w_gate, ffn2_w_up, ffn2_w_down,
           ab_w_in, ab_v_norm, ab_w_spatial, ab_b_spatial, ab_w_out,
           c_w_in, c_w_out, final_norm):
    f = lambda a: np.asarray(a, dtype=np.float32)
    xf = f(x).reshape(-1, D)
    w_in = f(ab_w_in)[0]
    common = {
        "A_gains": _gains(f(ffn1_norm)[0], f(mix_norm)[0]),
        "A_fwgu0": pack_wgu(f(ffn1_w_gate)[0], f(ffn1_w_up)[0]), "A_fwd0": pack_wd(f(ffn1_w_down)[0]),
        "A_w_proj": pack_pairs(np.concatenate([_head_perm(w_in[:, :1536], 8, 2), w_in[:, 1536:2048]], axis=1)),
        "A_w_vb": np.ascontiguousarray(w_in[:, 2048:2560].reshape(KC, 128, 512).transpose(1, 0, 2)),
        "A_vg_rep": np.ascontiguousarray(np.broadcast_to(f(ab_v_norm)[0].reshape(1, 512), (128, 512))),
        "A_wsT": np.ascontiguousarray(f(ab_w_spatial)[0].transpose(2, 0, 1)),
        "A_trilT": np.ascontiguousarray(np.broadcast_to(np.triu(np.ones((128, 128), np.float32))[:, None, :], (128, 4, 128))),
        "A_bs_rep": np.ascontiguousarray(np.broadcast_to(f(ab_b_spatial)[0][None], (128, 4, 128))),
        "B_masks": _dil_consts(),
        "C_gains": _gains(f(ffn2_norm)[0], f(ffn1_norm)[1], f(mix_norm)[1]),
        "C_w_out": pack_pairs(f(ab_w_out)[0]),
        "C_fwgu0": pack_wgu(f(ffn2_w_gate)[0], f(ffn2_w_up)[0]), "C_fwd0": pack_wd(f(ffn2_w_down)[0]),
        "C_fwgu1": pack_wgu(f(ffn1_w_gate)[1], f(ffn1_w_up)[1]), "C_fwd1": pack_wd(f(ffn1_w_down)[1]),
        "E_gains": _gains(f(ffn2_norm)[1], f(final_norm)),
        "E_w_out": pack_pairs(f(c_w_out)[0]),
        "E_fwgu0": pack_wgu(f(ffn2_w_gate)[1], f(ffn2_w_up)[1]), "E_fwd0": pack_wd(f(ffn2_w_down)[1]),
    }
    common.update(_moba_consts())
    cw = f(c_w_in)[0]

    def wslice(t, j):
        w = cw[:, t * 1024 + 256 * j: t * 1024 + 256 * (j + 1)]
        return np.ascontiguousarray(w.reshape(KC, 128, 256).transpose(1, 0, 2))
    ins = [dict(common, xT=to_fm(xf[c * T:(c + 1) * T]), rk=np.array([[c % 4, (c % 4) * 64, (c % 4) * T, 0]], np.int32),
                D_wq=wslice(0, c % 4), D_wk=wslice(1, c % 4), D_wv=wslice(2, c % 4)) for c in range(NCORES)]
    if "F" not in _PROGS:
        _PROGS["F"] = build_fused()
    res = run_bass_kernel_spmd(_PROGS["F"], ins, core_ids=list(range(NCORES))).results
    out = np.concatenate([from_fm(np.asarray(res[c]["out_fm"])) for c in range(NCORES)], axis=0)
    return out.reshape(2, SEQ, D).astype(np.float32)
```

```python
import contextlib
import numpy as np
import ml_dtypes
import concourse.bass as bass
import concourse.mybir as mybir
from concourse.bass_utils import run_bass_kernel_spmd

F32 = mybir.dt.float32
BF16 = mybir.dt.bfloat16
ALU = mybir.AluOpType
AF = mybir.ActivationFunctionType
AX = mybir.AxisListType

NCORES = 8
D = 1024
KC = 8
DFF = 2816
NFC = 22
T = 2048
NTT = 4
SEQ = 8192
EPS = 1e-6
FC_GROUPS = [(0, 6), (6, 12), (12, 18), (18, 22)]


class Buf:
    __slots__ = ("name", "lw", "rd")

    def __init__(self, name):
        self.name = name
        self.lw = None
        self.rd = []


class _Op:
    __slots__ = ("eng", "fn", "deps", "dma", "signal", "cnt", "dsem", "dtarget", "dprev", "cc", "wait_cc")

    def __init__(self, eng, fn, deps, dma):
        self.eng = eng
        self.fn = fn
        self.deps = deps
        self.dma = dma
        self.signal = False
        self.cnt = 0
        self.dsem = None
        self.dtarget = 0
        self.dprev = 0
        self.cc = False
        self.wait_cc = 0


ENGINES = ("pe", "act", "dve", "pool", "sp")
NDSEM = 10


class Sched:
    def __init__(self, nc, stack):
        self.nc = nc
        self.stack = stack
        self.ops = []
        self.bufs = set()
        self.csem = None
        self.ccount = {e: 0 for e in ENGINES}
        self.dcount = {e: 0 for e in ("sp", "pool", "act")}
        self.cccount = 0
        self.sp_init = None

    def op(self, eng, fn, reads=(), writes=(), dma=False, cc=False, wait_cc=0):
        idx = len(self.ops)
        deps = set()
        for b in reads:
            if b.lw is not None:
                deps.add(b.lw)
        for b in writes:
            if b.lw is not None:
                deps.add(b.lw)
            deps.update(b.rd)
        deps.discard(idx)
        if eng == "pe" and not dma:
            deps = {d for d in deps if not (self.ops[d].eng == "pe" and not self.ops[d].dma)}
        o = _Op(eng, fn, sorted(deps), dma or cc)
        o.cc = cc
        o.wait_cc = wait_cc
        if cc:
            self.cccount += 1
            o.dtarget = self.cccount
        self.ops.append(o)
        for b in reads:
            b.rd.append(idx)
            self.bufs.add(b)
        for b in writes:
            b.lw = idx
            b.rd = []
            self.bufs.add(b)
        return idx

    def _init_sems(self):
        nc = self.nc
        self.csem = {e: self.stack.enter_context(nc.semaphore("cs_" + e)) for e in ENGINES}
        self.dsems = {e: [self.stack.enter_context(nc.semaphore("ds_%s_%d" % (e, i))) for i in range(NDSEM)]
                      for e in ("sp", "pool", "act")}
        self.ccsem = self.stack.enter_context(nc.semaphore("ccsem"))

    def emit_phase(self, final=False):
        nc = self.nc
        ops = self.ops
        if self.csem is None:
            self._init_sems()
        csem, dsems, ccsem = self.csem, self.dsems, self.ccsem
        for o in ops:
            for d in o.deps:
                if not ops[d].dma:
                    ops[d].signal = True
        last = {}
        for i, o in enumerate(ops):
            if not o.dma:
                last[o.eng] = i
        for i in last.values():
            ops[i].signal = True
        for o in ops:
            if o.cc:
                o.dsem = ccsem
                o.dprev = 0
            elif o.dma:
                n = self.dcount[o.eng]
                self.dcount[o.eng] += 1
                o.dsem = dsems[o.eng][n % NDSEM]
                o.dprev = 16 * (n // NDSEM)
                o.dtarget = o.dprev + 16
            elif o.signal:
                self.ccount[o.eng] += 1
                o.cnt = self.ccount[o.eng]
        final_d = []
        for e in dsems:
            n = self.dcount[e]
            for i in range(NDSEM):
                k = (n - i + NDSEM - 1) // NDSEM if n > i else 0
                if k > 0:
                    final_d.append((dsems[e][i], 16 * k))
        final_c = [(csem[e], self.ccount[e]) for e in ENGINES if self.ccount[e] > 0]
        if self.cccount and final:
            final_d.append((ccsem, self.cccount))

        def run_engine(ename, eng):
            if self.sp_init is not None and ename in self.sp_init:
                self.sp_init.pop(ename)(eng)
            waited = {}
            for o in ops:
                if o.eng != ename:
                    continue
                for d in o.deps:
                    od = ops[d]
                    if od.dma:
                        key, val = ("d", id(od.dsem)), od.dtarget
                        sem = od.dsem
                    else:
                        key, val = ("c", od.eng), od.cnt
                        sem = csem[od.eng]
                    if waited.get(key, 0) < val:
                        eng.wait_ge(sem, val)
                        waited[key] = val
                if o.wait_cc and waited.get("ccw", 0) < o.wait_cc:
                    eng.wait_ge(ccsem, o.wait_cc)
                    waited["ccw"] = o.wait_cc
                if o.cc:
                    o.fn(eng).then_inc(o.dsem)
                elif o.dma:
                    if o.dprev > 0:
                        key = ("d", id(o.dsem))
                        if waited.get(key, 0) < o.dprev:
                            eng.wait_ge(o.dsem, o.dprev)
                            waited[key] = o.dprev
                    o.fn(eng).then_inc(o.dsem, 16)
                else:
                    ins = o.fn(eng)
                    if o.signal:
                        ins.then_inc(csem[ename], 1)
            for sem, val in final_c + final_d:
                eng.wait_ge(sem, val)

        with nc.Block() as block:
            @block.sync
            def _(e):
                run_engine("sp", e)

            @block.tensor
            def _(e):
                run_engine("pe", e)

            @block.scalar
            def _(e):
                run_engine("act", e)

            @block.vector
            def _(e):
                run_engine("dve", e)

            @block.gpsimd
            def _(e):
                run_engine("pool", e)
        for b in self.bufs:
            b.lw = None
            b.rd = []
        self.bufs = set()
        self.ops = []


class Ctx:
    def __init__(self):
        self.nc = bass.Bass("TRN2", target_bir_lowering=False)
        self.stack = contextlib.ExitStack()
        self.pstack = contextlib.ExitStack()
        self.s = Sched(self.nc, self.stack)
        self._n = 0
        self.psum_banks = []
        self.psum_bufs = []
        self.rank = None

    def dram_in(self, name, shape, dt=F32):
        return self.nc.dram_tensor(name, list(shape), dt, kind="ExternalInput").ap()

    def dram_out(self, name, shape, dt=F32):
        return self.nc.dram_tensor(name, list(shape), dt, kind="ExternalOutput").ap()

    def dram(self, name, shape, dt=F32):
        return self.nc.dram_tensor(name, list(shape), dt).ap()

    def sb(self, name, shape, dt):
        self._n += 1
        return self.pstack.enter_context(self.nc.sbuf_tensor("s%d_%s" % (self._n, name), list(shape), dt))

    def alloc_psum(self):
        for i in range(8):
            self.psum_banks.append(self.stack.enter_context(self.nc.psum_tensor("psb%d" % i, [128, 512], F32)))
            self.psum_bufs.append(Buf("psb%d" % i))

    def buf(self, name=None):
        self._n += 1
        return Buf(name or ("b%d" % self._n))

    def end_phase(self, final=False):
        self.s.emit_phase(final)
        self.pstack.close()
        self.pstack = contextlib.ExitStack()

    def finish(self):
        self.end_phase()
        self.stack.close()
        return self.nc


class Ring:
    def __init__(self, items):
        self.items = items
        self.i = 0

    def next(self):
        it = self.items[self.i % len(self.items)]
        self.i += 1
        return it


def emit_norm(cx, hT, hbufs, gcol, gb, xn, xnbufs, ones_bf, ones_b, sq_ring, ps_ring, st_ring):
    s = cx.s
    for tt in range(NTT):
        tsl = slice(tt * 512, (tt + 1) * 512)
        ps, psb = ps_ring.next()
        for c in range(KC):
            sq, sqb = sq_ring.next()
            s.op("act", lambda e, sq=sq, c=c, tsl=tsl: e.activation(out=sq[:], in_=hT[:, c, tsl], func=AF.Square),
                 reads=[hbufs[tt]], writes=[sqb])
            s.op("pe", lambda e, ps=ps, sq=sq, c=c: e.matmul(ps[:], lhsT=ones_bf[:], rhs=sq[:], start=(c == 0), stop=(c == KC - 1)),
                 reads=[sqb, ones_b], writes=[psb])
        (sd, rs), stb = st_ring.next()
        s.op("act", lambda e, ps=ps, sd=sd: e.activation(out=sd[:], in_=ps[:], func=AF.Sqrt, bias=EPS, scale=1.0 / D),
             reads=[psb], writes=[stb])
        s.op("dve", lambda e, sd=sd, rs=rs: e.reciprocal(out=rs[:], in_=sd[:]), reads=[stb], writes=[stb])
        for c in range(KC):
            s.op("dve", lambda e, c=c, tsl=tsl, rs=rs: e.scalar_tensor_tensor(
                out=xn[:, c, tsl], in0=hT[:, c, tsl], scalar=gcol[:, c:c + 1], in1=rs[:], op0=ALU.mult, op1=ALU.mult),
                reads=[hbufs[tt], stb, gb], writes=[xnbufs[tt]])


def emit_ffn(cx, hT, hbufs, xn, xnbufs, wgu_dram, wd_dram, res):
    s = cx.s
    wgu_ring, wd_ring, act_ring, sg_ring = res["wgu_ring"], res["wd_ring"], res["act_ring"], res["sg_ring"]
    psg_ring, psu_ring, psd_ring = res["psg_ring"], res["psu_ring"], res["psd_ring"]
    for (f0, f1) in FC_GROUPS:
        nf = f1 - f0
        act, actb = act_ring.next()
        wds = []
        for fi in range(nf):
            fc = f0 + fi
            wgu, wgub = wgu_ring.next()
            wd, wdb = wd_ring.next()
            wds.append((wd, wdb))
            s.op("pool", lambda e, wgu=wgu, fc=fc: e.dma_start(out=wgu[:], in_=wgu_dram[fc]), writes=[wgub], dma=True)
            s.op("pool", lambda e, wd=wd, fc=fc: e.dma_start(out=wd[:], in_=wd_dram[fc]), writes=[wdb], dma=True)
            for tt in range(NTT):
                tsl = slice(tt * 512, (tt + 1) * 512)
                pg, pgb = psg_ring.next()
                pu, pub = psu_ring.next()
                for kc in range(KC):
                    s.op("pe", lambda e, pg=pg, wgu=wgu, kc=kc, tsl=tsl: e.matmul(
                        pg[:], lhsT=wgu[:, kc * 128:(kc + 1) * 128], rhs=xn[:, kc, tsl], start=(kc == 0), stop=(kc == KC - 1)),
                        reads=[wgub, xnbufs[tt]], writes=[pgb])
                for kc in range(KC):
                    s.op("pe", lambda e, pu=pu, wgu=wgu, kc=kc, tsl=tsl: e.matmul(
                        pu[:], lhsT=wgu[:, 1024 + kc * 128:1024 + (kc + 1) * 128], rhs=xn[:, kc, tsl], start=(kc == 0), stop=(kc == KC - 1)),
                        reads=[wgub, xnbufs[tt]], writes=[pub])
                sg, sgb = sg_ring.next()
                s.op("act", lambda e, sg=sg, pg=pg: e.activation(out=sg[:], in_=pg[:], func=AF.Silu), reads=[pgb], writes=[sgb])
                s.op("dve", lambda e, sg=sg, pu=pu, act=act, fi=fi, tsl=tsl: e.tensor_tensor(
                    out=act[:, fi, tsl], in0=pu[:], in1=sg[:], op=ALU.mult), reads=[pub, sgb], writes=[actb])
        for dc in range(KC):
            for tt in range(NTT):
                tsl = slice(tt * 512, (tt + 1) * 512)
                pd, pdb = psd_ring.next()
                for fi in range(nf):
                    wd, wdb = wds[fi]
                    s.op("pe", lambda e, pd=pd, wd=wd, fi=fi, dc=dc, tsl=tsl, act=act, nf=nf: e.matmul(
                        pd[:], lhsT=wd[:, dc * 128:(dc + 1) * 128], rhs=act[:, fi, tsl], start=(fi == 0), stop=(fi == nf - 1)),
                        reads=[wdb, actb], writes=[pdb])
                s.op("dve", lambda e, pd=pd, dc=dc, tsl=tsl: e.scalar_tensor_tensor(
                    out=hT[:, dc, tsl], in0=pd[:], scalar=0.5, in1=hT[:, dc, tsl], op0=ALU.mult, op1=ALU.add),
                    reads=[pdb, hbufs[tt]], writes=[hbufs[tt]])


def alloc_ffn_resources(cx):
    res = {}
    mk = lambda name, shape, dt, n: Ring([(cx.sb("%s%d" % (name, i), shape, dt), cx.buf()) for i in range(n)])
    res["wgu_ring"] = mk("wgu", [128, 2048], BF16, 3)
    res["wd_ring"] = mk("wd", [128, 1024], BF16, 8)
    res["act_ring"] = mk("actb", [128, 6, 2048], BF16, 1)
    res["sg_ring"] = mk("sg", [128, 512], F32, 2)
    pb = list(zip(cx.psum_banks, cx.psum_bufs))
    res["psg_ring"] = Ring(pb[0:2])
    res["psu_ring"] = Ring(pb[2:4])
    res["psd_ring"] = Ring(pb[4:6])
    res["psm_ring"] = Ring(pb[6:8])
    res["sq_ring"] = mk("sq", [128, 512], BF16, 3)
    res["st_ring"] = Ring([((cx.sb("sd%d" % i, [128, 512], F32), cx.sb("rs%d" % i, [128, 512], F32)), cx.buf()) for i in range(1)])
    return res


def to_fm(x2d):
    t = x2d.shape[0]
    return np.ascontiguousarray(x2d.reshape(t, KC, 128).transpose(2, 1, 0))


def from_fm(a):
    t = a.shape[2]
    return np.ascontiguousarray(a.transpose(2, 1, 0).reshape(t, KC * 128))


def gain_fm(g):
    return np.ascontiguousarray(g.reshape(KC, 128).T)


def pack_wgu(wg, wu):
    def one(w):
        return w.reshape(KC, 128, NFC, 128).transpose(2, 1, 0, 3).reshape(NFC, 128, KC * 128)
    return np.ascontiguousarray(np.concatenate([one(wg), one(wu)], axis=2))


def pack_wd(wd):
    return np.ascontiguousarray(wd.reshape(NFC, 128, D))


def emit_proj_fm(cx, xn, xnbufs, w_dram, nunits, res, sink, kin=KC):
    s = cx.s
    wgu_ring, ring = res["wgu_ring"], res["psg_ring"]
    for i in range(nunits):
        wgu, wgub = wgu_ring.next()
        s.op("pool", lambda e, wgu=wgu, i=i: e.dma_start(out=wgu[:, 0:2 * kin * 128], in_=w_dram[i]), writes=[wgub], dma=True)
        for hf in range(2):
            for tt in range(NTT):
                tsl = slice(tt * 512, (tt + 1) * 512)
                ps, psb = ring.next()
                for kc in range(kin):
                    s.op("pe", lambda e, ps=ps, wgu=wgu, kc=kc, hf=hf, tsl=tsl: e.matmul(
                        ps[:], lhsT=wgu[:, (hf * kin + kc) * 128:(hf * kin + kc + 1) * 128], rhs=xn[:, kc, tsl],
                        start=(kc == 0), stop=(kc == kin - 1)), reads=[wgub, xnbufs[tt]], writes=[psb])
                sink(i, hf, tt, ps, psb)


def pack_pairs(w):
    k, n = w.shape
    kin = k // 128
    nu = n // 256
    a = w.reshape(kin, 128, nu, 2, 128).transpose(2, 1, 3, 0, 4)
    return np.ascontiguousarray(a.reshape(nu, 128, 2 * kin * 128))


GELU_C = 0.044715
GELU_S = 1.5957691216057308


def emit_gelu(cx, src, srcb, dst, dstb, tmp_ring, reads_extra=()):
    s = cx.s
    (t1, t2), tb = tmp_ring.next()
    s.op("act", lambda e: e.activation(out=t1[:], in_=src, func=AF.Square), reads=[srcb], writes=[tb])
    s.op("dve", lambda e: e.tensor_scalar(out=t1[:], in0=t1[:], scalar1=GELU_C, scalar2=1.0, op0=ALU.mult, op1=ALU.add),
         reads=[tb], writes=[tb])
    s.op("dve", lambda e: e.tensor_tensor(out=t1[:], in0=t1[:], in1=src, op=ALU.mult), reads=[tb, srcb], writes=[tb])
    s.op("act", lambda e: e.activation(out=t2[:], in_=t1[:], func=AF.Sigmoid, scale=GELU_S), reads=[tb], writes=[tb])
    s.op("dve", lambda e: e.tensor_tensor(out=dst, in0=t2[:], in1=src, op=ALU.mult), reads=[tb, srcb], writes=[dstb])


def phase_tok(cx, P):
    s = cx.s
    ffn_w = P.get("ffn_w", [])
    proj_units = P.get("proj_units", 0)
    gm = P.get("gmlp")
    final = P.get("final", False)
    hn_out = P.get("hn_out")
    ng = len(ffn_w) + (1 if (proj_units or final or hn_out) else 0)
    hT = cx.sb("hT", [128, KC, T], F32)
    xn = cx.sb("xn", [128, KC, T], BF16)
    gcol = cx.sb("gcol", [128, ng, KC], F32)
    ones = cx.sb("ones", [128, 128], BF16)
    hb = [cx.buf() for _ in range(NTT)]
    xb = [cx.buf() for _ in range(NTT)]
    gb, ob = cx.buf(), cx.buf()
    res = alloc_ffn_resources(cx)
    stg_ring = Ring([(cx.sb("stg%d" % i, [128, 512], BF16), cx.buf()) for i in range(4)])
    h_d, g_d, hout_d = P["h_in"], P["gains"], P["h_out"]
    for tt in range(NTT):
        s.op("sp", lambda e, tt=tt: e.dma_start(out=hT[:, :, tt * 512:(tt + 1) * 512], in_=h_d[:, :, tt * 512:(tt + 1) * 512]),
             writes=[hb[tt]], dma=True)
    s.op("sp", lambda e: e.dma_start(out=gcol[:], in_=g_d[:, :, :]), writes=[gb], dma=True)
    s.op("dve", lambda e: e.memset(ones[:], 1.0), writes=[ob])
    gi = 0
    if P.get("o_load") is not None:
        P["o_load"](xn, xb)
        wo_d = P["w_out"]

        def sink_mix(i, hf, tt, ps, psb):
            dc = 2 * i + hf
            tsl = slice(tt * 512, (tt + 1) * 512)
            s.op("dve", lambda e: e.tensor_tensor(out=hT[:, dc, tsl], in0=ps[:], in1=hT[:, dc, tsl], op=ALU.add),
                 reads=[psb, hb[tt]], writes=[hb[tt]])
        emit_proj_fm(cx, xn, xb, wo_d, 4, res, sink_mix, kin=8)
    for fi in range(len(ffn_w)):
        emit_norm(cx, hT, hb, gcol[:, gi, :], gb, xn, xb, ones, ob, res["sq_ring"], res["psm_ring"], res["st_ring"])
        gi += 1
        emit_ffn(cx, hT, hb, xn, xb, ffn_w[fi][0], ffn_w[fi][1], res)
    if proj_units:
        emit_norm(cx, hT, hb, gcol[:, gi, :], gb, xn, xb, ones, ob, res["sq_ring"], res["psm_ring"], res["st_ring"])
        gi += 1
        act, actb = res["act_ring"].items[0]
        gu = act
        nz_units = proj_units - (2 if gm else 0)
        tmp_ring = Ring([((cx.sb("gt1_%d" % i, [128, 512], F32), cx.sb("gt2_%d" % i, [128, 512], F32)), cx.buf()) for i in range(2)])

        def sink_z(i, hf, tt, ps, psb):
            tsl = slice(tt * 512, (tt + 1) * 512)
            ch = 2 * i + hf
            if i < nz_units:
                stg, stgb = stg_ring.next()
                s.op("act", lambda e: e.activation(out=stg[:], in_=ps[:], func=AF.Copy), reads=[psb], writes=[stgb])
                P["z_sink"](ch, tt, tsl, stg, stgb)
            else:
                g = ch - 2 * nz_units
                emit_gelu(cx, ps[:], psb, gu[:, g, tsl], actb, tmp_ring)
        emit_proj_fm(cx, xn, xb, P["w_proj"], proj_units, res, sink_z)
        if P.get("flush") is not None and not gm:
            P["flush"]()
        if gm:
            wvb_d, vg_d, ws_d, tril_d, bs_d, bo_d = gm["w_vb"], gm["vg_rep"], gm["wsT"], gm["trilT"], gm["bs_rep"], gm["bo_dst"]
            wd0 = cx.sb("wvb", [128, KC, 512], BF16)
            wd0b = cx.buf()
            s.op("pool", lambda e: e.dma_start(out=wd0[:, :, :], in_=wvb_d[:, :, :]), writes=[wd0b], dma=True)
            if P.get("flush") is not None:
                P["flush"]()
            bo_ring = Ring([(cx.sb("bo%d" % i, [128, 4, 128], BF16), cx.buf()) for i in range(2)])
            vg = cx.sb("vg", [128, 512], F32)
            wsT = cx.sb("wsT", [128, 4, 128], F32)
            tril = cx.sb("tril", [128, 4, 128], F32)
            wsm = cx.sb("wsm", [128, 4, 128], BF16)
            bsr = cx.sb("bsr", [128, 4, 128], F32)
            cb = cx.buf()
            s.op("sp", lambda e: e.dma_start(out=vg[:], in_=vg_d[:, :]), writes=[cb], dma=True)
            s.op("sp", lambda e: e.dma_start(out=wsT[:], in_=ws_d[:, :, :]), writes=[cb], dma=True)
            s.op("sp", lambda e: e.dma_start(out=tril[:], in_=tril_d[:, :, :]), writes=[cb], dma=True)
            s.op("sp", lambda e: e.dma_start(out=bsr[:], in_=bs_d[:, :, :]), writes=[cb], dma=True)
            s.op("dve", lambda e: e.tensor_tensor(out=wsm[:], in0=wsT[:], in1=tril[:], op=ALU.mult), reads=[cb], writes=[cb])
            gv_ring = Ring([(cx.sb("gv%d" % i, [128, 512], F32), cx.buf()) for i in range(2)])
            sqv_ring = Ring([(cx.sb("sqv%d" % i, [128, 512], F32), cx.buf()) for i in range(2)])
            vn_ring = Ring([(cx.sb("vn%d" % i, [128, 512], BF16), cx.buf()) for i in range(2)])
            ss_ring = Ring([((cx.sb("ss%d" % i, [128, 4], F32), cx.sb("rr%d" % i, [128, 4], F32)), cx.buf()) for i in range(2)])
            mx_ring = Ring([(cx.sb("mxd%d" % i, [128, 512], F32), cx.buf()) for i in range(2)])
            for n in range(T // 128):
                nsl = slice(n * 128, (n + 1) * 128)
                tt = n // 4
                ps, psb = res["psu_ring"].next()
                for kc in range(KC):
                    s.op("pe", lambda e, ps=ps, kc=kc, nsl=nsl: e.matmul(ps[:], lhsT=xn[:, kc, nsl], rhs=wd0[:, kc, 0:512],
                                                                       start=(kc == 0), stop=(kc == KC - 1)),
                         reads=[wd0b, xb[tt]], writes=[psb])
                gv, gvb = gv_ring.next()
                emit_gelu(cx, ps[:], psb, gv[:], gvb, tmp_ring)
                sqv, sqvb = sqv_ring.next()
                (ss, rr), ssb = ss_ring.next()
                s.op("dve", lambda e, sqv=sqv, gv=gv: e.tensor_tensor(out=sqv[:], in0=gv[:], in1=gv[:], op=ALU.mult), reads=[gvb], writes=[sqvb])
                s.op("dve", lambda e, sqv=sqv, ss=ss: e.tensor_reduce(out=ss[:], in_=sqv[:].rearrange("p (g c) -> p g c", g=4), axis=AX.X, op=ALU.add),
                     reads=[sqvb], writes=[ssb])
                s.op("act", lambda e, ss=ss: e.activation(out=ss[:], in_=ss[:], func=AF.Sqrt, bias=EPS, scale=1.0 / 128), reads=[ssb], writes=[ssb])
                s.op("dve", lambda e, ss=ss, rr=rr: e.reciprocal(out=rr[:], in_=ss[:]), reads=[ssb], writes=[ssb])
                vn, vnb = vn_ring.next()
                for g in range(4):
                    s.op("dve", lambda e, g=g, vn=vn, gv=gv, rr=rr: e.scalar_tensor_tensor(
                        out=vn[:, g * 128:(g + 1) * 128], in0=gv[:, g * 128:(g + 1) * 128], scalar=rr[:, g:g + 1],
                        in1=vg[:, g * 128:(g + 1) * 128], op0=ALU.mult, op1=ALU.mult), reads=[gvb, ssb, cb], writes=[vnb])
                pm, pmb = res["psd_ring"].next()
                for g in range(4):
                    s.op("pe", lambda e, g=g, pm=pm, vn=vn: e.matmul(pm[:, g * 128:(g + 1) * 128], lhsT=vn[:, g * 128:(g + 1) * 128],
                                                                   rhs=wsm[:, g, :], start=True, stop=True),
                         reads=[vnb, cb], writes=[pmb])
                mx, mxb = mx_ring.next()
                s.op("dve", lambda e, mx=mx, pm=pm: e.tensor_tensor(out=mx[:], in0=pm[:], in1=bsr[:].rearrange("p g c -> p (g c)"), op=ALU.add),
                     reads=[pmb, cb], writes=[mxb])
                bo, bob = bo_ring.next()
                s.op("dve", lambda e, mx=mx, nsl=nsl, bo=bo: e.tensor_tensor(out=bo[:], in0=mx[:].rearrange("p (g c) -> p g c", g=4),
                                                                             in1=gu[:, 0:4, nsl], op=ALU.mult), reads=[mxb, actb], writes=[bob])
                s.op("sp", lambda e, bo=bo, nsl=nsl: e.dma_start(out=bo_d[:, :, nsl], in_=bo[:]), reads=[bob], dma=True)
    if hn_out is not None:
        emit_norm(cx, hT, hb, gcol[:, gi, :], gb, xn, xb, ones, ob, res["sq_ring"], res["psm_ring"], res["st_ring"])
        gi += 1
        hn_out(xn, xb)
    if final:
        emit_norm(cx, hT, hb, gcol[:, gi, :], gb, hT, hb, ones, ob, res["sq_ring"], res["psm_ring"], res["st_ring"])
    for tt in range(NTT):
        s.op("sp", lambda e, tt=tt: e.dma_start(out=hout_d[:, :, tt * 512:(tt + 1) * 512], in_=hT[:, :, tt * 512:(tt + 1) * 512]),
             reads=[hb[tt]], dma=True)
    cx.end_phase(final=final)


DILS = (1, 4, 16)
BIG = 30000.0
LAG = 3


def emit_vext(cx, vx, vxb, vsrc, vsrcb, colsl, ident_blk, identb, pt_ring):
    s = cx.s
    for g in range(4):
        pt, ptb = pt_ring.next()
        ptv = pt[:].bitcast(BF16)
        for i in range(16):
            blk = g * 16 + i
            s.op("pe", lambda e, ptv=ptv, i=i, blk=blk: e.transpose(out=ptv[:, i * 64:(i + 1) * 64], in_=vsrc[:, colsl(blk)], identity=ident_blk),
                 reads=[vsrcb, identb], writes=[ptb])
        s.op("act", lambda e, ptv=ptv, g=g: e.activation(out=vx[:, g * 16:(g + 1) * 16, 0:64], in_=ptv[:, :].rearrange("p (b c) -> p b c", c=64), func=AF.Copy),
             reads=[ptb], writes=[vxb])


def phase_dil(cx, P):
    s = cx.s
    zg, os_d, m_d, id_d = P["zg"], P["os"], P["masks"], P["ident"]
    pb = list(zip(cx.psum_banks, cx.psum_bufs))
    st_ring, po_ring, pt_ring = Ring(pb[0:5]), Ring(pb[5:7]), Ring(pb[7:8])
    qkv = cx.sb("qkv", [128, 3, SEQ], BF16)
    q, k, vT = qkv[:, 0, :], qkv[:, 1, :], qkv[:, 2, :]
    ldb = [cx.buf(), cx.buf()]
    acc = cx.sb("acc", [128, SEQ], F32)
    accall = cx.buf()
    masks = cx.sb("masks", [128, 2, 512], BF16)
    ident = cx.sb("ident", [128, 128], BF16)
    mb, idb = cx.buf(), cx.buf()
    s.op("sp", lambda e: e.dma_start(out=masks[:], in_=m_d.rearrange("m p f -> p m f")), writes=[mb], dma=True)
    s.op("sp", lambda e: e.dma_start(out=ident[:], in_=id_d[:, :]), writes=[idb], dma=True)
    for hh in range(2):
        for t in range(3):
            s.op("sp", lambda e, hh=hh, t=t: e.dma_start(
                out=qkv[hh * 64:hh * 64 + 64, t, :].rearrange("p (r t) -> p r t", r=4),
                in_=zg[3 * hh + t].rearrange("(r x) t -> x r t", r=4)[bass.ds(cx.rk64["sp"], 64)]), writes=[ldb[hh]], dma=True,
                 wait_cc=P["wait"](hh))
    vx_items = []
    for i in range(2):
        vx = cx.sb("vx%d" % i, [128, 64, 128], BF16)
        vxb = cx.buf()
        s.op("pool", lambda e, vx=vx: e.memset(vx[:, :, 64:128], 1.0), writes=[vxb])
        vx_items.append((vx, vxb))
    vx_ring = Ring(vx_items)
    p_ring = Ring([(cx.sb("p%d" % i, [128, 512], BF16), cx.buf()) for i in range(6)])
    rsh_ring = Ring([(cx.sb("rsh%d" % i, [64, 2048], F32), cx.buf()) for i in range(2)])
    rcp_ring = Ring([(cx.sb("rcp%d" % i, [128, 2048], F32), cx.buf()) for i in range(1)])
    ostg_ring = Ring([(cx.sb("ostg%d" % i, [64, 2048], BF16), cx.buf()) for i in range(2)])
    osb = [[cx.buf() for _ in range(4)] for _ in range(2)]
    qkvp = cx.sb("qkvp", [128, 3, SEQ], BF16)
    qp, kp, vp = qkvp[:, 0, :], qkvp[:, 1, :], qkvp[:, 2, :]
    qpb, kpb, vpb = cx.buf(), cx.buf(), cx.buf()

    def colsl(blk):
        return slice(blk * 128, (blk + 1) * 128)
    for hh in range(2):
        hs = slice(hh * 64, hh * 64 + 64)
        for pi, d in enumerate(DILS):
            nb = 64 // d
            qb = kb = vTb = ldb[hh]
            if d == 1:
                qs, ks, vs_, qsb, ksb, vsb_ = q, k, vT, qb, kb, vTb
            else:
                s.op("dve", lambda e, d=d, hs=hs: e.tensor_copy(out=qp[hs].rearrange("p (r m) -> p r m", r=d),
                                                                in_=q[hs].rearrange("p (m r) -> p r m", r=d)), reads=[qb], writes=[qpb])
                for (src, srcb, dst, dstb) in ((k, kb, kp, kpb), (vT, vTb, vp, vpb)):
                    s.op("act", lambda e, src=src, dst=dst, d=d, hs=hs: e.activation(out=dst[hs].rearrange("p (r m) -> p r m", r=d),
                                                                                    in_=src[hs].rearrange("p (m r) -> p r m", r=d), func=AF.Copy),
                         reads=[srcb], writes=[dstb])
                qs, ks, vs_, qsb, ksb, vsb_ = qp, kp, vp, qpb, kpb, vpb
            v, vb = vx_ring.next()
            emit_vext(cx, v, vb, vs_[hs], vsb_, colsl, ident[hs, hs], idb, pt_ring)
            po, pob = None, None
            pending = []
            for jp in range(32):
                j0 = 2 * jp
                n0 = j0 % nb
                first = (n0 == 0)
                ps, psb = st_ring.next()
                kprev = j0 if first else j0 - 1
                for ci, (kblk, qblk) in enumerate(((kprev, j0), (j0, j0), (j0, j0 + 1), (j0 + 1, j0 + 1))):
                    s.op("pe", lambda e, ps=ps, ci=ci, kblk=kblk, qblk=qblk, ks=ks, qs=qs, hs=hs: e.matmul(
                        ps[:, ci * 128:(ci + 1) * 128], lhsT=ks[hs, colsl(kblk)], rhs=qs[hs, colsl(qblk)],
                        start=True, stop=True), reads=[ksb, qsb], writes=[psb])
                p, pbuf = p_ring.next()
                s.op("act", lambda e, p=p, ps=ps: e.activation(out=p[:], in_=ps[:], func=AF.Exp, scale=0.125), reads=[psb], writes=[pbuf])
                mi = 1 if first else 0
                s.op("dve", lambda e, p=p, mi=mi: e.tensor_tensor(out=p[:], in0=p[:], in1=masks[:, mi, :], op=ALU.mult),
                     reads=[pbuf, mb], writes=[pbuf])
                if jp % 2 == 0:
                    po, pob = po_ring.next()

                def pv(po=po, pob=pob, v=v, vb=vb, j0=j0, p=p, pbuf=pbuf, first=first, jp=jp, nb=nb, d=d, pi=pi):
                    c0 = (j0 % 4) * 128
                    if not first:
                        s.op("pe", lambda e: e.matmul(po[:, c0:c0 + 128], lhsT=v[:, j0 - 1, :], rhs=p[:, 0:128], start=True, stop=False),
                             reads=[vb, pbuf], writes=[pob])
                    s.op("pe", lambda e: e.matmul(po[:, c0:c0 + 128], lhsT=v[:, j0, :], rhs=p[:, 128:256], start=first, stop=True),
                         reads=[vb, pbuf], writes=[pob])
                    s.op("pe", lambda e: e.matmul(po[:, c0 + 128:c0 + 256], lhsT=v[:, j0, :], rhs=p[:, 256:384], start=True, stop=False),
                         reads=[vb, pbuf], writes=[pob])
                    s.op("pe", lambda e: e.matmul(po[:, c0 + 128:c0 + 256], lhsT=v[:, j0 + 1, :], rhs=p[:, 384:512], start=False, stop=True),
                         reads=[vb, pbuf], writes=[pob])
                    if jp % 2 == 1:
                        j = j0 - 2
                        start = (j // nb) + d * 128 * (j % nb)
                        dst = acc[:, start:start + 511 * d + 1:d]
                        if pi == 0:
                            s.op("dve", lambda e: e.tensor_copy(out=dst, in_=po[:]), reads=[pob], writes=[accall])
                        else:
                            s.op("dve", lambda e: e.tensor_tensor(out=dst, in0=po[:], in1=dst, op=ALU.add), reads=[pob, accall], writes=[accall])
                pending.append(pv)
                if len(pending) > 4:
                    pending.pop(0)()
            while pending:
                pending.pop(0)()
        for c in range(4):
            csl = slice(c * 2048, (c + 1) * 2048)
            rcp, rcpb = rcp_ring.next()
            s.op("dve", lambda e, rcp=rcp, csl=csl: e.reciprocal(out=rcp[64:128, :], in_=acc[64:128, csl]), reads=[accall], writes=[rcpb])
            rsh, rshb = rsh_ring.next()
            s.op("sp", lambda e, rsh=rsh, rcp=rcp: e.dma_start(out=rsh[:, :], in_=rcp[64:128, :]), reads=[rcpb], writes=[rshb], dma=True)
            og, ogb = ostg_ring.next()
            s.op("dve", lambda e, og=og, csl=csl, rsh=rsh: e.tensor_tensor(out=og[:], in0=acc[0:64, csl], in1=rsh[:, :], op=ALU.mult),
                 reads=[accall, rshb], writes=[ogb])
            s.op("sp", lambda e, og=og, c=c, hh=hh: e.dma_start(out=os_d[hh, :, c * 2048:(c + 1) * 2048], in_=og[:]), reads=[ogb], writes=[osb[hh][c]], dma=True)
        P["after_head"](hh, osb[0] + osb[1])
    cx.end_phase()


def phase_hproj(cx, P):
    s = cx.s
    hg, qk_s, v_s = P["hg"], P["qk_s"], P["v_s"]
    pb = list(zip(cx.psum_banks, cx.psum_bufs))
    ps_ring, pv_ring = Ring(pb[0:4]), Ring(pb[4:8])
    wt = cx.sb("wqkv", [128, 3, KC, 256], BF16)
    wb = cx.buf()
    for i, wd_ in enumerate((P["wq"], P["wk"], P["wv"])):
        s.op("pool", lambda e, i=i, wd_=wd_: e.dma_start(out=wt[:, i, :, :], in_=wd_[:, :, :]), writes=[wb], dma=True)
    hn_ring = Ring([(cx.sb("hn%d" % i, [128, KC, 512], BF16), cx.buf()) for i in range(4)])
    stg_ring = Ring([(cx.sb("stg%d" % i, [128, 512], BF16), cx.buf()) for i in range(4)])
    vst_items = []
    for i in range(2):
        t = cx.sb("vst%d" % i, [128, 4, 4, 128], BF16)
        b = cx.buf()
        s.op("dve", lambda e, t=t: e.memset(t[:, :, :, 64:128], 1.0), writes=[b])
        vst_items.append((t, b))
    vst_ring = Ring(vst_items)
    tiles = [(qd, r) for qd in range(4) for r in range(4)]
    loaded = {}

    def load(i):
        qd, r = tiles[i]
        hn, hnb = hn_ring.next()
        s.op("sp", lambda e: e.dma_start(out=hn[:], in_=hg[qd, r * 1024:(r + 1) * 1024, :].rearrange("(c p) t -> p c t", p=128)),
             writes=[hnb], dma=True, wait_cc=P["wait"](qd))
        loaded[i] = (hn, hnb)
    load(0)
    load(1)
    for ti, (qd, r) in enumerate(tiles):
        if True:
            if ti + 2 < len(tiles):
                load(ti + 2)
            hn, hnb = loaded.pop(ti)
            gsl = slice(r * T + qd * 512, r * T + (qd + 1) * 512)
            for which in range(2):
                for cp in range(2):
                    ps, psb = ps_ring.next()
                    for kc in range(KC):
                        s.op("pe", lambda e, ps=ps, which=which, cp=cp, kc=kc, hn=hn: e.matmul(
                            ps[:], lhsT=wt[:, which, kc, cp * 128:(cp + 1) * 128], rhs=hn[:, kc, :], start=(kc == 0), stop=(kc == KC - 1)),
                            reads=[wb, hnb], writes=[psb])
                    stg, stgb = stg_ring.next()
                    s.op("act", lambda e, stg=stg, ps=ps: e.activation(out=stg[:], in_=ps[:], func=AF.Copy), reads=[psb], writes=[stgb])
                    for hh in range(2):
                        s.op("sp", lambda e, stg=stg, hh=hh, cp=cp, which=which, gsl=gsl: e.dma_start(
                            out=qk_s[2 * cp + hh, which, :, gsl], in_=stg[hh * 64:(hh + 1) * 64, :]), reads=[stgb], dma=True)
            vst, vstb = vst_ring.next()
            for n4 in range(4):
                nsl = slice(n4 * 128, (n4 + 1) * 128)
                pv, pvb = pv_ring.next()
                for kc in range(KC):
                    s.op("pe", lambda e, pv=pv, kc=kc, hn=hn, nsl=nsl: e.matmul(pv[:, 0:256], lhsT=hn[:, kc, nsl], rhs=wt[:, 2, kc, :],
                                                                              start=(kc == 0), stop=(kc == KC - 1)),
                         reads=[wb, hnb], writes=[pvb])
                s.op("dve", lambda e, vst=vst, pv=pv, n4=n4: e.tensor_copy(out=vst[:, :, n4, 0:64], in_=pv[:, 0:256].rearrange("p (i c) -> p i c", c=64)),
                     reads=[pvb], writes=[vstb])
            n0 = r * 16 + qd * 4
            for it in range(4):
                s.op("sp", lambda e, vst=vst, it=it, n0=n0: e.dma_start(out=v_s[it, :, n0:n0 + 4, :], in_=vst[:, it, :, :]), reads=[vstb], dma=True)
    cx.end_phase()


def phase_moba(cx, P, nitems=4):
    s = cx.s
    os_d = P["os"]
    er_d, id_d, cb_d, past_d, ownb_d, dm_d = P["erows"], P["ident"], P["cb"], P["past"], P["ownb"], P["dm"]
    pb = list(zip(cx.psum_banks, cx.psum_bufs))
    st_ring, po_ring, pg_ring, pt_ring = Ring(pb[0:4]), Ring(pb[4:6]), Ring(pb[6:7]), Ring(pb[7:8])
    ident = cx.sb("ident", [128, 128], BF16)
    cbt = cx.sb("cbt", [128, 4, 512], F32)
    pastt = cx.sb("pastt", [128, 4, 512], F32)
    ownbt = cx.sb("ownbt", [128, 4, 512], F32)
    dmt = cx.sb("dmt", [128, 4, 512], BF16)
    cb = cx.buf()
    s.op("sp", lambda e: e.dma_start(out=ident[:], in_=id_d[:, :]), writes=[cb], dma=True)
    s.op("sp", lambda e: e.dma_start(out=cbt[:], in_=cb_d[:, :, :]), writes=[cb], dma=True)
    s.op("sp", lambda e: e.dma_start(out=pastt[:], in_=past_d[:, :, :]), writes=[cb], dma=True)
    s.op("sp", lambda e: e.dma_start(out=ownbt[:], in_=ownb_d[:, :, :]), writes=[cb], dma=True)
    s.op("sp", lambda e: e.dma_start(out=dmt[:], in_=dm_d[:, :, :]), writes=[cb], dma=True)
    items = []
    for i in range(2):
        t3 = cx.sb("t3_%d" % i, [96, 2, SEQ], BF16)
        qa, ka = t3[:, 0, :], t3[:, 1, :]
        va = cx.sb("va%d" % i, [128, 64, 128], BF16)
        tb, eb, vab, qbias = cx.buf(), cx.buf(), cx.buf(), cx.buf()
        s.op("sp", lambda e, ka=ka: e.dma_start(out=ka[64:96], in_=er_d[:, :]), writes=[eb], dma=True)
        items.append(((t3, qa, ka, va, None), (tb, eb, vab, qbias)))
    it_ring = Ring(items)
    kmf = cx.sb("kmf", [64, 32], F32)
    km = cx.sb("km", [64, 32], BF16)
    kmb = cx.buf()
    gm = cx.sb("gm", [128, 512], F32)
    sel = cx.sb("sel", [128, 512], F32)
    mx = cx.sb("mx", [128, 16, 8], F32)
    bq_ring = Ring([(cx.sb("bq%d" % i, [128, 512], BF16), cx.buf()) for i in range(2)])
    gmb = cx.buf()
    p_ring = Ring([(cx.sb("p%d" % i, [128, 512], BF16), cx.buf()) for i in range(6)])
    rec_ring = Ring([(cx.sb("rec%d" % i, [128, 512], F32), cx.buf()) for i in range(2)])
    ostg_ring = Ring([(cx.sb("ostg%d" % i, [64, 512], BF16), cx.buf()) for i in range(3)])
    osb = [[cx.buf() for _ in range(4)] for _ in range(nitems)]
    qk_s, v_s = P["qk_s"], P["v_s"]

    def issue_loads(it, item):
        (t3, qa, ka, va, _), (tb, eb, vab, qbias) = item
        s.op("sp", lambda e: e.dma_start(out=t3[0:64, :, :], in_=qk_s[it].rearrange("w p t -> p w t")), writes=[tb], dma=True)
        s.op("sp", lambda e: e.dma_start(out=va[:], in_=v_s[it]), writes=[vab], dma=True)
    cur = it_ring.next()
    issue_loads(0, cur)
    for it in range(nitems):
        (t3, qa, ka, va, vs), (tb, eb, vab, qbias) = cur
        qab = kab = vsb = tb
        s.op("dve", lambda e, ka=ka: e.tensor_reduce(out=kmf[:], in_=ka[0:64].rearrange("p (j k) -> p j k", k=256), axis=AX.X, op=ALU.add),
             reads=[kab], writes=[kmb])
        s.op("act", lambda e: e.activation(out=km[:], in_=kmf[:], func=AF.Copy, scale=1.0 / 256), reads=[kmb], writes=[kmb])
        for grp in range(4):
            pg, pgb = pg_ring.next()
            for i in range(16):
                qt = grp * 16 + i
                s.op("pe", lambda e, pg=pg, i=i, qt=qt, qa=qa: e.matmul(pg[:, i * 32:(i + 1) * 32], lhsT=qa[0:64, qt * 128:(qt + 1) * 128], rhs=km[:, :],
                                                                       start=True, stop=True), reads=[qab, kmb], writes=[pgb])
            s.op("dve", lambda e, pg=pg, grp=grp: e.tensor_tensor(out=gm[:], in0=pg[:], in1=cbt[:, grp, :], op=ALU.add), reads=[pgb, cb], writes=[gmb])
            for i in range(16):
                s.op("dve", lambda e, i=i: e.max(out=mx[:, i, :], in_=gm[:, i * 32:(i + 1) * 32]), reads=[gmb], writes=[gmb])
            s.op("dve", lambda e: e.tensor_tensor(out=sel[:].rearrange("p (a j) -> p a j", j=32), in0=gm[:].rearrange("p (a j) -> p a j", j=32),
                                                  in1=mx[:, :, 2:3].to_broadcast([128, 16, 32]), op=ALU.is_ge), reads=[gmb], writes=[gmb])
            s.op("dve", lambda e, grp=grp: e.tensor_tensor(out=sel[:], in0=sel[:], in1=pastt[:, grp, :], op=ALU.mult), reads=[gmb, cb], writes=[gmb])
            bq, bqb = bq_ring.next()
            s.op("dve", lambda e, grp=grp, bq=bq: e.scalar_tensor_tensor(out=bq[:], in0=sel[:], scalar=BIG, in1=ownbt[:, grp, :], op0=ALU.mult, op1=ALU.add),
                 reads=[gmb, cb], writes=[bqb])
            for half in range(2):
                pt, ptb = pt_ring.next()
                ptv = pt[:].bitcast(BF16)
                for i8 in range(8):
                    i = half * 8 + i8
                    s.op("pe", lambda e, ptv=ptv, i8=i8, i=i, bq=bq: e.transpose(out=ptv[0:32, i8 * 128:(i8 + 1) * 128], in_=bq[:, i * 32:(i + 1) * 32],
                                                                              identity=ident[:]), reads=[bqb, cb], writes=[ptb])
                c0 = (grp * 16 + half * 8) * 128
                s.op("act", lambda e, ptv=ptv, c0=c0, qa=qa: e.activation(out=qa[64:96, c0:c0 + 1024], in_=ptv[0:32, :], func=AF.Copy),
                     reads=[ptb], writes=[qbias])
        pending = []
        nxt = None
        for tq in range(16):
            if tq == 13 and it + 1 < nitems:
                nxt = it_ring.next()
                issue_loads(it + 1, nxt)
            po, pob = po_ring.next()
            nk = 4 * tq + 4
            for kt in range(nk):
                ps, psb = st_ring.next()
                s.op("pe", lambda e, ps=ps, kt=kt, tq=tq, ka=ka, qa=qa: e.matmul(ps[:], lhsT=ka[:, kt * 128:(kt + 1) * 128], rhs=qa[:, tq * 512:(tq + 1) * 512],
                                                                              start=True, stop=True), reads=[kab, eb, qab, qbias], writes=[psb])
                p, pbuf = p_ring.next()
                s.op("act", lambda e, p=p, ps=ps: e.activation(out=p[:], in_=ps[:], func=AF.Exp, scale=0.125), reads=[psb], writes=[pbuf])
                if kt >= 4 * tq:
                    di = kt - 4 * tq
                    s.op("dve", lambda e, p=p, di=di: e.tensor_tensor(out=p[:], in0=p[:], in1=dmt[:, di, :], op=ALU.mult), reads=[pbuf, cb], writes=[pbuf])

                def pv(po=po, pob=pob, kt=kt, p=p, pbuf=pbuf, nk=nk, tq=tq, va=va, vab=vab, it=it):
                    s.op("pe", lambda e: e.matmul(po[:], lhsT=va[:, kt, :], rhs=p[:], start=(kt == 0), stop=(kt == nk - 1)),
                         reads=[vab, pbuf], writes=[pob])
                    if kt == nk - 1:
                        rec, recb = rec_ring.next()
                        s.op("dve", lambda e: e.reciprocal(out=rec[64:128, :], in_=po[64:128, :]), reads=[pob], writes=[recb])
                        og, ogb = ostg_ring.next()
                        s.op("dve", lambda e: e.tensor_tensor(out=og[:], in0=po[0:64, :], in1=rec[64:128, :], op=ALU.mult),
                             reads=[pob, recb], writes=[ogb])
                        s.op("sp", lambda e: e.dma_start(out=os_d[it, :, tq * 512:(tq + 1) * 512], in_=og[:]),
                             reads=[ogb], writes=[osb[it][tq // 4]], dma=True)
                pending.append(pv)
                if len(pending) > LAG:
                    pending.pop(0)()
        while pending:
            pending.pop(0)()
        if it + 1 < nitems:
            cur = nxt
        P["after_item"](it, osb[it])
    cx.end_phase()


RG = [[0, 1, 2, 3], [4, 5, 6, 7]]


def build_fused():
    cx = Ctx()
    s = cx.s
    I32 = mybir.dt.int32
    di = cx.dram_in
    xT_d = di("xT", [128, KC, T])
    rk_d = cx.nc.dram_tensor("rk", [1, 4], I32, kind="ExternalInput").ap()
    out_d = cx.dram_out("out_fm", [128, KC, T])
    A = {"gains": di("A_gains", [128, 2, KC]), "ffn": [(di("A_fwgu0", [NFC, 128, 2048]), di("A_fwd0", [NFC, 128, D]))],
         "w_proj": di("A_w_proj", [8, 128, 2048]), "w_vb": di("A_w_vb", [128, KC, 512]), "vg_rep": di("A_vg_rep", [128, 512]),
         "wsT": di("A_wsT", [128, 4, 128]), "trilT": di("A_trilT", [128, 4, 128]), "bs_rep": di("A_bs_rep", [128, 4, 128])}
    Bc = {"masks": di("B_masks", [2, 128, 512], BF16), "ident": di("ident", [128, 128], BF16)}
    C = {"gains": di("C_gains", [128, 3, KC]), "w_out": di("C_w_out", [4, 128, 2048]),
         "ffn": [(di("C_fwgu0", [NFC, 128, 2048]), di("C_fwd0", [NFC, 128, D])), (di("C_fwgu1", [NFC, 128, 2048]), di("C_fwd1", [NFC, 128, D]))],
         "wq": di("D_wq", [128, KC, 256]), "wk": di("D_wk", [128, KC, 256]), "wv": di("D_wv", [128, KC, 256])}
    Dc = {"erows": di("D_erows", [32, SEQ], BF16), "cb": di("D_cb", [128, 4, 512]), "past": di("D_past", [128, 4, 512]),
          "ownb": di("D_ownb", [128, 4, 512]), "dm": di("D_dm", [128, 4, 512], BF16)}
    E = {"gains": di("E_gains", [128, 2, KC]), "w_out": di("E_w_out", [4, 128, 2048]),
         "ffn": [(di("E_fwgu0", [NFC, 128, 2048]), di("E_fwd0", [NFC, 128, D]))]}
    h_s = cx.dram("h_s", [128, KC, T])
    bo_s = cx.dram("bo_s", [128, 4, T], BF16)
    zsA = cx.dram("zsA", [6, 256, T], BF16)
    zgA = cx.dram("zgA", [6, 1024, T], BF16)
    osA = cx.dram("osA", [2, 64, SEQ], BF16)
    ogA = cx.dram("ogA", [2, 256, SEQ], BF16)
    hs_d = cx.dram("hs_d", [4, 1024, 512], BF16)
    hg_d = cx.dram("hg_d", [4, 4096, 512], BF16)
    qk_s = cx.dram("qk_s", [4, 2, 64, SEQ], BF16)
    v_s = cx.dram("v_s", [4, 128, 64, 128], BF16)
    osC = cx.dram("osC", [4, 64, SEQ], BF16)
    ogC = cx.dram("ogC", [4, 256, SEQ], BF16)
    cx.alloc_psum()

    cx.rk = {}
    cx.rk64 = {}
    cx.rk2048 = {}

    def mk_init(name):
        def init(eng):
            reg = eng.alloc_register("rkreg_" + name)
            eng.reg_load(reg, rk_d[0:1, 0:1])
            cx.rk[name] = eng.snap(reg, min_val=0, max_val=3)
            reg2 = eng.alloc_register("rkreg64_" + name)
            eng.reg_load(reg2, rk_d[0:1, 1:2])
            cx.rk64[name] = eng.snap(reg2, min_val=0, max_val=192)
            if name == "sp":
                reg3 = eng.alloc_register("rkreg2k_" + name)
                eng.reg_load(reg3, rk_d[0:1, 2:3])
                cx.rk2048[name] = eng.snap(reg3, min_val=0, max_val=3 * T)
        return init
    s.sp_init = {"sp": mk_init("sp")}
    gdone = cx.buf()

    cct = {}

    def gather(key, src, dst, rbufs):
        s.op("pool", lambda e: e.collective_compute("AllGather", ALU.bypass, replica_groups=RG, ins=[src], outs=[dst]),
             reads=rbufs, cc=True)
        cct[key] = s.cccount

    def make_sink(name, zs, zg, nunits):
        bufs = [cx.buf() for _ in range(nunits)]
        todo = []

        def sink(ch, tt, tsl, stg, stgb):
            u, hf = ch // 2, ch % 2
            s.op("sp", lambda e: e.dma_start(out=zs[u, hf * 128:(hf + 1) * 128, tsl], in_=stg[:]), reads=[stgb], writes=[bufs[u]], dma=True)
            if hf == 1 and tt == NTT - 1:
                todo.append(u)

        def flush():
            for u in todo:
                gather((name, u), zs[u], zg[u], [bufs[u]])
        return sink, flush

    sinkA, flushA = make_sink("A", zsA, zgA, 6)
    phase_tok(cx, {"h_in": xT_d, "h_out": h_s, "gains": A["gains"], "ffn_w": A["ffn"], "w_proj": A["w_proj"], "proj_units": 8,
                   "z_sink": sinkA, "flush": flushA,
                   "gmlp": {"w_vb": A["w_vb"], "vg_rep": A["vg_rep"], "wsT": A["wsT"], "trilT": A["trilT"], "bs_rep": A["bs_rep"], "bo_dst": bo_s}})

    def after_head_B(hh, osb):
        gather(("oA", hh), osA[hh], ogA[hh], list(osb[4 * hh:4 * hh + 4]))
    phase_dil(cx, {"zg": zgA, "os": osA, "masks": Bc["masks"], "ident": Bc["ident"], "after_head": after_head_B,
                   "wait": lambda hh: cct[("A", 3 * hh + 2)]})

    def o_load_C(xn, xb):
        for hh in range(2):
            s.op("sp", lambda e, hh=hh: e.dma_start(out=xn[hh * 64:hh * 64 + 64, 0:4, :], in_=ogA[hh].rearrange("(r x) t -> x r t", r=4)[:, :, bass.ds(cx.rk2048["sp"], T)]),
                 writes=list(xb), dma=True, wait_cc=cct[("oA", hh)])
        for tt in range(NTT):
            tsl = slice(tt * 512, (tt + 1) * 512)
            s.op("sp", lambda e, tsl=tsl: e.dma_start(out=xn[:, 4:8, tsl], in_=bo_s[:, :, tsl]), writes=[xb[tt]], dma=True)
    def hn_out_C(xn, xb):
        for tt in range(NTT):
            b = cx.buf()
            tsl = slice(tt * 512, (tt + 1) * 512)
            s.op("sp", lambda e, tt=tt, tsl=tsl: e.dma_start(out=hs_d[tt].rearrange("(c p) t -> p c t", p=128), in_=xn[:, :, tsl]),
                 reads=[xb[tt]], writes=[b], dma=True)
            gather(("H", tt), hs_d[tt], hg_d[tt], [b])
    phase_tok(cx, {"h_in": h_s, "h_out": h_s, "gains": C["gains"], "o_load": o_load_C, "w_out": C["w_out"], "ffn_w": C["ffn"],
                   "hn_out": hn_out_C})

    phase_hproj(cx, {"hg": hg_d, "qk_s": qk_s, "v_s": v_s, "wq": C["wq"], "wk": C["wk"], "wv": C["wv"], "wait": lambda qd: cct[("H", qd)]})

    def after_item_D(it, osb):
        gather(("oC", it), osC[it], ogC[it], list(osb))
    phase_moba(cx, {"qk_s": qk_s, "v_s": v_s, "os": osC, "erows": Dc["erows"], "ident": Bc["ident"], "cb": Dc["cb"], "past": Dc["past"],
                    "ownb": Dc["ownb"], "dm": Dc["dm"], "after_item": after_item_D})

    def o_load_E(xn, xb):
        for it in range(4):
            ps = slice((it % 2) * 64, (it % 2) * 64 + 64)
            s.op("sp", lambda e, it=it, ps=ps: e.dma_start(out=xn[ps, (it // 2):8:2, :], in_=ogC[it].rearrange("(r x) t -> x r t", r=4)[:, :, bass.ds(cx.rk2048["sp"], T)]),
                 writes=list(xb), dma=True, wait_cc=cct[("oC", it)])
    phase_tok(cx, {"h_in": h_s, "h_out": out_d, "gains": E["gains"], "o_load": o_load_E, "w_out": E["w_out"], "ffn_w": E["ffn"], "final": True})
    cx.stack.close()
    return cx.nc


BF = ml_dtypes.bfloat16
_PROGS = {}


def _gains(*gs):
    return np.ascontiguousarray(np.stack([gain_fm(np.asarray(g, np.float32)) for g in gs], axis=1))


def _head_perm(w, nheads, per_core):
    cols = []
    for i in range(per_core):
        for t in range(3):
            for c in range(4):
                h = per_core * c + i
                cols.append(w[:, t * nheads * 64 + h * 64: t * nheads * 64 + (h + 1) * 64])
    return np.concatenate(cols, axis=1)


def _dil_consts():
    k = np.arange(128)[:, None]
    q = np.arange(128)[None, :]
    prev = (k >= q).astype(np.float32)
    own = (k <= q).astype(np.float32)
    z = np.zeros_like(prev)
    m = np.stack([np.concatenate([prev, own, prev, own], 1), np.concatenate([z, own, prev, own], 1)], 0)
    return m.astype(BF)


def _moba_consts():
    er = (np.arange(SEQ)[None, :] // 256 == np.arange(32)[:, None]).astype(np.float32).astype(BF)
    ident = np.eye(128, dtype=np.float32).astype(BF)
    cb = np.zeros((128, 4, 16, 32), np.float32)
    past = np.zeros((128, 4, 16, 32), np.float32)
    ownb = np.zeros((128, 4, 16, 32), np.float32)
    j = np.arange(32)
    for grp in range(4):
        for i in range(16):
            own = (grp * 16 + i) // 2
            cb[:, grp, i, :] = np.where(j < own, 0.0, -2 * BIG)
            past[:, grp, i, :] = (j < own)
            ownb[:, grp, i, :] = np.where(j == own, 0.0, -BIG)
    dm = np.ones((128, 4, 512), np.float32)
    kk = np.arange(128)[:, None]
    qq = np.arange(512)[None, :]
    for di in range(4):
        kpos = 128 * di + kk
        same = (kpos // 256) == (qq // 256)
        dm[:, di, :] = np.where(same & (qq < kpos), 0.0, 1.0)
    return {"D_erows": er, "ident": ident, "D_cb": cb.reshape(128, 4, 512), "D_past": past.reshape(128, 4, 512),
            "D_ownb": ownb.reshape(128, 4, 512), "D_dm": dm.astype(BF)}


def kernel(x, ffn1_norm, ffn1_w_gate, ffn1_w_up, ffn1_w_down, mix_norm,
           ffn2_norm, ffn2_w_gate, ffn2_w_up, ffn2_w_down,
           ab_w_in, ab_v_norm, ab_w_spatial, ab_b_spatial, ab_w_out,
           c_w_in, c_w_out, final_norm):
    f = lambda a: np.asarray(a, dtype=np.float32)
    xf = f(x).reshape(-1, D)
    w_in = f(ab_w_in)[0]
    common = {
        "A_gains": _gains(f(ffn1_norm)[0], f(mix_norm)[0]),
        "A_fwgu0": pack_wgu(f(ffn1_w_gate)[0], f(ffn1_w_up)[0]), "A_fwd0": pack_wd(f(ffn1_w_down)[0]),
        "A_w_proj": pack_pairs(np.concatenate([_head_perm(w_in[:, :1536], 8, 2), w_in[:, 1536:2048]], axis=1)),
        "A_w_vb": np.ascontiguousarray(w_in[:, 2048:2560].reshape(KC, 128, 512).transpose(1, 0, 2)),
        "A_vg_rep": np.ascontiguousarray(np.broadcast_to(f(ab_v_norm)[0].reshape(1, 512), (128, 512))),
        "A_wsT": np.ascontiguousarray(f(ab_w_spatial)[0].transpose(2, 0, 1)),
        "A_trilT": np.ascontiguousarray(np.broadcast_to(np.triu(np.ones((128, 128), np.float32))[:, None, :], (128, 4, 128))),
        "A_bs_rep": np.ascontiguousarray(np.broadcast_to(f(ab_b_spatial)[0][None], (128, 4, 128))),
        "B_masks": _dil_consts(),
        "C_gains": _gains(f(ffn2_norm)[0], f(ffn1_norm)[1], f(mix_norm)[1]),
        "C_w_out": pack_pairs(f(ab_w_out)[0]),
        "C_fwgu0": pack_wgu(f(ffn2_w_gate)[0], f(ffn2_w_up)[0]), "C_fwd0": pack_wd(f(ffn2_w_down)[0]),
        "C_fwgu1": pack_wgu(f(ffn1_w_gate)[1], f(ffn1_w_up)[1]), "C_fwd1": pack_wd(f(ffn1_w_down)[1]),
        "E_gains": _gains(f(ffn2_norm)[1], f(final_norm)),
        "E_w_out": pack_pairs(f(c_w_out)[0]),
        "E_fwgu0": pack_wgu(f(ffn2_w_gate)[1], f(ffn2_w_up)[1]), "E_fwd0": pack_wd(f(ffn2_w_down)[1]),
    }
    common.update(_moba_consts())
    cw = f(c_w_in)[0]

    def wslice(t, j):
        w = cw[:, t * 1024 + 256 * j: t * 1024 + 256 * (j + 1)]
        return np.ascontiguousarray(w.reshape(KC, 128, 256).transpose(1, 0, 2))
    ins = [dict(common, xT=to_fm(xf[c * T:(c + 1) * T]), rk=np.array([[c % 4, (c % 4) * 64, (c % 4) * T, 0]], np.int32),
                D_wq=wslice(0, c % 4), D_wk=wslice(1, c % 4), D_wv=wslice(2, c % 4)) for c in range(NCORES)]
    if "F" not in _PROGS:
        _PROGS["F"] = build_fused()
    res = run_bass_kernel_spmd(_PROGS["F"], ins, core_ids=list(range(NCORES))).results
    out = np.concatenate([from_fm(np.asarray(res[c]["out_fm"])) for c in range(NCORES)], axis=0)
    return out.reshape(2, SEQ, D).astype(np.float32)
```

```python
import contextlib
import numpy as np
import ml_dtypes
import concourse.bass as bass
import concourse.mybir as mybir
from concourse.bass_utils import run_bass_kernel_spmd

F32 = mybir.dt.float32
BF16 = mybir.dt.bfloat16
ALU = mybir.AluOpType
AF = mybir.ActivationFunctionType
AX = mybir.AxisListType

NCORES = 8
D = 1024
KC = 8
DFF = 2816
NFC = 22
T = 2048
NTT = 4
SEQ = 8192
EPS = 1e-6
FC_GROUPS = [(0, 6), (6, 12), (12, 18), (18, 22)]


class Buf:
    __slots__ = ("name", "lw", "rd")

    def __init__(self, name):
        self.name = name
        self.lw = None
        self.rd = []


class _Op:
    __slots__ = ("eng", "fn", "deps", "dma", "signal", "cnt", "dsem", "dtarget", "dprev", "cc", "wait_cc")

    def __init__(self, eng, fn, deps, dma):
        self.eng = eng
        self.fn = fn
        self.deps = deps
        self.dma = dma
        self.signal = False
        self.cnt = 0
        self.dsem = None
        self.dtarget = 0
        self.dprev = 0
        self.cc = False
        self.wait_cc = 0


ENGINES = ("pe", "act", "dve", "pool", "sp")
NDSEM = 10


class Sched:
    def __init__(self, nc, stack):
        self.nc = nc
        self.stack = stack
        self.ops = []
        self.bufs = set()
        self.csem = None
        self.ccount = {e: 0 for e in ENGINES}
        self.dcount = {e: 0 for e in ("sp", "pool", "act")}
        self.cccount = 0
        self.sp_init = None

    def op(self, eng, fn, reads=(), writes=(), dma=False, cc=False, wait_cc=0):
        idx = len(self.ops)
        deps = set()
        for b in reads:
            if b.lw is not None:
                deps.add(b.lw)
        for b in writes:
            if b.lw is not None:
                deps.add(b.lw)
            deps.update(b.rd)
        deps.discard(idx)
        if eng == "pe" and not dma:
            deps = {d for d in deps if not (self.ops[d].eng == "pe" and not self.ops[d].dma)}
        o = _Op(eng, fn, sorted(deps), dma or cc)
        o.cc = cc
        o.wait_cc = wait_cc
        if cc:
            self.cccount += 1
            o.dtarget = self.cccount
        self.ops.append(o)
        for b in reads:
            b.rd.append(idx)
            self.bufs.add(b)
        for b in writes:
            b.lw = idx
            b.rd = []
            self.bufs.add(b)
        return idx

    def _init_sems(self):
        nc = self.nc
        self.csem = {e: self.stack.enter_context(nc.semaphore("cs_" + e)) for e in ENGINES}
        self.dsems = {e: [self.stack.enter_context(nc.semaphore("ds_%s_%d" % (e, i))) for i in range(NDSEM)]
                      for e in ("sp", "pool", "act")}
        self.ccsem = self.stack.enter_context(nc.semaphore("ccsem"))

    def emit_phase(self, final=False):
        nc = self.nc
        ops = self.ops
        if self.csem is None:
            self._init_sems()
        csem, dsems, ccsem = self.csem, self.dsems, self.ccsem
        for o in ops:
            for d in o.deps:
                if not ops[d].dma:
                    ops[d].signal = True
        last = {}
        for i, o in enumerate(ops):
            if not o.dma:
                last[o.eng] = i
        for i in last.values():
            ops[i].signal = True
        for o in ops:
            if o.cc:
                o.dsem = ccsem
                o.dprev = 0
            elif o.dma:
                n = self.dcount[o.eng]
                self.dcount[o.eng] += 1
                o.dsem = dsems[o.eng][n % NDSEM]
                o.dprev = 16 * (n // NDSEM)
                o.dtarget = o.dprev + 16
            elif o.signal:
                self.ccount[o.eng] += 1
                o.cnt = self.ccount[o.eng]
        final_d = []
        for e in dsems:
            n = self.dcount[e]
            for i in range(NDSEM):
                k = (n - i + NDSEM - 1) // NDSEM if n > i else 0
                if k > 0:
                    final_d.append((dsems[e][i], 16 * k))
        final_c = [(csem[e], self.ccount[e]) for e in ENGINES if self.ccount[e] > 0]
        if self.cccount and final:
            final_d.append((ccsem, self.cccount))

        def run_engine(ename, eng):
            if self.sp_init is not None and ename in self.sp_init:
                self.sp_init.pop(ename)(eng)
            waited = {}
            for o in ops:
                if o.eng != ename:
                    continue
                for d in o.deps:
                    od = ops[d]
                    if od.dma:
                        key, val = ("d", id(od.dsem)), od.dtarget
                        sem = od.dsem
                    else:
                        key, val = ("c", od.eng), od.cnt
                        sem = csem[od.eng]
                    if waited.get(key, 0) < val:
                        eng.wait_ge(sem, val)
                        waited[key] = val
                if o.wait_cc and waited.get("ccw", 0) < o.wait_cc:
                    eng.wait_ge(ccsem, o.wait_cc)
                    waited["ccw"] = o.wait_cc
                if o.cc:
                    o.fn(eng).then_inc(o.dsem)
                elif o.dma:
                    if o.dprev > 0:
                        key = ("d", id(o.dsem))
                        if waited.get(key, 0) < o.dprev:
                            eng.wait_ge(o.dsem, o.dprev)
                            waited[key] = o.dprev
                    o.fn(eng).then_inc(o.dsem, 16)
                else:
                    ins = o.fn(eng)
                    if o.signal:
                        ins.then_inc(csem[ename], 1)
            for sem, val in final_c + final_d:
                eng.wait_ge(sem, val)

        with nc.Block() as block:
            @block.sync
            def _(e):
                run_engine("sp", e)

            @block.tensor
            def _(e):
                run_engine("pe", e)

            @block.scalar
            def _(e):
                run_engine("act", e)

            @block.vector
            def _(e):
                run_engine("dve", e)

            @block.gpsimd
            def _(e):
                run_engine("pool", e)
        for b in self.bufs:
            b.lw = None
            b.rd = []
        self.bufs = set()
        self.ops = []


class Ctx:
    def __init__(self):
        self.nc = bass.Bass("TRN2", target_bir_lowering=False)
        self.stack = contextlib.ExitStack()
        self.pstack = contextlib.ExitStack()
        self.s = Sched(self.nc, self.stack)
        self._n = 0
        self.psum_banks = []
        self.psum_bufs = []
        self.rank = None

    def dram_in(self, name, shape, dt=F32):
        return self.nc.dram_tensor(name, list(shape), dt, kind="ExternalInput").ap()

    def dram_out(self, name, shape, dt=F32):
        return self.nc.dram_tensor(name, list(shape), dt, kind="ExternalOutput").ap()

    def dram(self, name, shape, dt=F32):
        return self.nc.dram_tensor(name, list(shape), dt).ap()

    def sb(self, name, shape, dt):
        self._n += 1
        return self.pstack.enter_context(self.nc.sbuf_tensor("s%d_%s" % (self._n, name), list(shape), dt))

    def alloc_psum(self):
        for i in range(8):
            self.psum_banks.append(self.stack.enter_context(self.nc.psum_tensor("psb%d" % i, [128, 512], F32)))
            self.psum_bufs.append(Buf("psb%d" % i))

    def buf(self, name=None):
        self._n += 1
        return Buf(name or ("b%d" % self._n))

    def end_phase(self, final=False):
        self.s.emit_phase(final)
        self.pstack.close()
        self.pstack = contextlib.ExitStack()

    def finish(self):
        self.end_phase()
        self.stack.close()
        return self.nc


class Ring:
    def __init__(self, items):
        self.items = items
        self.i = 0

    def next(self):
        it = self.items[self.i % len(self.items)]
        self.i += 1
        return it


def emit_norm(cx, hT, hbufs, gcol, gb, xn, xnbufs, ones_bf, ones_b, sq_ring, ps_ring, st_ring):
    s = cx.s
    for tt in range(NTT):
        tsl = slice(tt * 512, (tt + 1) * 512)
        ps, psb = ps_ring.next()
        for c in range(KC):
            sq, sqb = sq_ring.next()
            s.op("act", lambda e, sq=sq, c=c, tsl=tsl: e.activation(out=sq[:], in_=hT[:, c, tsl], func=AF.Square),
                 reads=[hbufs[tt]], writes=[sqb])
            s.op("pe", lambda e, ps=ps, sq=sq, c=c: e.matmul(ps[:], lhsT=ones_bf[:], rhs=sq[:], start=(c == 0), stop=(c == KC - 1)),
                 reads=[sqb, ones_b], writes=[psb])
        (sd, rs), stb = st_ring.next()
        s.op("act", lambda e, ps=ps, sd=sd: e.activation(out=sd[:], in_=ps[:], func=AF.Sqrt, bias=EPS, scale=1.0 / D),
             reads=[psb], writes=[stb])
        s.op("dve", lambda e, sd=sd, rs=rs: e.reciprocal(out=rs[:], in_=sd[:]), reads=[stb], writes=[stb])
        for c in range(KC):
            s.op("dve", lambda e, c=c, tsl=tsl, rs=rs: e.scalar_tensor_tensor(
                out=xn[:, c, tsl], in0=hT[:, c, tsl], scalar=gcol[:, c:c + 1], in1=rs[:], op0=ALU.mult, op1=ALU.mult),
                reads=[hbufs[tt], stb, gb], writes=[xnbufs[tt]])


def emit_ffn(cx, hT, hbufs, xn, xnbufs, wgu_dram, wd_dram, res):
    s = cx.s
    wgu_ring, wd_ring, act_ring, sg_ring = res["wgu_ring"], res["wd_ring"], res["act_ring"], res["sg_ring"]
    psg_ring, psu_ring, psd_ring = res["psg_ring"], res["psu_ring"], res["psd_ring"]
    for (f0, f1) in FC_GROUPS:
        nf = f1 - f0
        act, actb = act_ring.next()
        wds = []
        for fi in range(nf):
            fc = f0 + fi
            wgu, wgub = wgu_ring.next()
            wd, wdb = wd_ring.next()
            wds.append((wd, wdb))
            s.op("pool", lambda e, wgu=wgu, fc=fc: e.dma_start(out=wgu[:], in_=wgu_dram[fc]), writes=[wgub], dma=True)
            s.op("pool", lambda e, wd=wd, fc=fc: e.dma_start(out=wd[:], in_=wd_dram[fc]), writes=[wdb], dma=True)
            for tt in range(NTT):
                tsl = slice(tt * 512, (tt + 1) * 512)
                pg, pgb = psg_ring.next()
                pu, pub = psu_ring.next()
                for kc in range(KC):
                    s.op("pe", lambda e, pg=pg, wgu=wgu, kc=kc, tsl=tsl: e.matmul(
                        pg[:], lhsT=wgu[:, kc * 128:(kc + 1) * 128], rhs=xn[:, kc, tsl], start=(kc == 0), stop=(kc == KC - 1)),
                        reads=[wgub, xnbufs[tt]], writes=[pgb])
                for kc in range(KC):
                    s.op("pe", lambda e, pu=pu, wgu=wgu, kc=kc, tsl=tsl: e.matmul(
                        pu[:], lhsT=wgu[:, 1024 + kc * 128:1024 + (kc + 1) * 128], rhs=xn[:, kc, tsl], start=(kc == 0), stop=(kc == KC - 1)),
                        reads=[wgub, xnbufs[tt]], writes=[pub])
                sg, sgb = sg_ring.next()
                s.op("act", lambda e, sg=sg, pg=pg: e.activation(out=sg[:], in_=pg[:], func=AF.Silu), reads=[pgb], writes=[sgb])
                s.op("dve", lambda e, sg=sg, pu=pu, act=act, fi=fi, tsl=tsl: e.tensor_tensor(
                    out=act[:, fi, tsl], in0=pu[:], in1=sg[:], op=ALU.mult), reads=[pub, sgb], writes=[actb])
        for dc in range(KC):
            for tt in range(NTT):
                tsl = slice(tt * 512, (tt + 1) * 512)
                pd, pdb = psd_ring.next()
                for fi in range(nf):
                    wd, wdb = wds[fi]
                    s.op("pe", lambda e, pd=pd, wd=wd, fi=fi, dc=dc, tsl=tsl, act=act, nf=nf: e.matmul(
                        pd[:], lhsT=wd[:, dc * 128:(dc + 1) * 128], rhs=act[:, fi, tsl], start=(fi == 0), stop=(fi == nf - 1)),
                        reads=[wdb, actb], writes=[pdb])
                s.op("dve", lambda e, pd=pd, dc=dc, tsl=tsl: e.scalar_tensor_tensor(
                    out=hT[:, dc, tsl], in0=pd[:], scalar=0.5, in1=hT[:, dc, tsl], op0=ALU.mult, op1=ALU.add),
                    reads=[pdb, hbufs[tt]], writes=[hbufs[tt]])


def alloc_ffn_resources(cx):
    res = {}
    mk = lambda name, shape, dt, n: Ring([(cx.sb("%s%d" % (name, i), shape, dt), cx.buf()) for i in range(n)])
    res["wgu_ring"] = mk("wgu", [128, 2048], BF16, 3)
    res["wd_ring"] = mk("wd", [128, 1024], BF16, 8)
    res["act_ring"] = mk("actb", [128, 6, 2048], BF16, 1)
    res["sg_ring"] = mk("sg", [128, 512], F32, 2)
    pb = list(zip(cx.psum_banks, cx.psum_bufs))
    res["psg_ring"] = Ring(pb[0:2])
    res["psu_ring"] = Ring(pb[2:4])
    res["psd_ring"] = Ring(pb[4:6])
    res["psm_ring"] = Ring(pb[6:8])
    res["sq_ring"] = mk("sq", [128, 512], BF16, 3)
    res["st_ring"] = Ring([((cx.sb("sd%d" % i, [128, 512], F32), cx.sb("rs%d" % i, [128, 512], F32)), cx.buf()) for i in range(1)])
    return res


def to_fm(x2d):
    t = x2d.shape[0]
    return np.ascontiguousarray(x2d.reshape(t, KC, 128).transpose(2, 1, 0))


def from_fm(a):
    t = a.shape[2]
    return np.ascontiguousarray(a.transpose(2, 1, 0).reshape(t, KC * 128))


def gain_fm(g):
    return np.ascontiguousarray(g.reshape(KC, 128).T)


def pack_wgu(wg, wu):
    def one(w):
        return w.reshape(KC, 128, NFC, 128).transpose(2, 1, 0, 3).reshape(NFC, 128, KC * 128)
    return np.ascontiguousarray(np.concatenate([one(wg), one(wu)], axis=2))


def pack_wd(wd):
    return np.ascontiguousarray(wd.reshape(NFC, 128, D))


def emit_proj_fm(cx, xn, xnbufs, w_dram, nunits, res, sink, kin=KC):
    s = cx.s
    wgu_ring, ring = res["wgu_ring"], res["psg_ring"]
    for i in range(nunits):
        wgu, wgub = wgu_ring.next()
        s.op("pool", lambda e, wgu=wgu, i=i: e.dma_start(out=wgu[:, 0:2 * kin * 128], in_=w_dram[i]), writes=[wgub], dma=True)
        for hf in range(2):
            for tt in range(NTT):
                tsl = slice(tt * 512, (tt + 1) * 512)
                ps, psb = ring.next()
                for kc in range(kin):
                    s.op("pe", lambda e, ps=ps, wgu=wgu, kc=kc, hf=hf, tsl=tsl: e.matmul(
                        ps[:], lhsT=wgu[:, (hf * kin + kc) * 128:(hf * kin + kc + 1) * 128], rhs=xn[:, kc, tsl],
                        start=(kc == 0), stop=(kc == kin - 1)), reads=[wgub, xnbufs[tt]], writes=[psb])
                sink(i, hf, tt, ps, psb)


def pack_pairs(w):
    k, n = w.shape
    kin = k // 128
    nu = n // 256
    a = w.reshape(kin, 128, nu, 2, 128).transpose(2, 1, 3, 0, 4)
    return np.ascontiguousarray(a.reshape(nu, 128, 2 * kin * 128))


GELU_C = 0.044715
GELU_S = 1.5957691216057308


def emit_gelu(cx, src, srcb, dst, dstb, tmp_ring, reads_extra=()):
    s = cx.s
    (t1, t2), tb = tmp_ring.next()
    s.op("act", lambda e: e.activation(out=t1[:], in_=src, func=AF.Square), reads=[srcb], writes=[tb])
    s.op("dve", lambda e: e.tensor_scalar(out=t1[:], in0=t1[:], scalar1=GELU_C, scalar2=1.0, op0=ALU.mult, op1=ALU.add),
         reads=[tb], writes=[tb])
    s.op("dve", lambda e: e.tensor_tensor(out=t1[:], in0=t1[:], in1=src, op=ALU.mult), reads=[tb, srcb], writes=[tb])
    s.op("act", lambda e: e.activation(out=t2[:], in_=t1[:], func=AF.Sigmoid, scale=GELU_S), reads=[tb], writes=[tb])
    s.op("dve", lambda e: e.tensor_tensor(out=dst, in0=t2[:], in1=src, op=ALU.mult), reads=[tb, srcb], writes=[dstb])


def phase_tok(cx, P):
    s = cx.s
    ffn_w = P.get("ffn_w", [])
    proj_units = P.get("proj_units", 0)
    gm = P.get("gmlp")
    final = P.get("final", False)
    hn_out = P.get("hn_out")
    ng = len(ffn_w) + (1 if (proj_units or final or hn_out) else 0)
    hT = cx.sb("hT", [128, KC, T], F32)
    xn = cx.sb("xn", [128, KC, T], BF16)
    gcol = cx.sb("gcol", [128, ng, KC], F32)
    ones = cx.sb("ones", [128, 128], BF16)
    hb = [cx.buf() for _ in range(NTT)]
    xb = [cx.buf() for _ in range(NTT)]
    gb, ob = cx.buf(), cx.buf()
    res = alloc_ffn_resources(cx)
    stg_ring = Ring([(cx.sb("stg%d" % i, [128, 512], BF16), cx.buf()) for i in range(4)])
    h_d, g_d, hout_d = P["h_in"], P["gains"], P["h_out"]
    for tt in range(NTT):
        s.op("sp", lambda e, tt=tt: e.dma_start(out=hT[:, :, tt * 512:(tt + 1) * 512], in_=h_d[:, :, tt * 512:(tt + 1) * 512]),
             writes=[hb[tt]], dma=True)
    s.op("sp", lambda e: e.dma_start(out=gcol[:], in_=g_d[:, :, :]), writes=[gb], dma=True)
    s.op("dve", lambda e: e.memset(ones[:], 1.0), writes=[ob])
    gi = 0
    if P.get("o_load") is not None:
        P["o_load"](xn, xb)
        wo_d = P["w_out"]

        def sink_mix(i, hf, tt, ps, psb):
            dc = 2 * i + hf
            tsl = slice(tt * 512, (tt + 1) * 512)
            s.op("dve", lambda e: e.tensor_tensor(out=hT[:, dc, tsl], in0=ps[:], in1=hT[:, dc, tsl], op=ALU.add),
                 reads=[psb, hb[tt]], writes=[hb[tt]])
        emit_proj_fm(cx, xn, xb, wo_d, 4, res, sink_mix, kin=8)
    for fi in range(len(ffn_w)):
        emit_norm(cx, hT, hb, gcol[:, gi, :], gb, xn, xb, ones, ob, res["sq_ring"], res["psm_ring"], res["st_ring"])
        gi += 1
        emit_ffn(cx, hT, hb, xn, xb, ffn_w[fi][0], ffn_w[fi][1], res)
    if proj_units:
        emit_norm(cx, hT, hb, gcol[:, gi, :], gb, xn, xb, ones, ob, res["sq_ring"], res["psm_ring"], res["st_ring"])
        gi += 1
        act, actb = res["act_ring"].items[0]
        gu = act
        nz_units = proj_units - (2 if gm else 0)
        tmp_ring = Ring([((cx.sb("gt1_%d" % i, [128, 512], F32), cx.sb("gt2_%d" % i, [128, 512], F32)), cx.buf()) for i in range(2)])

        def sink_z(i, hf, tt, ps, psb):
            tsl = slice(tt * 512, (tt + 1) * 512)
            ch = 2 * i + hf
            if i < nz_units:
                stg, stgb = stg_ring.next()
                s.op("act", lambda e: e.activation(out=stg[:], in_=ps[:], func=AF.Copy), reads=[psb], writes=[stgb])
                P["z_sink"](ch, tt, tsl, stg, stgb)
            else:
                g = ch - 2 * nz_units
                emit_gelu(cx, ps[:], psb, gu[:, g, tsl], actb, tmp_ring)
        emit_proj_fm(cx, xn, xb, P["w_proj"], proj_units, res, sink_z)
        if P.get("flush") is not None and not gm:
            P["flush"]()
        if gm:
            wvb_d, vg_d, ws_d, tril_d, bs_d, bo_d = gm["w_vb"], gm["vg_rep"], gm["wsT"], gm["trilT"], gm["bs_rep"], gm["bo_dst"]
            wd0 = cx.sb("wvb", [128, KC, 512], BF16)
            wd0b = cx.buf()
            s.op("pool", lambda e: e.dma_start(out=wd0[:, :, :], in_=wvb_d[:, :, :]), writes=[wd0b], dma=True)
            if P.get("flush") is not None:
                P["flush"]()
            bo_ring = Ring([(cx.sb("bo%d" % i, [128, 4, 128], BF16), cx.buf()) for i in range(2)])
            vg = cx.sb("vg", [128, 512], F32)
            wsT = cx.sb("wsT", [128, 4, 128], F32)
            tril = cx.sb("tril", [128, 4, 128], F32)
            wsm = cx.sb("wsm", [128, 4, 128], BF16)
            bsr = cx.sb("bsr", [128, 4, 128], F32)
            cb = cx.buf()
            s.op("sp", lambda e: e.dma_start(out=vg[:], in_=vg_d[:, :]), writes=[cb], dma=True)
            s.op("sp", lambda e: e.dma_start(out=wsT[:], in_=ws_d[:, :, :]), writes=[cb], dma=True)
            s.op("sp", lambda e: e.dma_start(out=tril[:], in_=tril_d[:, :, :]), writes=[cb], dma=True)
            s.op("sp", lambda e: e.dma_start(out=bsr[:], in_=bs_d[:, :, :]), writes=[cb], dma=True)
            s.op("dve", lambda e: e.tensor_tensor(out=wsm[:], in0=wsT[:], in1=tril[:], op=ALU.mult), reads=[cb], writes=[cb])
            gv_ring = Ring([(cx.sb("gv%d" % i, [128, 512], F32), cx.buf()) for i in range(2)])
            sqv_ring = Ring([(cx.sb("sqv%d" % i, [128, 512], F32), cx.buf()) for i in range(2)])
            vn_ring = Ring([(cx.sb("vn%d" % i, [128, 512], BF16), cx.buf()) for i in range(2)])
            ss_ring = Ring([((cx.sb("ss%d" % i, [128, 4], F32), cx.sb("rr%d" % i, [128, 4], F32)), cx.buf()) for i in range(2)])
            mx_ring = Ring([(cx.sb("mxd%d" % i, [128, 512], F32), cx.buf()) for i in range(2)])
            for n in range(T // 128):
                nsl = slice(n * 128, (n + 1) * 128)
                tt = n // 4
                ps, psb = res["psu_ring"].next()
                for kc in range(KC):
                    s.op("pe", lambda e, ps=ps, kc=kc, nsl=nsl: e.matmul(ps[:], lhsT=xn[:, kc, nsl], rhs=wd0[:, kc, 0:512],
                                                                       start=(kc == 0), stop=(kc == KC - 1)),
                         reads=[wd0b, xb[tt]], writes=[psb])
                gv, gvb = gv_ring.next()
                emit_gelu(cx, ps[:], psb, gv[:], gvb, tmp_ring)
                sqv, sqvb = sqv_ring.next()
                (ss, rr), ssb = ss_ring.next()
                s.op("dve", lambda e, sqv=sqv, gv=gv: e.tensor_tensor(out=sqv[:], in0=gv[:], in1=gv[:], op=ALU.mult), reads=[gvb], writes=[sqvb])
                s.op("dve", lambda e, sqv=sqv, ss=ss: e.tensor_reduce(out=ss[:], in_=sqv[:].rearrange("p (g c) -> p g c", g=4), axis=AX.X, op=ALU.add),
                     reads=[sqvb], writes=[ssb])
                s.op("act", lambda e, ss=ss: e.activation(out=ss[:], in_=ss[:], func=AF.Sqrt, bias=EPS, scale=1.0 / 128), reads=[ssb], writes=[ssb])
                s.op("dve", lambda e, ss=ss, rr=rr: e.reciprocal(out=rr[:], in_=ss[:]), reads=[ssb], writes=[ssb])
                vn, vnb = vn_ring.next()
                for g in range(4):
                    s.op("dve", lambda e, g=g, vn=vn, gv=gv, rr=rr: e.scalar_tensor_tensor(
                        out=vn[:, g * 128:(g + 1) * 128], in0=gv[:, g * 128:(g + 1) * 128], scalar=rr[:, g:g + 1],
                        in1=vg[:, g * 128:(g + 1) * 128], op0=ALU.mult, op1=ALU.mult), reads=[gvb, ssb, cb], writes=[vnb])
                pm, pmb = res["psd_ring"].next()
                for g in range(4):
                    s.op("pe", lambda e, g=g, pm=pm, vn=vn: e.matmul(pm[:, g * 128:(g + 1) * 128], lhsT=vn[:, g * 128:(g + 1) * 128],
                                                                   rhs=wsm[:, g, :], start=True, stop=True),
                         reads=[vnb, cb], writes=[pmb])
                mx, mxb = mx_ring.next()
                s.op("dve", lambda e, mx=mx, pm=pm: e.tensor_tensor(out=mx[:], in0=pm[:], in1=bsr[:].rearrange("p g c -> p (g c)"), op=ALU.add),
                     reads=[pmb, cb], writes=[mxb])
                bo, bob = bo_ring.next()
                s.op("dve", lambda e, mx=mx, nsl=nsl, bo=bo: e.tensor_tensor(out=bo[:], in0=mx[:].rearrange("p (g c) -> p g c", g=4),
                                                                             in1=gu[:, 0:4, nsl], op=ALU.mult), reads=[mxb, actb], writes=[bob])
                s.op("sp", lambda e, bo=bo, nsl=nsl: e.dma_start(out=bo_d[:, :, nsl], in_=bo[:]), reads=[bob], dma=True)
    if hn_out is not None:
        emit_norm(cx, hT, hb, gcol[:, gi, :], gb, xn, xb, ones, ob, res["sq_ring"], res["psm_ring"], res["st_ring"])
        gi += 1
        hn_out(xn, xb)
    if final:
        emit_norm(cx, hT, hb, gcol[:, gi, :], gb, hT, hb, ones, ob, res["sq_ring"], res["psm_ring"], res["st_ring"])
    for tt in range(NTT):
        s.op("sp", lambda e, tt=tt: e.dma_start(out=hout_d[:, :, tt * 512:(tt + 1) * 512], in_=hT[:, :, tt * 512:(tt + 1) * 512]),
             reads=[hb[tt]], dma=True)
    cx.end_phase(final=final)


DILS = (1, 4, 16)
BIG = 30000.0
LAG = 3


def emit_vext(cx, vx, vxb, vsrc, vsrcb, colsl, ident_blk, identb, pt_ring):
    s = cx.s
    for g in range(4):
        pt, ptb = pt_ring.next()
        ptv = pt[:].bitcast(BF16)
        for i in range(16):
            blk = g * 16 + i
            s.op("pe", lambda e, ptv=ptv, i=i, blk=blk: e.transpose(out=ptv[:, i * 64:(i + 1) * 64], in_=vsrc[:, colsl(blk)], identity=ident_blk),
                 reads=[vsrcb, identb], writes=[ptb])
        s.op("act", lambda e, ptv=ptv, g=g: e.activation(out=vx[:, g * 16:(g + 1) * 16, 0:64], in_=ptv[:, :].rearrange("p (b c) -> p b c", c=64), func=AF.Copy),
             reads=[ptb], writes=[vxb])


def phase_dil(cx, P):
    s = cx.s
    zg, os_d, m_d, id_d = P["zg"], P["os"], P["masks"], P["ident"]
    pb = list(zip(cx.psum_banks, cx.psum_bufs))
    st_ring, po_ring, pt_ring = Ring(pb[0:5]), Ring(pb[5:7]), Ring(pb[7:8])
    qkv = cx.sb("qkv", [128, 3, SEQ], BF16)
    q, k, vT = qkv[:, 0, :], qkv[:, 1, :], qkv[:, 2, :]
    ldb = [cx.buf(), cx.buf()]
    acc = cx.sb("acc", [128, SEQ], F32)
    accall = cx.buf()
    masks = cx.sb("masks", [128, 2, 512], BF16)
    ident = cx.sb("ident", [128, 128], BF16)
    mb, idb = cx.buf(), cx.buf()
    s.op("sp", lambda e: e.dma_start(out=masks[:], in_=m_d.rearrange("m p f -> p m f")), writes=[mb], dma=True)
    s.op("sp", lambda e: e.dma_start(out=ident[:], in_=id_d[:, :]), writes=[idb], dma=True)
    for hh in range(2):
        for t in range(3):
            s.op("sp", lambda e, hh=hh, t=t: e.dma_start(
                out=qkv[hh * 64:hh * 64 + 64, t, :].rearrange("p (r t) -> p r t", r=4),
                in_=zg[3 * hh + t].rearrange("(r x) t -> x r t", r=4)[bass.ds(cx.rk64["sp"], 64)]), writes=[ldb[hh]], dma=True,
                 wait_cc=P["wait"](hh))
    vx_items = []
    for i in range(2):
        vx = cx.sb("vx%d" % i, [128, 64, 128], BF16)
        vxb = cx.buf()
        s.op("pool", lambda e, vx=vx: e.memset(vx[:, :, 64:128], 1.0), writes=[vxb])
        vx_items.append((vx, vxb))
    vx_ring = Ring(vx_items)
    p_ring = Ring([(cx.sb("p%d" % i, [128, 512], BF16), cx.buf()) for i in range(6)])
    rsh_ring = Ring([(cx.sb("rsh%d" % i, [64, 2048], F32), cx.buf()) for i in range(2)])
    rcp_ring = Ring([(cx.sb("rcp%d" % i, [128, 2048], F32), cx.buf()) for i in range(1)])
    ostg_ring = Ring([(cx.sb("ostg%d" % i, [64, 2048], BF16), cx.buf()) for i in range(2)])
    osb = [[cx.buf() for _ in range(4)] for _ in range(2)]
    qkvp = cx.sb("qkvp", [128, 3, SEQ], BF16)
    qp, kp, vp = qkvp[:, 0, :], qkvp[:, 1, :], qkvp[:, 2, :]
    qpb, kpb, vpb = cx.buf(), cx.buf(), cx.buf()

    def colsl(blk):
        return slice(blk * 128, (blk + 1) * 128)
    for hh in range(2):
        hs = slice(hh * 64, hh * 64 + 64)
        for pi, d in enumerate(DILS):
            nb = 64 // d
            qb = kb = vTb = ldb[hh]
            if d == 1:
                qs, ks, vs_, qsb, ksb, vsb_ = q, k, vT, qb, kb, vTb
            else:
                s.op("dve", lambda e, d=d, hs=hs: e.tensor_copy(out=qp[hs].rearrange("p (r m) -> p r m", r=d),
                                                                in_=q[hs].rearrange("p (m r) -> p r m", r=d)), reads=[qb], writes=[qpb])
                for (src, srcb, dst, dstb) in ((k, kb, kp, kpb), (vT, vTb, vp, vpb)):
                    s.op("act", lambda e, src=src, dst=dst, d=d, hs=hs: e.activation(out=dst[hs].rearrange("p (r m) -> p r m", r=d),
                                                                                    in_=src[hs].rearrange("p (m r) -> p r m", r=d), func=AF.Copy),
                         reads=[srcb], writes=[dstb])
                qs, ks, vs_, qsb, ksb, vsb_ = qp, kp, vp, qpb, kpb, vpb
            v, vb = vx_ring.next()
            emit_vext(cx, v, vb, vs_[hs], vsb_, colsl, ident[hs, hs], idb, pt_ring)
            po, pob = None, None
            pending = []
            for jp in range(32):
                j0 = 2 * jp
                n0 = j0 % nb
                first = (n0 == 0)
                ps, psb = st_ring.next()
                kprev = j0 if first else j0 - 1
                for ci, (kblk, qblk) in enumerate(((kprev, j0), (j0, j0), (j0, j0 + 1), (j0 + 1, j0 + 1))):
                    s.op("pe", lambda e, ps=ps, ci=ci, kblk=kblk, qblk=qblk, ks=ks, qs=qs, hs=hs: e.matmul(
                        ps[:, ci * 128:(ci + 1) * 128], lhsT=ks[hs, colsl(kblk)], rhs=qs[hs, colsl(qblk)],
                        start=True, stop=True), reads=[ksb, qsb], writes=[psb])
                p, pbuf = p_ring.next()
                s.op("act", lambda e, p=p, ps=ps: e.activation(out=p[:], in_=ps[:], func=AF.Exp, scale=0.125), reads=[psb], writes=[pbuf])
                mi = 1 if first else 0
                s.op("dve", lambda e, p=p, mi=mi: e.tensor_tensor(out=p[:], in0=p[:], in1=masks[:, mi, :], op=ALU.mult),
                     reads=[pbuf, mb], writes=[pbuf])
                if jp % 2 == 0:
                    po, pob = po_ring.next()

                def pv(po=po, pob=pob, v=v, vb=vb, j0=j0, p=p, pbuf=pbuf, first=first, jp=jp, nb=nb, d=d, pi=pi):
                    c0 = (j0 % 4) * 128
                    if not first:
                        s.op("pe", lambda e: e.matmul(po[:, c0:c0 + 128], lhsT=v[:, j0 - 1, :], rhs=p[:, 0:128], start=True, stop=False),
                             reads=[vb, pbuf], writes=[pob])
                    s.op("pe", lambda e: e.matmul(po[:, c0:c0 + 128], lhsT=v[:, j0, :], rhs=p[:, 128:256], start=first, stop=True),
                         reads=[vb, pbuf], writes=[pob])
                    s.op("pe", lambda e: e.matmul(po[:, c0 + 128:c0 + 256], lhsT=v[:, j0, :], rhs=p[:, 256:384], start=True, stop=False),
                         reads=[vb, pbuf], writes=[pob])
                    s.op("pe", lambda e: e.matmul(po[:, c0 + 128:c0 + 256], lhsT=v[:, j0 + 1, :], rhs=p[:, 384:512], start=False, stop=True),
                         reads=[vb, pbuf], writes=[pob])
                    if jp % 2 == 1:
                        j = j0 - 2
                        start = (j // nb) + d * 128 * (j % nb)
                        dst = acc[:, start:start + 511 * d + 1:d]
                        if pi == 0:
                            s.op("dve", lambda e: e.tensor_copy(out=dst, in_=po[:]), reads=[pob], writes=[accall])
                        else:
                            s.op("dve", lambda e: e.tensor_tensor(out=dst, in0=po[:], in1=dst, op=ALU.add), reads=[pob, accall], writes=[accall])
                pending.append(pv)
                if len(pending) > 4:
                    pending.pop(0)()
            while pending:
                pending.pop(0)()
        for c in range(4):
            csl = slice(c * 2048, (c + 1) * 2048)
            rcp, rcpb = rcp_ring.next()
            s.op("dve", lambda e, rcp=rcp, csl=csl: e.reciprocal(out=rcp[64:128, :], in_=acc[64:128, csl]), reads=[accall], writes=[rcpb])
            rsh, rshb = rsh_ring.next()
            s.op("sp", lambda e, rsh=rsh, rcp=rcp: e.dma_start(out=rsh[:, :], in_=rcp[64:128, :]), reads=[rcpb], writes=[rshb], dma=True)
            og, ogb = ostg_ring.next()
            s.op("dve", lambda e, og=og, csl=csl, rsh=rsh: e.tensor_tensor(out=og[:], in0=acc[0:64, csl], in1=rsh[:, :], op=ALU.mult),
                 reads=[accall, rshb], writes=[ogb])
            s.op("sp", lambda e, og=og, c=c, hh=hh: e.dma_start(out=os_d[hh, :, c * 2048:(c + 1) * 2048], in_=og[:]), reads=[ogb], writes=[osb[hh][c]], dma=True)
        P["after_head"](hh, osb[0] + osb[1])
    cx.end_phase()


def phase_hproj(cx, P):
    s = cx.s
    hg, qk_s, v_s = P["hg"], P["qk_s"], P["v_s"]
    pb = list(zip(cx.psum_banks, cx.psum_bufs))
    ps_ring, pv_ring = Ring(pb[0:4]), Ring(pb[4:8])
    wt = cx.sb("wqkv", [128, 3, KC, 256], BF16)
    wb = cx.buf()
    for i, wd_ in enumerate((P["wq"], P["wk"], P["wv"])):
        s.op("pool", lambda e, i=i, wd_=wd_: e.dma_start(out=wt[:, i, :, :], in_=wd_[:, :, :]), writes=[wb], dma=True)
    hn_ring = Ring([(cx.sb("hn%d" % i, [128, KC, 512], BF16), cx.buf()) for i in range(6)])
    stg_ring = Ring([(cx.sb("stg%d" % i, [128, 512], BF16), cx.buf()) for i in range(8)])
    vst_items = []
    for i in range(3):
        t = cx.sb("vst%d" % i, [128, 4, 4, 128], BF16)
        b = cx.buf()
        s.op("dve", lambda e, t=t: e.memset(t[:, :, :, 64:128], 1.0), writes=[b])
        vst_items.append((t, b))
    vst_ring = Ring(vst_items)
    tiles = [(qd, r) for qd in range(4) for r in range(4)]
    loaded = {}

    def load(i):
        qd, r = tiles[i]
        hn, hnb = hn_ring.next()
        s.op("sp", lambda e: e.dma_start(out=hn[:], in_=hg[qd, r * 1024:(r + 1) * 1024, :].rearrange("(c p) t -> p c t", p=128)),
             writes=[hnb], dma=True, wait_cc=P["wait"](qd))
        loaded[i] = (hn, hnb)
    load(0)
    load(1)
    load(2)
    load(3)
    for ti, (qd, r) in enumerate(tiles):
        if True:
            if ti + 4 < len(tiles):
                load(ti + 4)
            hn, hnb = loaded.pop(ti)
            gsl = slice(r * T + qd * 512, r * T + (qd + 1) * 512)
            for which in range(2):
                for cp in range(2):
                    ps, psb = ps_ring.next()
                    for kc in range(KC):
                        s.op("pe", lambda e, ps=ps, which=which, cp=cp, kc=kc, hn=hn: e.matmul(
                            ps[:], lhsT=wt[:, which, kc, cp * 128:(cp + 1) * 128], rhs=hn[:, kc, :], start=(kc == 0), stop=(kc == KC - 1)),
                            reads=[wb, hnb], writes=[psb])
                    stg, stgb = stg_ring.next()
                    s.op("act", lambda e, stg=stg, ps=ps: e.activation(out=stg[:], in_=ps[:], func=AF.Copy), reads=[psb], writes=[stgb])
                    for hh in range(2):
                        s.op("sp", lambda e, stg=stg, hh=hh, cp=cp, which=which, gsl=gsl: e.dma_start(
                            out=qk_s[2 * cp + hh, which, :, gsl], in_=stg[hh * 64:(hh + 1) * 64, :]), reads=[stgb], dma=True)
            vst, vstb = vst_ring.next()
            for n4 in range(4):
                nsl = slice(n4 * 128, (n4 + 1) * 128)
                pv, pvb = pv_ring.next()
                for kc in range(KC):
                    s.op("pe", lambda e, pv=pv, kc=kc, hn=hn, nsl=nsl: e.matmul(pv[:, 0:256], lhsT=hn[:, kc, nsl], rhs=wt[:, 2, kc, :],
                                                                              start=(kc == 0), stop=(kc == KC - 1)),
                         reads=[wb, hnb], writes=[pvb])
                s.op("dve", lambda e, vst=vst, pv=pv, n4=n4: e.tensor_copy(out=vst[:, :, n4, 0:64], in_=pv[:, 0:256].rearrange("p (i c) -> p i c", c=64)),
                     reads=[pvb], writes=[vstb])
            n0 = r * 16 + qd * 4
            for it in range(4):
                s.op("sp", lambda e, vst=vst, it=it, n0=n0: e.dma_start(out=v_s[it, :, n0:n0 + 4, :], in_=vst[:, it, :, :]), reads=[vstb], dma=True)
    cx.end_phase()


def phase_moba(cx, P, nitems=4):
    s = cx.s
    os_d = P["os"]
    er_d, id_d, cb_d, past_d, ownb_d, dm_d = P["erows"], P["ident"], P["cb"], P["past"], P["ownb"], P["dm"]
    pb = list(zip(cx.psum_banks, cx.psum_bufs))
    st_ring, po_ring, pg_ring, pt_ring = Ring(pb[0:4]), Ring(pb[4:6]), Ring(pb[6:7]), Ring(pb[7:8])
    ident = cx.sb("ident", [128, 128], BF16)
    cbt = cx.sb("cbt", [128, 4, 512], F32)
    pastt = cx.sb("pastt", [128, 4, 512], F32)
    ownbt = cx.sb("ownbt", [128, 4, 512], F32)
    dmt = cx.sb("dmt", [128, 4, 512], BF16)
    cb = cx.buf()
    s.op("sp", lambda e: e.dma_start(out=ident[:], in_=id_d[:, :]), writes=[cb], dma=True)
    s.op("sp", lambda e: e.dma_start(out=cbt[:], in_=cb_d[:, :, :]), writes=[cb], dma=True)
    s.op("sp", lambda e: e.dma_start(out=pastt[:], in_=past_d[:, :, :]), writes=[cb], dma=True)
    s.op("sp", lambda e: e.dma_start(out=ownbt[:], in_=ownb_d[:, :, :]), writes=[cb], dma=True)
    s.op("sp", lambda e: e.dma_start(out=dmt[:], in_=dm_d[:, :, :]), writes=[cb], dma=True)
    items = []
    for i in range(2):
        t3 = cx.sb("t3_%d" % i, [96, 2, SEQ], BF16)
        qa, ka = t3[:, 0, :], t3[:, 1, :]
        va = cx.sb("va%d" % i, [128, 64, 128], BF16)
        tb, eb, vab, qbias = cx.buf(), cx.buf(), cx.buf(), cx.buf()
        s.op("sp", lambda e, ka=ka: e.dma_start(out=ka[64:96], in_=er_d[:, :]), writes=[eb], dma=True)
        items.append(((t3, qa, ka, va, None), (tb, eb, vab, qbias)))
    it_ring = Ring(items)
    kmf = cx.sb("kmf", [64, 32], F32)
    km = cx.sb("km", [64, 32], BF16)
    kmb = cx.buf()
    gm = cx.sb("gm", [128, 512], F32)
    sel = cx.sb("sel", [128, 512], F32)
    mx = cx.sb("mx", [128, 16, 8], F32)
    bq_ring = Ring([(cx.sb("bq%d" % i, [128, 512], BF16), cx.buf()) for i in range(2)])
    gmb = cx.buf()
    p_ring = Ring([(cx.sb("p%d" % i, [128, 512], BF16), cx.buf()) for i in range(6)])
    rec_ring = Ring([(cx.sb("rec%d" % i, [128, 512], F32), cx.buf()) for i in range(2)])
    ostg_ring = Ring([(cx.sb("ostg%d" % i, [64, 512], BF16), cx.buf()) for i in range(3)])
    osb = [[cx.buf() for _ in range(4)] for _ in range(nitems)]
    qk_s, v_s = P["qk_s"], P["v_s"]

    def issue_loads(it, item):
        (t3, qa, ka, va, _), (tb, eb, vab, qbias) = item
        s.op("sp", lambda e: e.dma_start(out=t3[0:64, :, :], in_=qk_s[it].rearrange("w p t -> p w t")), writes=[tb], dma=True)
        s.op("sp", lambda e: e.dma_start(out=va[:], in_=v_s[it]), writes=[vab], dma=True)
    cur = it_ring.next()
    issue_loads(0, cur)
    for it in range(nitems):
        (t3, qa, ka, va, vs), (tb, eb, vab, qbias) = cur
        qab = kab = vsb = tb
        s.op("dve", lambda e, ka=ka: e.tensor_reduce(out=kmf[:], in_=ka[0:64].rearrange("p (j k) -> p j k", k=256), axis=AX.X, op=ALU.add),
             reads=[kab], writes=[kmb])
        s.op("act", lambda e: e.activation(out=km[:], in_=kmf[:], func=AF.Copy, scale=1.0 / 256), reads=[kmb], writes=[kmb])
        for grp in range(4):
            pg, pgb = pg_ring.next()
            for i in range(16):
                qt = grp * 16 + i
                s.op("pe", lambda e, pg=pg, i=i, qt=qt, qa=qa: e.matmul(pg[:, i * 32:(i + 1) * 32], lhsT=qa[0:64, qt * 128:(qt + 1) * 128], rhs=km[:, :],
                                                                       start=True, stop=True), reads=[qab, kmb], writes=[pgb])
            s.op("dve", lambda e, pg=pg, grp=grp: e.tensor_tensor(out=gm[:], in0=pg[:], in1=cbt[:, grp, :], op=ALU.add), reads=[pgb, cb], writes=[gmb])
            for i in range(16):
                s.op("dve", lambda e, i=i: e.max(out=mx[:, i, :], in_=gm[:, i * 32:(i + 1) * 32]), reads=[gmb], writes=[gmb])
            s.op("dve", lambda e: e.tensor_tensor(out=sel[:].rearrange("p (a j) -> p a j", j=32), in0=gm[:].rearrange("p (a j) -> p a j", j=32),
                                                  in1=mx[:, :, 2:3].to_broadcast([128, 16, 32]), op=ALU.is_ge), reads=[gmb], writes=[gmb])
            s.op("dve", lambda e, grp=grp: e.tensor_tensor(out=sel[:], in0=sel[:], in1=pastt[:, grp, :], op=ALU.mult), reads=[gmb, cb], writes=[gmb])
            bq, bqb = bq_ring.next()
            s.op("dve", lambda e, grp=grp, bq=bq: e.scalar_tensor_tensor(out=bq[:], in0=sel[:], scalar=BIG, in1=ownbt[:, grp, :], op0=ALU.mult, op1=ALU.add),
                 reads=[gmb, cb], writes=[bqb])
            for half in range(2):
                pt, ptb = pt_ring.next()
                ptv = pt[:].bitcast(BF16)
                for i8 in range(8):
                    i = half * 8 + i8
                    s.op("pe", lambda e, ptv=ptv, i8=i8, i=i, bq=bq: e.transpose(out=ptv[0:32, i8 * 128:(i8 + 1) * 128], in_=bq[:, i * 32:(i + 1) * 32],
                                                                              identity=ident[:]), reads=[bqb, cb], writes=[ptb])
                c0 = (grp * 16 + half * 8) * 128
                s.op("act", lambda e, ptv=ptv, c0=c0, qa=qa: e.activation(out=qa[64:96, c0:c0 + 1024], in_=ptv[0:32, :], func=AF.Copy),
                     reads=[ptb], writes=[qbias])
        pending = []
        nxt = None
        for tq in range(16):
            if tq == 13 and it + 1 < nitems:
                nxt = it_ring.next()
                issue_loads(it + 1, nxt)
            po, pob = po_ring.next()
            nk = 4 * tq + 4
            for kt in range(nk):
                ps, psb = st_ring.next()
                s.op("pe", lambda e, ps=ps, kt=kt, tq=tq, ka=ka, qa=qa: e.matmul(ps[:], lhsT=ka[:, kt * 128:(kt + 1) * 128], rhs=qa[:, tq * 512:(tq + 1) * 512],
                                                                              start=True, stop=True), reads=[kab, eb, qab, qbias], writes=[psb])
                p, pbuf = p_ring.next()
                s.op("act", lambda e, p=p, ps=ps: e.activation(out=p[:], in_=ps[:], func=AF.Exp, scale=0.125), reads=[psb], writes=[pbuf])
                if kt >= 4 * tq:
                    di = kt - 4 * tq
                    s.op("dve", lambda e, p=p, di=di: e.tensor_tensor(out=p[:], in0=p[:], in1=dmt[:, di, :], op=ALU.mult), reads=[pbuf, cb], writes=[pbuf])

                def pv(po=po, pob=pob, kt=kt, p=p, pbuf=pbuf, nk=nk, tq=tq, va=va, vab=vab, it=it):
                    s.op("pe", lambda e: e.matmul(po[:], lhsT=va[:, kt, :], rhs=p[:], start=(kt == 0), stop=(kt == nk - 1)),
                         reads=[vab, pbuf], writes=[pob])
                    if kt == nk - 1:
                        rec, recb = rec_ring.next()
                        s.op("dve", lambda e: e.reciprocal(out=rec[64:128, :], in_=po[64:128, :]), reads=[pob], writes=[recb])
                        og, ogb = ostg_ring.next()
                        s.op("dve", lambda e: e.tensor_tensor(out=og[:], in0=po[0:64, :], in1=rec[64:128, :], op=ALU.mult),
                             reads=[pob, recb], writes=[ogb])
                        s.op("sp", lambda e: e.dma_start(out=os_d[it, :, tq * 512:(tq + 1) * 512], in_=og[:]),
                             reads=[ogb], writes=[osb[it][tq // 4]], dma=True)
                pending.append(pv)
                if len(pending) > LAG:
                    pending.pop(0)()
        while pending:
            pending.pop(0)()
        if it + 1 < nitems:
            cur = nxt
        P["after_item"](it, osb[it])
    cx.end_phase()


RG = [[0, 1, 2, 3], [4, 5, 6, 7]]


def build_fused():
    cx = Ctx()
    s = cx.s
    I32 = mybir.dt.int32
    di = cx.dram_in
    xT_d = di("xT", [128, KC, T])
    rk_d = cx.nc.dram_tensor("rk", [1, 4], I32, kind="ExternalInput").ap()
    out_d = cx.dram_out("out_fm", [128, KC, T])
    A = {"gains": di("A_gains", [128, 2, KC]), "ffn": [(di("A_fwgu0", [NFC, 128, 2048]), di("A_fwd0", [NFC, 128, D]))],
         "w_proj": di("A_w_proj", [8, 128, 2048]), "w_vb": di("A_w_vb", [128, KC, 512]), "vg_rep": di("A_vg_rep", [128, 512]),
         "wsT": di("A_wsT", [128, 4, 128]), "trilT": di("A_trilT", [128, 4, 128]), "bs_rep": di("A_bs_rep", [128, 4, 128])}
    Bc = {"masks": di("B_masks", [2, 128, 512], BF16), "ident": di("ident", [128, 128], BF16)}
    C = {"gains": di("C_gains", [128, 3, KC]), "w_out": di("C_w_out", [4, 128, 2048]),
         "ffn": [(di("C_fwgu0", [NFC, 128, 2048]), di("C_fwd0", [NFC, 128, D])), (di("C_fwgu1", [NFC, 128, 2048]), di("C_fwd1", [NFC, 128, D]))],
         "wq": di("D_wq", [128, KC, 256]), "wk": di("D_wk", [128, KC, 256]), "wv": di("D_wv", [128, KC, 256])}
    Dc = {"erows": di("D_erows", [32, SEQ], BF16), "cb": di("D_cb", [128, 4, 512]), "past": di("D_past", [128, 4, 512]),
          "ownb": di("D_ownb", [128, 4, 512]), "dm": di("D_dm", [128, 4, 512], BF16)}
    E = {"gains": di("E_gains", [128, 2, KC]), "w_out": di("E_w_out", [4, 128, 2048]),
         "ffn": [(di("E_fwgu0", [NFC, 128, 2048]), di("E_fwd0", [NFC, 128, D]))]}
    h_s = cx.dram("h_s", [128, KC, T])
    bo_s = cx.dram("bo_s", [128, 4, T], BF16)
    zsA = cx.dram("zsA", [6, 256, T], BF16)
    zgA = cx.dram("zgA", [6, 1024, T], BF16)
    osA = cx.dram("osA", [2, 64, SEQ], BF16)
    ogA = cx.dram("ogA", [2, 256, SEQ], BF16)
    hs_d = cx.dram("hs_d", [4, 1024, 512], BF16)
    hg_d = cx.dram("hg_d", [4, 4096, 512], BF16)
    qk_s = cx.dram("qk_s", [4, 2, 64, SEQ], BF16)
    v_s = cx.dram("v_s", [4, 128, 64, 128], BF16)
    osC = cx.dram("osC", [4, 64, SEQ], BF16)
    ogC = cx.dram("ogC", [4, 256, SEQ], BF16)
    cx.alloc_psum()

    cx.rk = {}
    cx.rk64 = {}
    cx.rk2048 = {}

    def mk_init(name):
        def init(eng):
            reg = eng.alloc_register("rkreg_" + name)
            eng.reg_load(reg, rk_d[0:1, 0:1])
            cx.rk[name] = eng.snap(reg, min_val=0, max_val=3)
            reg2 = eng.alloc_register("rkreg64_" + name)
            eng.reg_load(reg2, rk_d[0:1, 1:2])
            cx.rk64[name] = eng.snap(reg2, min_val=0, max_val=192)
            if name == "sp":
                reg3 = eng.alloc_register("rkreg2k_" + name)
                eng.reg_load(reg3, rk_d[0:1, 2:3])
                cx.rk2048[name] = eng.snap(reg3, min_val=0, max_val=3 * T)
        return init
    s.sp_init = {"sp": mk_init("sp")}
    gdone = cx.buf()

    cct = {}

    def gather(key, src, dst, rbufs):
        s.op("pool", lambda e: e.collective_compute("AllGather", ALU.bypass, replica_groups=RG, ins=[src], outs=[dst]),
             reads=rbufs, cc=True)
        cct[key] = s.cccount

    def make_sink(name, zs, zg, nunits):
        bufs = [cx.buf() for _ in range(nunits)]
        todo = []

        def sink(ch, tt, tsl, stg, stgb):
            u, hf = ch // 2, ch % 2
            s.op("sp", lambda e: e.dma_start(out=zs[u, hf * 128:(hf + 1) * 128, tsl], in_=stg[:]), reads=[stgb], writes=[bufs[u]], dma=True)
            if hf == 1 and tt == NTT - 1:
                todo.append(u)

        def flush():
            for u in todo:
                gather((name, u), zs[u], zg[u], [bufs[u]])
        return sink, flush

    sinkA, flushA = make_sink("A", zsA, zgA, 6)
    phase_tok(cx, {"h_in": xT_d, "h_out": h_s, "gains": A["gains"], "ffn_w": A["ffn"], "w_proj": A["w_proj"], "proj_units": 8,
                   "z_sink": sinkA, "flush": flushA,
                   "gmlp": {"w_vb": A["w_vb"], "vg_rep": A["vg_rep"], "wsT": A["wsT"], "trilT": A["trilT"], "bs_rep": A["bs_rep"], "bo_dst": bo_s}})

    def after_head_B(hh, osb):
        gather(("oA", hh), osA[hh], ogA[hh], list(osb[4 * hh:4 * hh + 4]))
    phase_dil(cx, {"zg": zgA, "os": osA, "masks": Bc["masks"], "ident": Bc["ident"], "after_head": after_head_B,
                   "wait": lambda hh: cct[("A", 3 * hh + 2)]})

    def o_load_C(xn, xb):
        for hh in range(2):
            s.op("sp", lambda e, hh=hh: e.dma_start(out=xn[hh * 64:hh * 64 + 64, 0:4, :], in_=ogA[hh].rearrange("(r x) t -> x r t", r=4)[:, :, bass.ds(cx.rk2048["sp"], T)]),
                 writes=list(xb), dma=True, wait_cc=cct[("oA", hh)])
        for tt in range(NTT):
            tsl = slice(tt * 512, (tt + 1) * 512)
            s.op("sp", lambda e, tsl=tsl: e.dma_start(out=xn[:, 4:8, tsl], in_=bo_s[:, :, tsl]), writes=[xb[tt]], dma=True)
    def hn_out_C(xn, xb):
        for tt in range(NTT):
            b = cx.buf()
            tsl = slice(tt * 512, (tt + 1) * 512)
            s.op("sp", lambda e, tt=tt, tsl=tsl: e.dma_start(out=hs_d[tt].rearrange("(c p) t -> p c t", p=128), in_=xn[:, :, tsl]),
                 reads=[xb[tt]], writes=[b], dma=True)
            gather(("H", tt), hs_d[tt], hg_d[tt], [b])
    phase_tok(cx, {"h_in": h_s, "h_out": h_s, "gains": C["gains"], "o_load": o_load_C, "w_out": C["w_out"], "ffn_w": C["ffn"],
                   "hn_out": hn_out_C})

    phase_hproj(cx, {"hg": hg_d, "qk_s": qk_s, "v_s": v_s, "wq": C["wq"], "wk": C["wk"], "wv": C["wv"], "wait": lambda qd: cct[("H", qd)]})

    def after_item_D(it, osb):
        gather(("oC", it), osC[it], ogC[it], list(osb))
    phase_moba(cx, {"qk_s": qk_s, "v_s": v_s, "os": osC, "erows": Dc["erows"], "ident": Bc["ident"], "cb": Dc["cb"], "past": Dc["past"],
                    "ownb": Dc["ownb"], "dm": Dc["dm"], "after_item": after_item_D})

    def o_load_E(xn, xb):
        for it in range(4):
            ps = slice((it % 2) * 64, (it % 2) * 64 + 64)
            s.op("sp", lambda e, it=it, ps=ps: e.dma_start(out=xn[ps, (it // 2):8:2, :], in_=ogC[it].rearrange("(r x) t -> x r t", r=4)[:, :, bass.ds(cx.rk2048["sp"], T)]),
                 writes=list(xb), dma=True, wait_cc=cct[("oC", it)])
    phase_tok(cx, {"h_in": h_s, "h_out": out_d, "gains": E["gains"], "o_load": o_load_E, "w_out": E["w_out"], "ffn_w": E["ffn"], "final": True})
    cx.stack.close()
    return cx.nc


BF = ml_dtypes.bfloat16
_PROGS = {}


def _gains(*gs):
    return np.ascontiguousarray(np.stack([gain_fm(np.asarray(g, np.float32)) for g in gs], axis=1))


def _head_perm(w, nheads, per_core):
    cols = []
    for i in range(per_core):
        for t in range(3):
            for c in range(4):
                h = per_core * c + i
                cols.append(w[:, t * nheads * 64 + h * 64: t * nheads * 64 + (h + 1) * 64])
    return np.concatenate(cols, axis=1)


def _dil_consts():
    k = np.arange(128)[:, None]
    q = np.arange(128)[None, :]
    prev = (k >= q).astype(np.float32)
    own = (k <= q).astype(np.float32)
    z = np.zeros_like(prev)
    m = np.stack([np.concatenate([prev, own, prev, own], 1), np.concatenate([z, own, prev, own], 1)], 0)
    return m.astype(BF)


def _moba_consts():
    er = (np.arange(SEQ)[None, :] // 256 == np.arange(32)[:, None]).astype(np.float32).astype(BF)
    ident = np.eye(128, dtype=np.float32).astype(BF)
    cb = np.zeros((128, 4, 16, 32), np.float32)
    past = np.zeros((128, 4, 16, 32), np.float32)
    ownb = np.zeros((128, 4, 16, 32), np.float32)
    j = np.arange(32)
    for grp in range(4):
        for i in range(16):
            own = (grp * 16 + i) // 2
            cb[:, grp, i, :] = np.where(j < own, 0.0, -2 * BIG)
            past[:, grp, i, :] = (j < own)
            ownb[:, grp, i, :] = np.where(j == own, 0.0, -BIG)
    dm = np.ones((128, 4, 512), np.float32)
    kk = np.arange(128)[:, None]
    qq = np.arange(512)[None, :]
    for di in range(4):
        kpos = 128 * di + kk
        same = (kpos // 256) == (qq // 256)
        dm[:, di, :] = np.where(same & (qq < kpos), 0.0, 1.0)
    return {"D_erows": er, "ident": ident, "D_cb": cb.reshape(128, 4, 512), "D_past": past.reshape(128, 4, 512),
            "D_ownb": ownb.reshape(128, 4, 512), "D_dm": dm.astype(BF)}


def kernel(x, ffn1_norm, ffn1_w_gate, ffn1_w_up, ffn1_w_down, mix_norm,
           ffn2_norm, ffn2_w_gate, ffn2_w_up, ffn2_w_down,
           ab_w_in, ab_v_norm, ab_w_spatial, ab_b_spatial, ab_w_out,
           c_w_in, c_w_out, final_norm):
    f = lambda a: np.asarray(a, dtype=np.float32)
    xf = f(x).reshape(-1, D)
    w_in = f(ab_w_in)[0]
    common = {
        "A_gains": _gains(f(ffn1_norm)[0], f(mix_norm)[0]),
        "A_fwgu0": pack_wgu(f(ffn1_w_gate)[0], f(ffn1_w_up)[0]), "A_fwd0": pack_wd(f(ffn1_w_down)[0]),
        "A_w_proj": pack_pairs(np.concatenate([_head_perm(w_in[:, :1536], 8, 2), w_in[:, 1536:2048]], axis=1)),
        "A_w_vb": np.ascontiguousarray(w_in[:, 2048:2560].reshape(KC, 128, 512).transpose(1, 0, 2)),
        "A_vg_rep": np.ascontiguousarray(np.broadcast_to(f(ab_v_norm)[0].reshape(1, 512), (128, 512))),
        "A_wsT": np.ascontiguousarray(f(ab_w_spatial)[0].transpose(2, 0, 1)),
        "A_trilT": np.ascontiguousarray(np.broadcast_to(np.triu(np.ones((128, 128), np.float32))[:, None, :], (128, 4, 128))),
        "A_bs_rep": np.ascontiguousarray(np.broadcast_to(f(ab_b_spatial)[0][None], (128, 4, 128))),
        "B_masks": _dil_consts(),
        "C_gains": _gains(f(ffn2_norm)[0], f(ffn1_norm)[1], f(mix_norm)[1]),
        "C_w_out": pack_pairs(f(ab_w_out)[0]),
        "C_fwgu0": pack_wgu(f(ffn2_w_gate)[0], f(ffn2_w_up)[0]), "C_fwd0": pack_wd(f(ffn2_w_down)[0]),
        "C_fwgu1": pack_wgu(f(ffn1_w_gate)[1], f(ffn1_w_up)[1]), "C_fwd1": pack_wd(f(ffn1_w_down)[1]),
        "E_gains": _gains(f(ffn2_norm)[1], f(final_norm)),
        "E_w_out": pack_pairs(f(c_w_out)[0]),
        "E_fwgu0": pack_wgu(f(ffn2_w_gate)[1], f(ffn2_w_up)[1]), "E_fwd0": pack_wd(f(ffn2_w_down)[1]),
    }
    common.update(_moba_consts())
    cw = f(c_w_in)[0]

    def wslice(t, j):
        w = cw[:, t * 1024 + 256 * j: t * 1024 + 256 * (j + 1)]
        return np.ascontiguousarray(w.reshape(KC, 128, 256).transpose(1, 0, 2))
    ins = [dict(common, xT=to_fm(xf[c * T:(c + 1) * T]), rk=np.array([[c % 4, (c % 4) * 64, (c % 4) * T, 0]], np.int32),
                D_wq=wslice(0, c % 4), D_wk=wslice(1, c % 4), D_wv=wslice(2, c % 4)) for c in range(NCORES)]
    if "F" not in _PROGS:
        _PROGS["F"] = build_fused()
    res = run_bass_kernel_spmd(_PROGS["F"], ins, core_ids=list(range(NCORES))).results
    out = np.concatenate([from_fm(np.asarray(res[c]["out_fm"])) for c in range(NCORES)], axis=0)
    return out.reshape(2, SEQ, D).astype(np.float32)
```
